# Optimizing a Trainium2 kernel written in Bass

```python
import math
import jax, jax.numpy as jnp
from jax import lax
import numpy as np

D_MODEL = 1024
BATCH = 32
SEQ = 2048
DEPTH = 1
DEC_BATCH = 32
DEC_SEQ = 16
PAST_LEN = 1024

CHUNK = 64
Q_BLOCK = 128
MIX_WIDTH = D_MODEL
ATTN_WIDTH = MIX_WIDTH // 2
CONV_WIDTH = MIX_WIDTH - ATTN_WIDTH
N_HEADS_A = 4
V_HEAD_DIM = ATTN_WIDTH // N_HEADS_A
QK_HEAD_DIM = V_HEAD_DIM // 2
ROT_DIM = QK_HEAD_DIM // 4
ROPE_THETA = 500000.0
SC_WIDTH = 3
N_MEM = 256
N_HEADS_X = 4
X_HEAD_DIM = D_MODEL // N_HEADS_X
D_FF = 2816
FFN_CONV_WIDTH = 3
EPS = 1e-6
Q_COLS = N_HEADS_A * 2 * QK_HEAD_DIM
K_COLS = N_HEADS_A * 2 * QK_HEAD_DIM
V_COLS = N_HEADS_A * V_HEAD_DIM
IN_PROJ = Q_COLS + K_COLS + V_COLS + 3 * CONV_WIDTH

kernel_name = "hybrid_diffattn_shortconv_streaming_step"


def lambda_init_fn(layer_idx):
    return 0.8 - 0.6 * math.exp(-0.3 * layer_idx)


def rms_norm(x, g):
    xf = x.astype(jnp.float32)
    y = xf * lax.rsqrt(jnp.mean(xf * xf, axis=-1, keepdims=True) + EPS)
    return (y * g.astype(jnp.float32)).astype(x.dtype)


def rope_partial(t, pos):
    half = ROT_DIM // 2
    inv = 1.0 / (ROPE_THETA ** (jnp.arange(half, dtype=jnp.float32) * 2.0 / ROT_DIM))
    ang = pos.astype(jnp.float32)[:, None] * inv[None, :]
    cos = jnp.cos(ang)[None, :, None, None, :]
    sin = jnp.sin(ang)[None, :, None, None, :]
    tr = t[..., :ROT_DIM].astype(jnp.float32)
    x1, x2 = tr[..., :half], tr[..., half:]
    rot = jnp.concatenate([x1 * cos - x2 * sin, x2 * cos + x1 * sin], axis=-1).astype(t.dtype)
    return jnp.concatenate([rot, t[..., ROT_DIM:]], axis=-1)


def causal_dwconv(u, state, w):
    k_w = w.shape[0]
    t_len = u.shape[1]
    full = jnp.concatenate([state.astype(u.dtype), u], axis=1)
    y = w[0] * full[:, 0:t_len]
    for j in range(1, k_w):
        y = y + w[j] * full[:, j:j + t_len]
    return y, full[:, -(k_w - 1):]


def diff_lambda(lq1, lk1, lq2, lk2, lam_init):
    f = jnp.float32
    return (jnp.exp(jnp.sum(lq1.astype(f) * lk1.astype(f)))
            - jnp.exp(jnp.sum(lq2.astype(f) * lk2.astype(f))) + lam_init)


def diff_combine(s, v, lam):
    p = jax.nn.softmax(s, axis=-1)
    a = p[:, :, 0] - lam * p[:, :, 1]
    return jnp.einsum('bhqk,bkhe->bqhe', a.astype(v.dtype), v)


def diff_attn_prompt(q, k, v, lam):
    b, s_len = q.shape[0], q.shape[1]
    nblk = s_len // Q_BLOCK
    scale = QK_HEAD_DIM ** -0.5
    key_chunk = jnp.arange(s_len) // CHUNK
    qb = q.reshape(b, nblk, Q_BLOCK, N_HEADS_A, 2, QK_HEAD_DIM).swapaxes(0, 1)

    def block(args):
        q_blk, i = args
        s = jnp.einsum('bqhmd,bkhmd->bhmqk', q_blk, k,
                       preferred_element_type=jnp.float32) * scale
        q_chunk = (i * Q_BLOCK + jnp.arange(Q_BLOCK)) // CHUNK
        mask = q_chunk[:, None] >= key_chunk[None, :]
        s = jnp.where(mask, s, -jnp.inf)
        return diff_combine(s, v, lam)

    out = lax.map(block, (qb, jnp.arange(nblk)))
    return out.swapaxes(0, 1).reshape(b, s_len, N_HEADS_A, V_HEAD_DIM)


def diff_attn_sample(q, k_all, v_all, lam):
    scale = QK_HEAD_DIM ** -0.5
    s = jnp.einsum('bqhmd,bkhmd->bhmqk', q, k_all,
                   preferred_element_type=jnp.float32) * scale
    return diff_combine(s, v_all, lam)


def mem_kv(mem, g_mem, w_xk, w_xv):
    b, n = mem.shape[0], mem.shape[1]
    m = rms_norm(mem, g_mem)
    mk = (m @ w_xk).reshape(b, n, N_HEADS_X, X_HEAD_DIM)
    mv = (m @ w_xv).reshape(b, n, N_HEADS_X, X_HEAD_DIM)
    return mk, mv


def layer(x, pos, past_k, past_v, sc_state, ffn_state, mem_k, mem_v, lam_init,
          g_mix, w_in, lq1, lk1, lq2, lk2, g_sub, w_sc, w_out,
          g_x, w_xq, w_xo, g_ffn, w_up, w_gate, w_ffconv, w_down):
    b, t_len = x.shape[0], x.shape[1]
    h = rms_norm(x, g_mix)
    proj = h @ w_in
    c1 = Q_COLS
    c2 = c1 + K_COLS
    c3 = c2 + V_COLS
    c4 = c3 + CONV_WIDTH
    c5 = c4 + CONV_WIDTH
    q, k, v, bg, cg, xh = jnp.split(proj, [c1, c2, c3, c4, c5], axis=-1)
    q = rope_partial(q.reshape(b, t_len, N_HEADS_A, 2, QK_HEAD_DIM), pos)
    k = rope_partial(k.reshape(b, t_len, N_HEADS_A, 2, QK_HEAD_DIM), pos)
    v = v.reshape(b, t_len, N_HEADS_A, V_HEAD_DIM)
    lam = diff_lambda(lq1, lk1, lq2, lk2, lam_init)
    if past_k is None:
        o = diff_attn_prompt(q, k, v, lam)
    else:
        k_all = jnp.concatenate([past_k.astype(k.dtype), k], axis=1)
        v_all = jnp.concatenate([past_v.astype(v.dtype), v], axis=1)
        o = diff_attn_sample(q, k_all, v_all, lam)
    o = (rms_norm(o, g_sub) * (1.0 - lam_init)).reshape(b, t_len, ATTN_WIDTH)
    conv_y, new_sc = causal_dwconv(cg * xh, sc_state, w_sc)
    y_sc = bg * conv_y
    x = x + jnp.concatenate([o, y_sc], axis=-1) @ w_out
    hq = (rms_norm(x, g_x) @ w_xq).reshape(b, t_len, N_HEADS_X, X_HEAD_DIM)
    s = jnp.einsum('bthd,bmhd->bhtm', hq, mem_k.astype(hq.dtype),
                   preferred_element_type=jnp.float32) * (X_HEAD_DIM ** -0.5)
    p = jax.nn.softmax(s, axis=-1)
    xo = jnp.einsum('bhtm,bmhd->bthd', p.astype(x.dtype), mem_v.astype(x.dtype))
    x = x + xo.reshape(b, t_len, D_MODEL) @ w_xo
    hf = rms_norm(x, g_ffn)
    u_c, new_ffn = causal_dwconv(hf @ w_up, ffn_state, w_ffconv)
    x = x + (jax.nn.silu(u_c) * (hf @ w_gate)) @ w_down
    return x, k, v, new_sc, new_ffn


def setup_inputs(seed: int = 0) -> dict:
    key = jax.random.key(seed)
    ks = jax.random.split(key, 40)
    f = jnp.float32

    def nrm(i, shape, scale=1.0):
        return jax.random.normal(ks[i], shape, f) * scale

    def gain(i, n):
        return 1.0 + 0.02 * jax.random.normal(ks[i], (DEPTH, n), f)

    return {
        "x_prompt": nrm(0, (BATCH, SEQ, D_MODEL)),
        "x_sample": nrm(1, (DEC_BATCH, DEC_SEQ, D_MODEL)),
        "cache_attn_k": nrm(2, (DEPTH, DEC_BATCH, PAST_LEN, N_HEADS_A, 2, QK_HEAD_DIM)),
        "cache_attn_v": nrm(3, (DEPTH, DEC_BATCH, PAST_LEN, N_HEADS_A, V_HEAD_DIM)),
        "state_short_conv": nrm(4, (DEPTH, DEC_BATCH, SC_WIDTH - 1, CONV_WIDTH)),
        "state_ffn_conv": nrm(5, (DEPTH, DEC_BATCH, FFN_CONV_WIDTH - 1, D_FF)),
        "cache_mem_k": nrm(6, (DEPTH, DEC_BATCH, N_MEM, N_HEADS_X, X_HEAD_DIM)),
        "cache_mem_v": nrm(7, (DEPTH, DEC_BATCH, N_MEM, N_HEADS_X, X_HEAD_DIM)),
        "mem_prompt": nrm(8, (BATCH, N_MEM, D_MODEL)),
        "g_mix": gain(9, D_MODEL),
        "w_in": nrm(10, (DEPTH, D_MODEL, IN_PROJ), D_MODEL ** -0.5),
        "lam_q1": nrm(11, (DEPTH, QK_HEAD_DIM), 0.1),
        "lam_k1": nrm(12, (DEPTH, QK_HEAD_DIM), 0.1),
        "lam_q2": nrm(13, (DEPTH, QK_HEAD_DIM), 0.1),
        "lam_k2": nrm(14, (DEPTH, QK_HEAD_DIM), 0.1),
        "g_sub": gain(15, V_HEAD_DIM),
        "w_sc": nrm(16, (DEPTH, SC_WIDTH, CONV_WIDTH), SC_WIDTH ** -0.5),
        "w_out": nrm(17, (DEPTH, MIX_WIDTH, D_MODEL), MIX_WIDTH ** -0.5),
        "g_mem": gain(18, D_MODEL),
        "g_x": gain(19, D_MODEL),
        "w_xq": nrm(20, (DEPTH, D_MODEL, D_MODEL), D_MODEL ** -0.5),
        "w_xk": nrm(21, (DEPTH, D_MODEL, D_MODEL), D_MODEL ** -0.5),
        "w_xv": nrm(22, (DEPTH, D_MODEL, D_MODEL), D_MODEL ** -0.5),
        "w_xo": nrm(23, (DEPTH, D_MODEL, D_MODEL), D_MODEL ** -0.5),
        "g_ffn": gain(24, D_MODEL),
        "w_up": nrm(25, (DEPTH, D_MODEL, D_FF), D_MODEL ** -0.5),
        "w_gate": nrm(26, (DEPTH, D_MODEL, D_FF), D_MODEL ** -0.5),
        "w_ffconv": nrm(27, (DEPTH, FFN_CONV_WIDTH, D_FF), FFN_CONV_WIDTH ** -0.5),
        "w_down": nrm(28, (DEPTH, D_FF, D_MODEL), D_FF ** -0.5),
        "g_final": 1.0 + 0.02 * jax.random.normal(ks[29], (D_MODEL,), f),
    }


def reference(x_prompt, x_sample, cache_attn_k, cache_attn_v, state_short_conv, state_ffn_conv,
              cache_mem_k, cache_mem_v, mem_prompt,
              g_mix, w_in, lam_q1, lam_k1, lam_q2, lam_k2, g_sub, w_sc, w_out,
              g_mem, g_x, w_xq, w_xk, w_xv, w_xo, g_ffn, w_up, w_gate, w_ffconv, w_down,
              g_final):
    b_p, s_len = x_prompt.shape[0], x_prompt.shape[1]
    b_s, t_len = x_sample.shape[0], x_sample.shape[1]
    past_len = cache_attn_k.shape[2]
    pos_p = jnp.arange(s_len, dtype=jnp.int32)
    pos_s = past_len + jnp.arange(t_len, dtype=jnp.int32)

    hp, hs = x_prompt, x_sample
    kp_l, vp_l, scp_l, ffp_l, mkp_l, mvp_l = [], [], [], [], [], []
    ks_l, vs_l, scs_l, ffs_l = [], [], [], []
    for l in range(DEPTH):
        lam_init = lambda_init_fn(l)
        shared = (g_mix[l], w_in[l], lam_q1[l], lam_k1[l], lam_q2[l], lam_k2[l], g_sub[l],
                  w_sc[l], w_out[l], g_x[l], w_xq[l], w_xo[l], g_ffn[l], w_up[l], w_gate[l],
                  w_ffconv[l], w_down[l])
        mk_p, mv_p = mem_kv(mem_prompt, g_mem[l], w_xk[l], w_xv[l])
        sc0 = jnp.zeros((b_p, SC_WIDTH - 1, CONV_WIDTH), hp.dtype)
        ff0 = jnp.zeros((b_p, FFN_CONV_WIDTH - 1, D_FF), hp.dtype)
        hp, k_p, v_p, sc_p, ff_p = layer(hp, pos_p, None, None, sc0, ff0, mk_p, mv_p,
                                         lam_init, *shared)
        kp_l.append(k_p); vp_l.append(v_p); scp_l.append(sc_p); ffp_l.append(ff_p)
        mkp_l.append(mk_p); mvp_l.append(mv_p)
        hs, k_s, v_s, sc_s, ff_s = layer(hs, pos_s, cache_attn_k[l], cache_attn_v[l],
                                         state_short_conv[l], state_ffn_conv[l],
                                         cache_mem_k[l], cache_mem_v[l], lam_init, *shared)
        ks_l.append(k_s); vs_l.append(v_s); scs_l.append(sc_s); ffs_l.append(ff_s)

    y_prompt = rms_norm(hp, g_final)
    y_sample = rms_norm(hs, g_final)
    return (y_prompt, y_sample,
            jnp.stack(kp_l), jnp.stack(vp_l), jnp.stack(scp_l), jnp.stack(ffp_l),
            jnp.stack(mkp_l), jnp.stack(mvp_l),
            jnp.stack(ks_l), jnp.stack(vs_l), jnp.stack(scs_l), jnp.stack(ffs_l))
```

```python
import math
from contextlib import ExitStack
import numpy as np
import concourse.bass as bass
import concourse.mybir as mybir
from concourse.bass_utils import run_bass_kernel_spmd

F32 = mybir.dt.float32
BF16 = mybir.dt.bfloat16
AF = mybir.ActivationFunctionType
ALU = mybir.AluOpType

NCORES = 8
D = 1024
SEQ = 2048
NSEQ = 4
TT = 512
DEC = 16
PAST = 1024
DFF = 2816
NFC = 22
NMEM = 256
EPS = 1e-6
LAM_INIT = 0.8 - 0.6 * math.exp(-0.3 * 0)
NEG = -30000.0
WINDOW = 2
ENGS = ('pe', 'act', 'dve', 'pool', 'sp')


class Buf:
    __slots__ = ('name', 'lw', 'rd', 'al', 'excl')

    def __init__(self, name, excl=False):
        self.name = name
        self.excl = excl
        self.lw = None
        self.rd = {}
        self.al = [self]


class Op:
    __slots__ = ('eng', 'fn', 'key', 'eidx', 'gidx', 'deps', 'sig', 'cnt', 'dcount')


class Prog:
    def __init__(self, nc):
        self.nc = nc
        self.ops = {e: [] for e in ENGS}
        self.all = []
        self.tag = ''
        self.petags = []

    def add(self, eng, fn, rd=(), wr=(), key=None):
        op = Op()
        op.eng = eng
        op.fn = fn
        op.key = key
        op.eidx = len(self.ops[eng])
        op.gidx = len(self.all)
        op.sig = False
        op.cnt = 0
        op.dcount = 0
        deps = {}
        xw = [b for b in rd if b.excl]
        if xw:
            rd = [b for b in rd if not b.excl]
            wr = list(wr) + [b for b in xw if b not in wr]

        def dep(d):
            if d is None:
                return
            k = ('d', id(d.key), d.eng) if d.key is not None else d.eng
            o = deps.get(k)
            if o is None or o.gidx < d.gidx:
                deps[k] = d

        for b in rd:
            for a in b.al:
                dep(a.lw)
        for b in wr:
            for a in b.al:
                dep(a.lw)
                for r in a.rd.values():
                    dep(r)
        op.deps = list(deps.values())
        mk = ('d', id(key), eng) if key is not None else eng
        for b in rd:
            b.rd[mk] = op
        for b in wr:
            b.lw = op
            b.rd = {}
        self.ops[eng].append(op)
        self.all.append(op)
        return op

    def dma(self, out, in_, rd, wr, key, q='sp'):
        self.add(q, lambda e: e.dma_start(out=out, in_=in_), rd, wr, key=key)

    def mm(self, items, rd, wr):
        tg = self.tag

        def fn(e):
            ins = None
            for it in items:
                self.petags.append(tg)
                o, l, r, st, sp = it[:5]
                if len(it) > 5:
                    ins = e.matmul(o, l, r, start=st, stop=sp, skip_group_check=True)
                else:
                    ins = e.matmul(o, l, r, start=st, stop=sp)
            return ins
        self.add('pe', fn, rd, wr)

    def tr(self, items, ident, rd, wr):
        tg = self.tag

        def fn(e):
            ins = None
            for (o, i, n) in items:
                self.petags.append(tg)
                ins = e.transpose(o, i, ident[0:n, 0:n])
            return ins
        self.add('pe', fn, rd, wr)

    def act(self, out, in_, func, rd, wr, **kw):
        self.add('act', lambda e: e.activation(out, in_, func, **kw), rd, wr)

    def cp(self, eng, out, in_, rd, wr):
        if eng == 'act':
            self.add('act', lambda e: e.copy(out, in_), rd, wr)
        else:
            self.add(eng, lambda e: e.tensor_copy(out, in_), rd, wr)

    def tt(self, eng, out, a, b, op, rd, wr):
        self.add(eng, lambda e: e.tensor_tensor(out, a, b, op), rd, wr)

    def ts(self, eng, out, a, s1, s2, op0, op1, rd, wr):
        if s2 is None:
            self.add(eng, lambda e: e.tensor_scalar(out, a, s1, None, op0), rd, wr)
        else:
            self.add(eng, lambda e: e.tensor_scalar(out, a, s1, s2, op0, op1), rd, wr)

    def stt(self, eng, out, a, s, b, op0, op1, rd, wr):
        self.add(eng, lambda e: e.scalar_tensor_tensor(out, a, s, b, op0, op1), rd, wr)

    def memset(self, eng, ap, val, wr):
        self.add(eng, lambda e: e.memset(ap, val), (), wr)

    def build(self, es):
        nc = self.nc
        import os
        mx = int(os.environ.get("K_MAXOPS", "0"))
        if mx:
            self.all = [o for o in self.all if o.gidx < mx]
            for e in ENGS:
                self.ops[e] = [o for o in self.ops[e] if o.gidx < mx]
        for op in self.all:
            for d in op.deps:
                if d.key is not None:
                    continue
                if d.eng != op.eng or op.key is not None:
                    d.sig = True
                elif op.eng != 'pe' and d.eidx >= op.eidx - WINDOW:
                    d.sig = True
        esem = {}
        for e in ENGS:
            esem[e] = es.enter_context(nc.semaphore("es_" + e))
            c = 0
            for op in self.ops[e]:
                if op.key is None and op.sig:
                    c += 1
                op.cnt = c
        dsem = {}
        dcnt = {}
        for op in self.all:
            if op.key is not None:
                k = (id(op.key), op.eng)
                if k not in dsem:
                    dsem[k] = es.enter_context(nc.semaphore("ds_%d" % len(dsem)))
                    dcnt[k] = 0
                dcnt[k] += 16
                op.dcount = dcnt[k]
        self.nsem = len(dsem) + len(esem)

        def emit(ename, eng):
            waited = {}
            for op in self.ops[ename]:
                need = {}
                for d in op.deps:
                    if d.key is not None:
                        sem, v = dsem[(id(d.key), d.eng)], d.dcount
                    elif d.eng != ename or op.key is not None:
                        sem, v = esem[d.eng], d.cnt
                    elif ename != 'pe' and d.eidx >= op.eidx - WINDOW:
                        sem, v = esem[ename], d.cnt
                    else:
                        continue
                    if need.get(sem.num, (None, 0))[1] < v:
                        need[sem.num] = (sem, v)
                for num, (sem, v) in need.items():
                    if waited.get(num, 0) < v:
                        eng.wait_ge(sem, v)
                        waited[num] = v
                ins = op.fn(eng)
                if op.key is not None:
                    ins.then_inc(dsem[(id(op.key), op.eng)], 16)
                elif op.sig:
                    ins.then_inc(esem[ename], 1)
            if ename == 'sp':
                for k, sem in dsem.items():
                    if waited.get(sem.num, 0) < dcnt[k]:
                        eng.wait_ge(sem, dcnt[k])

        with nc.Block() as block:
            @block.tensor
            def _(e):
                emit('pe', e)

            @block.scalar
            def _(e):
                emit('act', e)

            @block.vector
            def _(e):
                emit('dve', e)

            @block.gpsimd
            def _(e):
                emit('pool', e)

            @block.sync
            def _(e):
                emit('sp', e)


def build_program(phases=99):
    nc = bass.Bass("TRN2", target_bir_lowering=False)
    es = ExitStack()
    es.enter_context(nc.allow_low_precision("bf16 matmul operands, fp32 accumulation"))
    es.enter_context(nc.allow_non_contiguous_dma("small constant / state layouts"))
    P = Prog(nc)

    def din(name, shape):
        return nc.dram_tensor(name, shape, F32, kind="ExternalInput").ap()

    def dout(name, shape):
        return nc.dram_tensor(name, shape, F32, kind="ExternalOutput").ap()

    xp = din("xp", [NSEQ * SEQ, D])
    xsm = din("xsm", [NSEQ * DEC, D])
    ck = din("ck", [NSEQ, PAST, 512])
    cv = din("cv", [NSEQ, PAST, 512])
    ssc = din("ssc", [NSEQ, 2, 512])
    sff = din("sff", [NSEQ, 2, DFF])
    cmk = din("cmk", [NSEQ, NMEM, D])
    cmv = din("cmv", [NSEQ, NMEM, D])
    memp = din("memp", [NSEQ, NMEM, D])
    g_mix = din("g_mix", [1, D]); g_mem = din("g_mem", [1, D]); g_x = din("g_x", [1, D])
    g_ffn = din("g_ffn", [1, D]); g_final = din("g_final", [1, D]); g_sub = din("g_sub", [1, 128])
    lq1 = din("lam_q1", [1, 64]); lk1 = din("lam_k1", [1, 64])
    lq2 = din("lam_q2", [1, 64]); lk2 = din("lam_k2", [1, 64])
    w_in = din("w_in", [D, 3072]); w_out = din("w_out", [D, D])
    w_xq = din("w_xq", [D, D]); w_xk = din("w_xk", [D, D]); w_xv = din("w_xv", [D, D]); w_xo = din("w_xo", [D, D])
    w_up = din("w_up", [D, DFF]); w_gate = din("w_gate", [D, DFF]); w_down = din("w_down", [DFF, D])
    w_sc = din("w_sc", [3, 512]); w_fc = din("w_ffconv", [3, DFF])
    cos_d = din("rope_cos", [128, 17, 8]); sin_d = din("rope_sin", [128, 17, 8])
    ident_d = din("ident", [128, 128]); cmask_d = din("cmask", [128, 8])

    y_p = dout("y_p", [NSEQ * SEQ, D]); y_s = dout("y_s", [NSEQ * DEC, D])
    k_p = dout("k_p", [NSEQ * SEQ, 512]); v_p = dout("v_p", [NSEQ * SEQ, 512])
    sc_p = dout("sc_p", [NSEQ, 2, 512]); ff_p = dout("ff_p", [NSEQ, 2, DFF])
    mk_p = dout("mk_p", [NSEQ * NMEM, D]); mv_p = dout("mv_p", [NSEQ * NMEM, D])
    k_s = dout("k_s", [NSEQ * DEC, 512]); v_s = dout("v_s", [NSEQ * DEC, 512])
    sc_s = dout("sc_s", [NSEQ, 2, 512]); ff_s = dout("ff_s", [NSEQ, 2, DFF])

    NBLK = {'in': 6, 'out': 2, 'xq': 2, 'xk': 2, 'xv': 2, 'xo': 2, 'ug': 11, 'dn': 6}
    scr = {}
    scrb = {}
    for nm, n in NBLK.items():
        scr[nm] = nc.dram_tensor("scr_" + nm, [n, 128, 8 * 512], BF16, kind="ExternalOutput").ap()
        scrb[nm] = Buf("scr_" + nm)

    def sb(name, shape, dt=F32):
        return es.enter_context(nc.sbuf_tensor("sb_" + name, shape, dt))

    xs_t = [sb("xs%d" % i, [128, 4, D]) for i in range(2)]
    xs_b = [Buf("xs%d" % i) for i in range(2)]
    xsub_b = [[Buf("xs%d_%d" % (i, j)) for j in range(4)] for i in range(2)]
    for i in range(2):
        xs_b[i].al = [xs_b[i]] + xsub_b[i]
        for j in range(4):
            xsub_b[i][j].al = [xsub_b[i][j], xs_b[i]]
    hT = sb("hT", [128, 8, TT], BF16); hT_bj = [Buf("hT%d" % j) for j in range(4)]; hT_b = hT_bj
    KT = sb("KT", [128, 4, SEQ], BF16); KT_b = [Buf("KT%d" % i) for i in range(4)]
    V1 = sb("V1", [128, 16, 4, 130], BF16); V1_b = [Buf("V1%d" % i) for i in range(4)]
    ring_t = [sb("ring%d" % i, [128, 8, 512], BF16) for i in range(4)]
    ring_b = [Buf("ring%d" % i) for i in range(4)]
    cst_t = [sb("cst%d" % i, [128, 512]) for i in range(4)]
    cst_b = [Buf("cst%d" % i) for i in range(4)]
    MKT = sb("MKT", [128, 8, NMEM], BF16); MKT_b = Buf("MKT")
    MV = sb("MV", [128, 2, D], BF16); MV_b = Buf("MV")
    ident32 = sb("ident32", [128, 128]); ident = sb("ident", [128, 128], BF16); ident_b = Buf("ident")
    ones = sb("ones", [128, 128], BF16)
    cosT = sb("cosT", [128, 17, 8]); sinT = sb("sinT", [128, 17, 8]); cmask = sb("cmask", [128, 8])
    const_b = Buf("const")
    gcol = sb("gcol", [128, 5, 8])
    gsub1 = sb("gsub1", [128, 1])
    epsc = sb("epsc", [128, 1])
    gfin = sb("gfin", [128, 1, D])
    lamv = sb("lamv", [128, 4, 64]); lamt = sb("lamt", [128, 2, 64]); lams = sb("lams", [128, 4])
    wsc = sb("wsc", [128, 4, 3]); wfc = sb("wfc", [128, NFC, 3])
    uhalo = sb("uhalo", [128, 4, 2]); uhalo_b = Buf("uhalo")
    uphalo = sb("uphalo", [128, NFC, 2]); uphalo_b = Buf("uphalo")
    sphalo = sb("sphalo", [128, NFC, 4, 2]); sphalo_b = Buf("sphalo")
    stat = sb("stat", [128, 128]); stat_b = [Buf("stat%d" % i) for i in range(8)]
    junk = sb("junk", [128, 4, D], mybir.dt.float8e4); junk_b = Buf("junk")
    hb_t = [sb("hb%d" % i, [128, D], BF16) for i in range(2)]
    hb_b = [Buf("hb%d" % i) for i in range(2)]

    OVL = 69 * 1024
    ovl = sb("ovl", [128, OVL // 2], BF16)
    ovl_bufs = []

    def ov(name, off, nbytes, dt, pattern=None, **kw):
        a = ovl[:, off // 2:(off + nbytes) // 2]
        if dt == F32:
            a = a.bitcast(F32)
        if pattern:
            a = a.rearrange(pattern, **kw)
        b = Buf(name)
        ovl_bufs.append((b, off, off + nbytes))
        return a, b

    K = 1024
    qT, qT_b = ov("qT", 0, 4 * K, BF16, "p (h t) -> p h t", h=4)
    catT, catT_b = ov("catT", 4 * K, 8 * K, BF16, "p (k t) -> p k t", k=8)
    cg32, cg32_b = ov("cg32", 12 * K, 8 * K, F32, "p (c t) -> p c t", c=4)
    ubuf, ubuf_b = ov("ubuf", 20 * K, 8 * K + 64, F32)
    kst, kst_b = ov("kst", 29 * K, 8 * K, F32, "p (j c) -> p j c", j=4)
    vst, vst_b = ov("vst", 37 * K, 8 * K, F32, "p (j c) -> p j c", j=4)
    q32x, q32x_b = ov("q32x", 49 * K, 8 * K, F32, "p (j c) -> p j c", j=4)
    qb16x, qb16x_b = ov("qb16x", 57 * K, 4 * K, BF16, "p (j c) -> p j c", j=4)
    kb16x, kb16x_b = ov("kb16x", 61 * K, 4 * K, BF16, "p (j c) -> p j c", j=4)
    rtmp, rtmp_b = ov("rtmp", 65 * K, 4 * K, F32, "p (a j b c) -> p a j b c", a=4, j=4, b=8)
    ptA = []
    for i in range(8):
        ptA.append(ov("pt%d" % i, 49 * K + i * K, K, BF16))
    ptP = []
    for i in range(4):
        ptP.append(ov("ptp%d" % i, 49 * K + 2 * i * K, 2 * K, BF16, "p (m q) -> p m q", m=2))
    t1, t1_b = ov("t1", 57 * K, 2 * K, F32, "p (q e) -> p q e", q=4)
    o32, o32_b = ov("o32", 59 * K, 2 * K, F32, "p (q e) -> p q e", q=4)
    otm, otm_b = ov("otm", 61 * K, K, BF16, "p (q e) -> p q e", q=4)
    ckst, ckst_b = ov("ckst", 29 * K, 16 * K, F32, "p (j c) -> p j c", j=8)
    ckb, ckb_b = ov("ckb", 65 * K, 4 * K, BF16, "p (j c) -> p j c", j=4)
    ckst2, ckst2_b = ov("ckst2", 12 * K, 16 * K, F32, "p (j c) -> p j c", j=8)
    hqT, hqT_b = ov("hqT", 0, 8 * K, BF16, "p (k t) -> p k t", k=8)
    xoT, xoT_b = ov("xoT", 8 * K, 8 * K, BF16, "p (k t) -> p k t", k=8)
    ptB = []
    for i in range(4):
        ptB.append(ov("ptB%d" % i, 16 * K + i * K, K, BF16))
    rdens = []
    for i in range(2):
        rdens.append(ov("rden%d" % i, 20 * K + i * 2 * K, 2 * K, F32))
    mld, mld_b = ov("mld", 24 * K, 8 * K, F32, "p (j c) -> p j c", j=2)
    mst, mst_b = ov("mst", 32 * K, 8 * K, F32, "p (j c) -> p j c", j=2)
    mb16, mb16_b = ov("mb16", 40 * K, 4 * K, BF16, "p (j c) -> p j c", j=2)
    mT, mT_b = ov("mT", 44 * K, 4 * K, BF16, "p (k t) -> p k t", k=8)
    mkb, mkb_b = ov("mkb", 48 * K, 2 * K, BF16)
    gT = ovl[:, 0:11 * K].rearrange("p (k t) -> p k t", k=NFC)
    gT_b3 = [ov("gT%d" % i, i * 8 * K, (8 if i < 2 else 6) * K, BF16)[1] for i in range(3)]
    upb = []
    for i in range(2):
        upb.append(ov("upb%d" % i, 22 * K + i * (2 * K + 64), 2 * K + 64, F32))
    cvb = []
    for i in range(2):
        cvb.append(ov("cvb%d" % i, 27 * K + i * 2 * K, 2 * K, F32))
    slb = []
    for i in range(2):
        slb.append(ov("slb%d" % i, 31 * K + i * 2 * K, 2 * K, F32))
    for (b, lo, hi) in ovl_bufs:
        b.al = [b2 for (b2, lo2, hi2) in ovl_bufs if lo2 < hi and lo < hi2]

    pacc = es.enter_context(nc.psum_tensor("pacc", [128, 2048], F32))
    pb = [pacc[:, i * 512:(i + 1) * 512] for i in range(4)]
    pst = es.enter_context(nc.psum_tensor("pst", [128, 2048], F32))
    pb += [pst[:, i * 512:(i + 1) * 512] for i in range(4)]
    pb_b = [Buf("pb%d" % i, excl=True) for i in range(8)]

    class Rot:
        def __init__(self, ids):
            self.ids = ids
            self.i = 0

        def nxt(self):
            r = self.ids[self.i % len(self.ids)]
            self.i += 1
            return r

    cb = [const_b]
    P.dma(ident32[:, :], ident_d[:, :], (), cb, const_b)
    P.dma(cosT[:, :, :], cos_d[:, :, :], (), cb, const_b)
    P.dma(sinT[:, :, :], sin_d[:, :, :], (), cb, const_b)
    P.dma(cmask[:, :], cmask_d[:, :], (), cb, const_b)
    for i, g in enumerate((g_mix, g_mem, g_x, g_ffn)):
        P.dma(gcol[:, i, :], g[0, :].rearrange("(k p) -> p k", p=128), (), cb, const_b)
    P.dma(gsub1[:, :], g_sub[0, :].rearrange("(p o) -> p o", o=1), (), cb, const_b)
    P.dma(gfin[:, :, :], g_final[0:1, :].partition_broadcast(128), (), cb, const_b)
    for i, l in enumerate((lq1, lk1, lq2, lk2)):
        P.dma(lamv[:, i:i + 1, :], l[0:1, :].partition_broadcast(128), (), cb, const_b)
    for j_ in range(3):
        P.dma(wsc[:, :, j_], w_sc[j_, :].rearrange("(c p) -> p c", p=128), (), cb, const_b)
        P.dma(wfc[:, :, j_], w_fc[j_, :].rearrange("(c p) -> p c", p=128), (), cb, const_b)

    def dma_state_out(dram2, sb3, rd, key):
        for r_ in range(2):
            P.dma(dram2[r_, :].rearrange("(c p) -> p c", p=128), sb3[:, :, r_], rd, (), key)

    def dma_state_in(sb3, dram2, wr, key):
        for r_ in range(2):
            P.dma(sb3[:, :, r_], dram2[r_, :].rearrange("(c p) -> p c", p=128), (), wr, key)
    P.cp('dve', ident[:, :], ident32[:, :], cb, [ident_b])
    P.memset('pool', ones[:, :], 1.0, [ident_b])
    P.memset('pool', epsc[:, :], EPS, cb)
    P.memset('pool', V1[:, :, :, :], 1.0, V1_b)
    P.memset('dve', gcol[:, 4, :], 1.0, cb)
    P.ts('dve', gcol[:, 4, 0:4], gcol[:, 4, 0:4], gsub1[:, 0:1], 1.0 - LAM_INIT, ALU.mult, ALU.mult, cb, cb)
    P.tt('dve', lamt[:, 0, :], lamv[:, 0, :], lamv[:, 1, :], ALU.mult, cb, cb)
    P.tt('dve', lamt[:, 1, :], lamv[:, 2, :], lamv[:, 3, :], ALU.mult, cb, cb)
    P.add('dve', lambda e: e.reduce_sum(lams[:, 0:2], lamt[:, :, :], mybir.AxisListType.X), cb, cb)
    P.act(lams[:, 0:2], lams[:, 0:2], AF.Exp, cb, cb)
    P.tt('dve', lams[:, 2:3], lams[:, 0:1], lams[:, 1:2], ALU.subtract, cb, cb)
    P.ts('dve', lams[:, 3:4], lams[:, 2:3], LAM_INIT, -1.0, ALU.add, ALU.mult, cb, cb)
    neg_lam = lams[:, 3:4]

    converted = set()
    wstate = {'slot': 0, 'cst': 0, 'ptr': 0, 'issued': 0}
    wseq = []
    gidx = {'in': 0, 'xk': 1, 'xv': 1, 'xq': 2, 'ug': 3, 'out': 4}

    def wsrc(nm, b, k):
        r0 = k * 128
        if nm == 'ug':
            return [(0, 256, w_up[r0:r0 + 128, b * 256:(b + 1) * 256]),
                    (256, 256, w_gate[r0:r0 + 128, b * 256:(b + 1) * 256])]
        if nm == 'dn':
            c, kb = b // 3, b % 3
            r0 = kb * 1024 + k * 128
            return [(0, 512, w_down[r0:r0 + 128, c * 512:(c + 1) * 512])]
        w = {'in': w_in, 'out': w_out, 'xq': w_xq, 'xk': w_xk, 'xv': w_xv, 'xo': w_xo}[nm]
        return [(0, 512, w[r0:r0 + 128, b * 512:(b + 1) * 512])]

    def wissue():
        i = wstate['issued']
        if i >= len(wseq):
            return
        nm, b = wseq[i]
        wstate['issued'] += 1
        s = i % 4
        slot, sbuf_ = ring_t[s], ring_b[s]
        if (nm, b) in converted:
            P.dma(slot[:, :, :], scr[nm][b].rearrange("p (k c) -> p k c", k=8), [scrb[nm]], [sbuf_], sbuf_)
        else:
            converted.add((nm, b))
            nk = 6 if (nm == 'dn' and b % 3 == 2) else 8
            for k in range(nk):
                ci = wstate['cst'] % 4
                wstate['cst'] += 1
                for (c0, ncol, src) in wsrc(nm, b, k):
                    P.dma(cst_t[ci][:, c0:c0 + ncol], src, (), [cst_b[ci]], cst_b[ci])
                dst = slot[:, k, :]
                if nm in gidx:
                    gs = gcol[:, gidx[nm], k:k + 1]
                    if k % 2:
                        P.act(dst, cst_t[ci][:, :], AF.Copy, [cst_b[ci], const_b], [sbuf_], scale=gs)
                    else:
                        P.ts('dve', dst, cst_t[ci][:, :], gs, None, ALU.mult, None, [cst_b[ci], const_b], [sbuf_])
                else:
                    P.cp('act' if k % 2 else 'dve', dst, cst_t[ci][:, :], [cst_b[ci]], [sbuf_])
            P.dma(scr[nm][b].rearrange("p (k c) -> p k c", k=8), slot[:, :, :], [sbuf_], [scrb[nm]], sbuf_, q='pool')

    def wget(nm, b, held=0):
        i = wstate['ptr']
        assert wseq[i] == (nm, b), (wseq[i], nm, b)
        wstate['ptr'] += 1
        while wstate['issued'] < min(len(wseq), i + 4 - held):
            wissue()
        s = i % 4
        return ring_t[s], ring_b[s]

    class Tile:
        pass

    tiles = []
    for s in range(NSEQ):
        for t in range(SEQ // TT):
            tl = Tile()
            tl.kind = 'p'; tl.s = s; tl.t = t
            tl.NS = 4; tl.PT = 128; tl.NT = 512; tl.G = 1; tl.L = 512
            tl.first = (t == 0); tl.last = (t == SEQ // TT - 1)
            tiles.append(tl)
    tl = Tile()
    tl.kind = 's'; tl.s = 0; tl.t = 0
    tl.NS = 1; tl.PT = 64; tl.NT = 64; tl.G = 4; tl.L = 16
    tl.first = True; tl.last = True
    if phases < 50:
        tiles = tiles[:phases]
    tiles = [tl] + tiles

    for tl in tiles:
        if tl.kind == 'p' and tl.first:
            wseq += [('xk', 0), ('xk', 1), ('xv', 0), ('xv', 1)]
        wseq += [('in', 0), ('in', 1), ('in', 2), ('in', 4), ('in', 5), ('in', 3)]
        wseq += [('out', 0), ('out', 1), ('xq', 0), ('xq', 1), ('xo', 0), ('xo', 1)]
        wseq += [('ug', i) for i in range(11)]
        wseq += [('dn', i) for i in range(6)]

    mmrot = Rot([0, 1, 2, 3, 6, 7])
    tprot = Rot([4, 5])
    strot = Rot([4, 5, 6, 7])
    hbrot = Rot([0, 1])
    statrot = Rot(list(range(8)))

    def newstat():
        i = statrot.nxt()
        return stat[:, i * 16:(i + 1) * 16], stat_b[i]

    def rstd_from_ss(ssap, n, sbuf_, inv_n):
        P.act(ssap, ssap, AF.Ln, [sbuf_, const_b], [sbuf_], bias=epsc[0:ssap.shape[0], 0:1], scale=inv_n)
        P.act(ssap, ssap, AF.Exp, [sbuf_], [sbuf_], scale=-0.5)

    def sq_group(xt, PT, NS, ss, xb, ssb):
        def fn(e):
            ins = None
            for j in range(NS):
                ins = e.activation(junk[0:PT, j, :], xt[0:PT, j, :], AF.Square, accum_out=ss[0:PT, j:j + 1], saturate=False)
            return ins
        P.add('act', fn, [xb], [ssb])

    def norm_to_hT(tl, xt, xb):
        PT, NS = tl.PT, tl.NS
        for j in range(NS):
            ss, ssb = newstat()
            P.act(junk[0:PT, j, :], xt[0:PT, j, :], AF.Square, [xb[j]], [ssb], accum_out=ss[0:PT, 0:1], saturate=False)
            rstd_from_ss(ss[0:PT, 0:1], 1, ssb, 1.0 / D)
            hi = hbrot.nxt()
            P.act(hb_t[hi][0:PT, :], xt[0:PT, j, :], AF.Copy, [xb[j], ssb], [hb_b[hi]], scale=ss[0:PT, 0:1])
            bk = tprot.nxt()
            pv = pb[bk].bitcast(BF16).rearrange("p (k t) -> p k t", k=8)
            P.tr([(pv[:, k, 0:PT], hb_t[hi][0:PT, k * 128:(k + 1) * 128], PT) for k in range(8)],
                 ident, [hb_b[hi], ident_b], [pb_b[bk]])
            P.cp('dve', hT[:, :, j * PT:(j + 1) * PT], pv[:, :, 0:PT], [pb_b[bk]], [hT_bj[j]])

    def proj_resid(tl, nm, src, srcb, xt, xb):
        Ws = [wget(nm, 0), wget(nm, 1, held=1)]
        for j in range(tl.NS):
            for c in range(2):
                W, Wb = Ws[c]
                bk = mmrot.nxt()
                o = pb[bk][0:tl.PT, :]
                P.mm([(o, src[:, k, j * tl.PT:(j + 1) * tl.PT], W[:, k, :], k == 0, k == 7) for k in range(8)],
                     [srcb, Wb], [pb_b[bk]])
                xsl = xt[0:tl.PT, j, c * 512:(c + 1) * 512]
                P.tt('dve', xsl, o, xsl, ALU.add, [pb_b[bk], xb[j]], [xb[j]])

    def proj_tokmajor(tl, W, Wb, src, srcb, evac):
        for j in range(tl.NS):
            bk = mmrot.nxt()
            o = pb[bk][0:tl.PT, :]
            sbj = [srcb[j]] if isinstance(srcb, list) else [srcb]
            P.mm([(o, src[:, k, j * tl.PT:(j + 1) * tl.PT], W[:, k, :], k == 0, k == 7) for k in range(8)],
                 sbj + [Wb], [pb_b[bk]])
            evac(j, o, pb_b[bk])

    def proj_featmajor(tl, W, Wb, c0, nchunk, src, srcb, evac, rot=None):
        rot = rot or mmrot
        for c in range(nchunk):
            bk = rot.nxt()
            o = pb[bk][:, 0:tl.NT]
            sbl = list(srcb) if isinstance(srcb, list) else [srcb]
            P.mm([(o, W[:, k, c0 + c * 128:c0 + (c + 1) * 128], src[:, k, 0:tl.NT], k == 0, k == 7) for k in range(8)],
                 sbl + [Wb], [pb_b[bk]])
            evac(c, o, pb_b[bk])

    def rope(tl, buf4, bufb, jj0):
        PT, NS = tl.PT, tl.NS
        x1 = buf4[0:PT, 0:NS, :, 0:8]
        x2 = buf4[0:PT, 0:NS, :, 8:16]
        c = cosT[0:PT, jj0:jj0 + NS, :].unsqueeze(2).to_broadcast([PT, NS, 8, 8])
        s = sinT[0:PT, jj0:jj0 + NS, :].unsqueeze(2).to_broadcast([PT, NS, 8, 8])
        ta, tb_, tc, td = (rtmp[0:PT, i, 0:NS, :, :] for i in range(4))
        rw = [bufb, rtmp_b, const_b]
        P.tt('dve', ta, x1, c, ALU.mult, rw, [rtmp_b])
        P.tt('dve', tb_, x2, s, ALU.mult, rw, [rtmp_b])
        P.tt('dve', tc, x2, c, ALU.mult, rw, [rtmp_b])
        P.tt('dve', td, x1, s, ALU.mult, rw, [rtmp_b])
        P.tt('dve', x1, ta, tb_, ALU.subtract, [rtmp_b], [bufb])
        P.tt('dve', x2, tc, td, ALU.add, [rtmp_b], [bufb])

    def conv3(eng, out, u3, wv, c, G, L, rd, wr):
        P.ts('dve', out, u3[:, :, 0:L], wv[:, c, 0:1], None, ALU.mult, None, rd, wr)
        P.stt('dve', out, u3[:, :, 1:L + 1], wv[:, c, 1:2], out, ALU.mult, ALU.add, rd + wr, wr)
        P.stt('dve', out, u3[:, :, 2:L + 2], wv[:, c, 2:3], out, ALU.mult, ALU.add, rd + wr, wr)

    def mem_phase(tl):
        s = tl.s
        P.dma(mld[:, :, :], memp[s].rearrange("(j p) d -> p j d", p=128), (), [mld_b], mld_b)
        ss, ssb = newstat()
        sq_group(mld, 128, 2, ss, mld_b, ssb)
        rstd_from_ss(ss[:, 0:2], 2, ssb, 1.0 / D)
        for j in range(2):
            P.act(mb16[:, j, :], mld[:, j, :], AF.Copy, [mld_b, ssb], [mb16_b], scale=ss[:, j:j + 1])
            bk = tprot.nxt()
            pv = pb[bk].bitcast(BF16).rearrange("p (k t) -> p k t", k=8)
            P.tr([(pv[:, k, :], mb16[:, j, k * 128:(k + 1) * 128], 128) for k in range(8)],
                 ident, [mb16_b, ident_b], [pb_b[bk]])
            P.cp('dve', mT[:, :, j * 128:(j + 1) * 128], pv, [pb_b[bk]], [mT_b])
        for which, outd in (('xk', mk_p), ('xv', mv_p)):
            for c in range(2):
                W, Wb = wget(which, c)
                for j in range(2):
                    bk = mmrot.nxt()
                    o = pb[bk][:, :]
                    P.mm([(o, mT[:, k, j * 128:(j + 1) * 128], W[:, k, :], k == 0, k == 7) for k in range(8)],
                         [mT_b, Wb], [pb_b[bk]])
                    P.cp('act', mst[:, j, c * 512:(c + 1) * 512], o, [pb_b[bk]], [mst_b])
                    if which == 'xv':
                        P.cp('dve', MV[:, j, c * 512:(c + 1) * 512], o, [pb_b[bk]], [MV_b])
                    else:
                        P.cp('dve', mkb[:, 0:512], o, [pb_b[bk]], [mkb_b])
                        tk = tprot.nxt()
                        pv = pb[tk].bitcast(BF16).rearrange("p (k t) -> p k t", k=8)
                        P.tr([(pv[:, k, :], mkb[:, k * 128:(k + 1) * 128], 128) for k in range(4)],
                             ident, [mkb_b, ident_b], [pb_b[tk]])
                        P.cp('act', MKT[:, c * 4:(c + 1) * 4, j * 128:(j + 1) * 128], pv[:, 0:4, :], [pb_b[tk]], [MKT_b])
            P.dma(outd[s * NMEM:(s + 1) * NMEM, :].rearrange("(j p) d -> p j d", p=128), mst[:, :, :], [mst_b], (), mst_b, q='pool')

    def mem_sample_load(s):
        P.dma(mld[:, :, :], cmk[s].rearrange("(j p) d -> p j d", p=128), (), [mld_b], mld_b)
        P.dma(mst[:, :, :], cmv[s].rearrange("(j p) d -> p j d", p=128), (), [mst_b], mst_b)

    def mem_sample(s):
        for j in range(2):
            P.cp('dve', mb16[:, j, :], mld[:, j, :], [mld_b], [mb16_b])
            bk = tprot.nxt()
            pv = pb[bk].bitcast(BF16).rearrange("p (k t) -> p k t", k=8)
            P.tr([(pv[:, k, :], mb16[:, j, k * 128:(k + 1) * 128], 128) for k in range(8)],
                 ident, [mb16_b, ident_b], [pb_b[bk]])
            P.cp('act', MKT[:, :, j * 128:(j + 1) * 128], pv, [pb_b[bk]], [MKT_b])
        for j in range(2):
            P.cp('act', MV[:, j, :], mst[:, j, :], [mst_b], [MV_b])
        if s + 1 < NSEQ:
            mem_sample_load(s + 1)

    def phase_A(tl, xt, xb):
        PT, NS, NT, G, L = tl.PT, tl.NS, tl.NT, tl.G, tl.L
        prompt = tl.kind == 'p'
        tq = tl.t if prompt else 0
        kcol0 = tl.t * TT if prompt else PAST
        jj0 = tl.t * 4 if prompt else 16
        W, Wb = wget('in', 0)

        def evq(j, o, ob):
            P.cp('act', q32x[0:PT, j, :], o, [ob], [q32x_b])
        proj_tokmajor(tl, W, Wb, hT, hT_b, evq)
        W, Wb = wget('in', 1)

        def evk(j, o, ob):
            P.cp('act', kst[0:PT, j, :], o, [ob], [kst_b])
        proj_tokmajor(tl, W, Wb, hT, hT_b, evk)
        rope(tl, q32x.rearrange("p j (a d) -> p j a d", a=8), q32x_b, jj0)
        P.cp('dve', qb16x[0:PT, 0:NS, :], q32x[0:PT, 0:NS, :], [q32x_b], [qb16x_b])
        rope(tl, kst.rearrange("p j (a d) -> p j a d", a=8), kst_b, jj0)
        P.cp('dve', kb16x[0:PT, 0:NS, :], kst[0:PT, 0:NS, :], [kst_b], [kb16x_b])
        if prompt:
            r0 = tl.s * SEQ + tl.t * TT
            P.dma(k_p[r0:r0 + TT, :].rearrange("(j p) c -> p j c", p=128), kst[:, :, :], [kst_b], (), kst_b, q='pool')
        else:
            P.dma(k_s[:, :], kst[0:64, 0, :], [kst_b], (), kst_b)
        W, Wb = wget('in', 2)

        def evv(j, o, ob):
            P.cp('act', vst[0:PT, j, :], o, [ob], [vst_b])
            kt = tl.t * 4 + j if prompt else 8
            P.cp('dve', V1[0:PT, kt, :, 0:128], vst[0:PT, j, :].rearrange("p (h e) -> p h e", h=4), [vst_b], [V1_b[tq if prompt else 2]])
        proj_tokmajor(tl, W, Wb, hT, hT_b, evv)
        if prompt:
            P.dma(v_p[r0:r0 + TT, :].rearrange("(j p) c -> p j c", p=128), vst[:, :, :], [vst_b], (), vst_b, q='pool')
        else:
            P.dma(v_s[:, :], vst[0:64, 0, :], [vst_b], (), vst_b)
        u4 = ubuf[:, 0:4 * G * (L + 2)].rearrange("p (c g l) -> p c g l", c=4, g=G)
        if prompt:
            if tl.first:
                P.memset('pool', u4[:, :, 0, 0:2], 0.0, [ubuf_b])
            else:
                P.cp('pool', u4[:, :, 0, 0:2], uhalo[:, :, :], [uhalo_b], [ubuf_b])
        else:
            for s in range(NSEQ):
                dma_state_in(u4[:, :, s, 0:2], ssc[s], [ubuf_b], ubuf_b)
        W, Wb = wget('in', 4)

        def evcg(c, o, ob):
            P.cp('act', cg32[:, c, 0:NT], o, [ob], [cg32_b])
        proj_featmajor(tl, W, Wb, 0, 4, hT, hT_b, evcg)
        for j in range(NS):
            bk = tprot.nxt()
            pv = pb[bk].bitcast(BF16).rearrange("p (k t) -> p k t", k=8)
            P.tr([(pv[:, h, 0:PT], qb16x[0:PT, j, h * 128:(h + 1) * 128], PT) for h in range(4)],
                 ident, [qb16x_b, ident_b], [pb_b[bk]])
            P.cp('dve', qT[:, :, j * PT:(j + 1) * PT], pv[:, 0:4, 0:PT], [pb_b[bk]], [qT_b])
        W, Wb = wget('in', 5)

        def evxh(c, o, ob):
            P.tt('dve', u4[:, c, :, 2:L + 2], cg32[:, c, 0:NT].rearrange("p (g l) -> p g l", g=G),
                 o.rearrange("p (g l) -> p g l", g=G), ALU.mult, [ob, cg32_b], [ubuf_b])
            conv3('pool', cg32[:, c, 0:NT].rearrange("p (g l) -> p g l", g=G), u4[:, c, :, :], wsc, c, G, L,
                  [ubuf_b, const_b], [cg32_b])
        proj_featmajor(tl, W, Wb, 0, 4, hT, hT_b, evxh)
        for j in range(NS):
            bk = tprot.nxt()
            pv = pb[bk].bitcast(BF16).rearrange("p (k t) -> p k t", k=8)
            P.tr([(pv[:, h, 0:PT], kb16x[0:PT, j, h * 128:(h + 1) * 128], PT) for h in range(4)],
                 ident, [kb16x_b, ident_b], [pb_b[bk]])
            P.cp('dve', KT[:, :, kcol0 + j * PT:kcol0 + (j + 1) * PT], pv[:, 0:4, 0:PT], [pb_b[bk]], [KT_b[tq]])
        if prompt:
            if tl.last:
                dma_state_out(sc_p[tl.s], u4[:, :, 0, L:L + 2], [ubuf_b], ubuf_b)
            else:
                P.cp('pool', uhalo[:, :, :], u4[:, :, 0, L:L + 2], [ubuf_b], [uhalo_b])
        else:
            for s in range(NSEQ):
                dma_state_out(sc_s[s], u4[:, :, s, L:L + 2], [ubuf_b], ubuf_b)
        W, Wb = wget('in', 3)

        def evbg(c, o, ob):
            P.tt('dve', catT[:, 4 + c, 0:NT], cg32[:, c, 0:NT], o, ALU.mult, [ob, cg32_b], [catT_b])
        proj_featmajor(tl, W, Wb, 0, 4, hT, hT_b, evbg)
        P.tag = 'A_attn'
        if prompt:
            attn_prompt(tl)
        else:
            attn_sample(tl)
        P.tag = 'A_out'
        proj_resid(tl, 'out', catT, catT_b, xt, xb)

    def attn_post_parts(h):
        A = pacc[:, :].rearrange("p (q c) -> p q c", q=4)[:, :, 0:258].rearrange("p q (m e) -> p q m e", m=2)
        accb = [pb_b[i] for i in range(4)]
        st, stb = newstat()
        rec = st[:, 0:8].rearrange("p (q m) -> p q m", q=4)
        rl = st[:, 8:12]
        rs = st[:, 12:16]

        def part1():
            P.add('dve', lambda e: e.reciprocal(rec, A[:, :, :, 128]), accb, [stb])
            P.ts('dve', rl, rec[:, :, 1], neg_lam, None, ALU.mult, None, [stb, const_b], [stb])
            P.tt('dve', t1[:, :, :], A[:, :, 0, 0:128], rec[:, :, 0:1].to_broadcast([128, 4, 128]), ALU.mult, accb + [stb], [t1_b])
            P.tt('dve', o32[:, :, :], A[:, :, 1, 0:128], rl.unsqueeze(2).to_broadcast([128, 4, 128]), ALU.mult, accb + [stb], [o32_b])
            P.tt('dve', o32[:, :, :], o32[:, :, :], t1[:, :, :], ALU.add, [t1_b, o32_b], [o32_b])

        def part2(bk):
            def fn(e):
                ins = None
                for q in range(4):
                    ins = e.activation(junk[:, q, 0:128], o32[:, q, :], AF.Square, accum_out=rs[:, q:q + 1], saturate=False)
                return ins
            P.add('act', fn, [o32_b], [stb])
            rstd_from_ss(rs, 4, stb, 1.0 / 128)

        def part3(bk):
            P.tt('dve', otm[:, :, :], o32[:, :, :], rs.unsqueeze(2).to_broadcast([128, 4, 128]), ALU.mult, [o32_b, stb], [otm_b])
            pv = pb[bk].bitcast(BF16).rearrange("p (k t) -> p k t", k=8)
            P.tr([(pv[:, q, :], otm[:, q, :], 128) for q in range(4)], ident, [otm_b, ident_b], [pb_b[bk]])
            P.cp('dve', catT[:, h, 0:512].rearrange("p (q t) -> p q t", q=4), pv[:, 0:4, :], [pb_b[bk]], [catT_b])
        return part1, part2, part3

    def attn_prompt(tl):
        t = tl.t
        nkt = t * 4 + 4
        steps = [(h, kt) for h in range(4) for kt in range(nkt)]
        stq = {}
        pti = [0]

        def emit_st(i):
            h, kt = steps[i]
            r = kt - t * 4
            q0 = max(0, r) * 128
            N = 512 - q0
            lst = []
            for m in range(2):
                bk = 4 + 2 * (i % 2) + m
                o = pb[bk][:, 0:N]
                P.mm([(o, KT[m * 64:(m + 1) * 64, h, kt * 128:(kt + 1) * 128], qT[m * 64:(m + 1) * 64, h, q0:512], True, True)],
                     [KT_b[kt // 4], qT_b], [pb_b[bk]])
                lst.append((bk, o))
            stq[i] = (lst, r, q0, N)

        def emit_exp_pv(i):
            h, kt = steps[i]
            lst, r, q0, N = stq.pop(i)
            par = i % 2
            pp, ppb = ptP[pti[0] % 4]
            pti[0] += 1
            stv = pst[:, par * 1024:(par + 1) * 1024].rearrange("p (m q) -> p m q", m=2)
            stb = [pb_b[4 + 2 * par], pb_b[5 + 2 * par]]
            P.act(pp[:, :, q0:512], stv[:, :, 0:N], AF.Exp, stb, [ppb], scale=0.125)
            if r >= 0:
                P.memset('dve', pp[64:128, :, q0:q0 + 64], 0.0, [ppb])
            for qb in range(max(0, r), 4):
                items = []
                for m in range(2):
                    A = pb[qb][:, 0:258].rearrange("p (m e) -> p m e", m=2)
                    items.append((A[:, m, :], pp[:, m, qb * 128:(qb + 1) * 128], V1[:, kt, h, 0:129],
                                  kt == 0 and m == 0, kt == t * 4 + qb, True))
                P.mm(items, [ppb, V1_b[kt // 4]], [pb_b[qb]])

        deferred = []
        emit_st(0)
        for i, (h, kt) in enumerate(steps):
            if i + 1 < len(steps):
                emit_st(i + 1)
            emit_exp_pv(i)
            for d in [d for d in deferred if d[0] <= i]:
                d[1](4 + 2 * (i % 2))
                deferred.remove(d)
            if kt == nkt - 1:
                p1, p2, p3 = attn_post_parts(h)
                p1()
                deferred.append((i + 2, p2))
                deferred.append((i + 4, p3))
        for d in deferred:
            d[1](4)

    def attn_sample(tl):
        ptz, ptz_b = ptA[0][0], ptA[0][1]
        ptn, ptn_b = ptA[2][0], ptA[2][1]
        P.dma(ckst[:, :, :], ck[0].rearrange("(j p) c -> p j c", p=128), (), [ckst_b], ckst_b)
        for s in range(NSEQ):
            P.dma(ckst2[:, :, :], cv[s].rearrange("(j p) c -> p j c", p=128), (), [ckst2_b], ckst2_b)
            for half in range(2):
                P.cp('dve', ckb[:, :, :], ckst[:, half * 4:(half + 1) * 4, :], [ckst_b], [ckb_b])
                for j4 in range(4):
                    j = half * 4 + j4
                    bk = tprot.nxt()
                    pv = pb[bk].bitcast(BF16).rearrange("p (k t) -> p k t", k=8)
                    P.tr([(pv[:, h, :], ckb[:, j4, h * 128:(h + 1) * 128], 128) for h in range(4)],
                         ident, [ckb_b, ident_b], [pb_b[bk]])
                    P.cp('dve', KT[:, :, j * 128:(j + 1) * 128], pv[:, 0:4, :], [pb_b[bk]], [KT_b[0]])
            if s + 1 < NSEQ:
                P.dma(ckst[:, :, :], ck[s + 1].rearrange("(j p) c -> p j c", p=128), (), [ckst_b], ckst_b)
            P.cp('act', V1[:, 0:8, :, 0:128], ckst2[:, :, :].rearrange("p j (h e) -> p j h e", h=4), [ckst2_b], [V1_b[0]])
            for h in range(4):
                pz = ptz[:, 0:256].rearrange("p (k m q) -> p k m q", k=8, m=2)
                pn = ptn[0:64, 0:32].rearrange("p (m q) -> p m q", m=2)
                for m in range(2):
                    bk = strot.nxt()
                    sv = pb[bk][:, 0:128].rearrange("p (k q) -> p k q", k=8)
                    nv = pb[bk][0:64, 128:144]
                    its = [(sv[:, kt, :], KT[m * 64:(m + 1) * 64, h, kt * 128:(kt + 1) * 128],
                            qT[m * 64:(m + 1) * 64, h, s * 16:(s + 1) * 16], True, True) for kt in range(8)]
                    its.append((nv, KT[m * 64:(m + 1) * 64, h, PAST:PAST + 64],
                                qT[m * 64:(m + 1) * 64, h, s * 16:(s + 1) * 16], True, True))
                    P.mm(its, [KT_b[0], qT_b], [pb_b[bk]])
                    P.act(pz[:, :, m, :], sv, AF.Exp, [pb_b[bk]], [ptz_b], scale=0.125)
                    P.act(pn[:, m, :], nv, AF.Exp, [pb_b[bk], const_b], [ptn_b], bias=cmask[0:64, 1 + s:2 + s], scale=0.125)
                A = pb[h][0:16, 0:258].rearrange("p (m e) -> p m e", m=2)
                items = []
                for m in range(2):
                    for kt in range(8):
                        items.append((A[:, m, :], pz[:, kt, m, 0:16], V1[:, kt, h, 0:129], kt == 0, False))
                    items.append((A[:, m, :], pn[:, m, :], V1[0:64, 8, h, 0:129], False, True))
                P.mm(items, [ptz_b, ptn_b, V1_b[0], V1_b[2]], [pb_b[h]])
            for h in range(4):
                attn_post_sample(s, h)

    def attn_post_sample(s, h):
        PT = 16
        bk = h
        A = pb[bk][0:PT, 0:258].rearrange("p (m e) -> p m e", m=2)
        st, stb = newstat()
        P.add('dve', lambda e: e.reciprocal(st[0:PT, 0:2], A[:, :, 128]), [pb_b[bk]], [stb])
        P.ts('dve', st[0:PT, 2:3], st[0:PT, 1:2], neg_lam[0:PT, :], None, ALU.mult, None, [stb, const_b], [stb])
        P.ts('dve', t1[0:PT, 0, :], A[:, 0, 0:128], st[0:PT, 0:1], None, ALU.mult, None, [pb_b[bk], stb], [t1_b])
        P.stt('dve', o32[0:PT, 0, :], A[:, 1, 0:128], st[0:PT, 2:3], t1[0:PT, 0, :], ALU.mult, ALU.add,
              [pb_b[bk], stb, t1_b], [o32_b])
        P.act(junk[0:PT, statrot.i % 4, 0:128], o32[0:PT, 0, :], AF.Square, [o32_b], [stb], accum_out=st[0:PT, 3:4], saturate=False)
        rstd_from_ss(st[0:PT, 3:4], 1, stb, 1.0 / 128)
        P.act(otm[0:PT, 0, :], o32[0:PT, 0, :], AF.Copy, [o32_b, stb], [otm_b], scale=st[0:PT, 3:4])
        tb = strot.nxt()
        pv = pb[tb].bitcast(BF16).rearrange("p (k t) -> p k t", k=8)
        P.tr([(pv[:, 0, 0:PT], otm[0:PT, 0, :], PT)], ident, [otm_b, ident_b], [pb_b[tb]])
        P.cp('dve', catT[:, h, s * 16:(s + 1) * 16], pv[:, 0, 0:PT], [pb_b[tb]], [catT_b])

    def phase_B(tl, xt, xb):
        PT, NS, NT = tl.PT, tl.NS, tl.NT
        norm_to_hT(tl, xt, xb)
        for c in range(2):
            W, Wb = wget('xq', c)

            def evq(cc, o, ob, c=c):
                P.cp('act', hqT[:, c * 4 + cc, 0:NT], o, [ob], [hqT_b])
            proj_featmajor(tl, W, Wb, 0, 4, hT, hT_b, evq)
        if tl.kind == 'p':
            cross_attn(tl, 0, NT)
        else:
            mem_sample_load(0)
            for s in range(NSEQ):
                mem_sample(s)
                cross_attn(tl, s * 16, 16)
        proj_resid(tl, 'xo', xoT, xoT_b, xt, xb)

    xorot = Rot([0, 1, 2, 3])

    cxs = {'i': 0}

    def cross_attn(tl, c0, n):
        scale = 256 ** -0.5
        for h in range(4):
            ci = cxs['i']
            cxs['i'] += 1
            pts = []
            for mt in range(2):
                bk = strot.nxt()
                o = pb[bk][:, 0:n]
                P.mm([(o, MKT[:, h * 2 + dd, mt * 128:(mt + 1) * 128], hqT[:, h * 2 + dd, c0:c0 + n], dd == 0, dd == 1)
                      for dd in range(2)], [MKT_b, hqT_b], [pb_b[bk]])
                pt, ptb = ptB[(ci % 2) * 2 + mt]
                P.act(pt[:, 0:n], o, AF.Exp, [pb_b[bk]], [ptb], scale=scale)
                pts.append((pt, ptb))
            rden, rden_b = rdens[ci % 2]
            dk = 7
            dn = pb[dk][:, 0:n]
            P.mm([(dn, ones[:, :], pts[mt][0][:, 0:n], mt == 0, mt == 1) for mt in range(2)],
                 [pts[0][1], pts[1][1], ident_b], [pb_b[dk]])
            P.add('dve', lambda e, dn=dn, rden=rden: e.reciprocal(rden[:, 0:n], dn), [pb_b[dk]], [rden_b])
            for dd in range(2):
                bk = xorot.nxt()
                o = pb[bk][:, 0:n]
                P.mm([(o, MV[:, mt, h * 256 + dd * 128:h * 256 + (dd + 1) * 128], pts[mt][0][:, 0:n], mt == 0, mt == 1)
                      for mt in range(2)], [MV_b, pts[0][1], pts[1][1]], [pb_b[bk]])
                P.tt('dve', xoT[:, h * 2 + dd, c0:c0 + n], o, rden[:, 0:n], ALU.mult, [pb_b[bk], rden_b], [xoT_b])

    ffrot = Rot([0, 1, 2, 3, 4, 5, 6, 7])

    def phase_C(tl, xt, xb):
        PT, NS, NT, G, L = tl.PT, tl.NS, tl.NT, tl.G, tl.L
        prompt = tl.kind == 'p'
        norm_to_hT(tl, xt, xb)
        if not prompt:
            for s in range(NSEQ):
                dma_state_in(sphalo[:, :, s, :], sff[s], [sphalo_b], sphalo_b)
        elif tl.first:
            P.memset('pool', uphalo[:, :, :], 0.0, [uphalo_b])
        ui = 0
        for i in range(11):
            W, Wb = wget('ug', i)
            for cc in range(2):
                ch = 2 * i + cc
                bu = ffrot.nxt()
                ou = pb[bu][:, 0:NT]
                P.mm([(ou, W[:, k, cc * 128:(cc + 1) * 128], hT[:, k, 0:NT], k == 0, k == 7) for k in range(8)],
                     hT_b + [Wb], [pb_b[bu]])
                bg_ = ffrot.nxt()
                og = pb[bg_][:, 0:NT]
                P.mm([(og, W[:, k, 256 + cc * 128:256 + (cc + 1) * 128], hT[:, k, 0:NT], k == 0, k == 7) for k in range(8)],
                     hT_b + [Wb], [pb_b[bg_]])
                ub, ubb = upb[ui % 2]
                cb_, cbb = cvb[ui % 2]
                sl, slb_ = slb[ui % 2]
                ui += 1
                u3 = ub[:, 0:G * (L + 2)].rearrange("p (g l) -> p g l", g=G)
                if prompt:
                    P.cp('pool', u3[:, 0, 0:2], uphalo[:, ch, :], [uphalo_b], [ubb])
                else:
                    P.cp('pool', u3[:, :, 0:2], sphalo[:, ch, :, :], [sphalo_b], [ubb])
                P.cp('act', u3[:, :, 2:L + 2], ou.rearrange("p (g l) -> p g l", g=G), [pb_b[bu]], [ubb])
                if prompt:
                    P.cp('pool', uphalo[:, ch, :], u3[:, 0, L:L + 2], [ubb], [uphalo_b])
                else:
                    P.cp('pool', sphalo[:, ch, :, :], u3[:, :, L:L + 2], [ubb], [sphalo_b])
                c3 = cb_[:, 0:NT].rearrange("p (g l) -> p g l", g=G)
                conv3('pool', c3, u3, wfc, ch, G, L, [ubb, const_b], [cbb])
                P.act(sl[:, 0:NT], cb_[:, 0:NT], AF.Silu, [cbb], [slb_])
                P.tt('dve', gT[:, ch, 0:NT], sl[:, 0:NT], og, ALU.mult, [pb_b[bg_], slb_], [gT_b3[ch // 8]])
        if prompt:
            if tl.last:
                dma_state_out(ff_p[tl.s], uphalo[:, :, :], [uphalo_b], uphalo_b)
        else:
            for s in range(NSEQ):
                dma_state_out(ff_s[s], sphalo[:, :, s, :], [sphalo_b], sphalo_b)
        P.tag = 'C_down'
        for c in range(2):
            banks = [c * 4 + j for j in range(NS)]
            for kb in range(3):
                W, Wb = wget('dn', c * 3 + kb)
                nk = 8 if kb < 2 else 6
                for j in range(NS):
                    o = pb[banks[j]][0:PT, :]
                    P.mm([(o, gT[:, kb * 8 + k, j * PT:(j + 1) * PT], W[:, k, :], kb == 0 and k == 0, kb == 2 and k == nk - 1)
                          for k in range(nk)], [gT_b3[kb], Wb], [pb_b[banks[j]]])
            for j in range(NS):
                xsl = xt[0:PT, j, c * 512:(c + 1) * 512]
                P.tt('dve', xsl, pb[banks[j]][0:PT, :], xsl, ALU.add, [pb_b[banks[j]], xb[j]], [xb[j]])

    def final_norm(tl, xt, xb, xtile_b):
        PT, NS = tl.PT, tl.NS
        for j in range(NS):
            ss, ssb = newstat()
            P.act(junk[0:PT, j, :], xt[0:PT, j, :], AF.Square, [xb[j]], [ssb], accum_out=ss[0:PT, 0:1], saturate=False)
            rstd_from_ss(ss[0:PT, 0:1], 1, ssb, 1.0 / D)
            P.stt('dve', xt[0:PT, j, :], xt[0:PT, j, :], ss[0:PT, 0:1], gfin[0:PT, 0, :], ALU.mult, ALU.mult,
                  [xb[j], ssb, const_b], [xb[j]])
        if tl.kind == 'p':
            r0 = tl.s * SEQ + tl.t * TT
            P.dma(y_p[r0:r0 + TT, :].rearrange("(j p) d -> p j d", p=128), xt[:, :, :], [xtile_b], (), xtile_b, q='pool')
        else:
            P.dma(y_s[:, :], xt[0:64, 0, :], [xtile_b], (), xtile_b)

    def load_x(i):
        tl = tiles[i]
        xt, xb = xs_t[i % 2], xs_b[i % 2]
        if tl.kind == 'p':
            r0 = tl.s * SEQ + tl.t * TT
            P.dma(xt[:, :, :], xp[r0:r0 + TT, :].rearrange("(j p) d -> p j d", p=128), (), [xb], xb)
        else:
            P.dma(xt[0:64, 0, :], xsm[:, :], (), [xb], xb)

    load_x(0)
    P.tag = 'A_inproj'
    norm_to_hT(tiles[0], xs_t[0], xsub_b[0])
    for i, tl in enumerate(tiles):
        xt, xb, xtb = xs_t[i % 2], xsub_b[i % 2], xs_b[i % 2]
        if i + 1 < len(tiles):
            load_x(i + 1)
        if tl.kind == 'p' and tl.first:
            P.tag = 'mem'
            mem_phase(tl)
        P.tag = 'A_inproj'
        phase_A(tl, xt, xb)
        P.tag = 'B'
        phase_B(tl, xt, xb)
        P.tag = 'C_upgate'
        phase_C(tl, xt, xb)
        if i + 1 < len(tiles):
            P.tag = 'A_inproj'
            norm_to_hT(tiles[i + 1], xs_t[(i + 1) % 2], xsub_b[(i + 1) % 2])
        final_norm(tl, xt, xb, xtb)
    assert wstate['ptr'] == len(wseq)
    P.sbuf_left = nc.sbuf_bytes_remaining
    P.build(es)
    es.close()
    return nc, P


def _consts():
    half = 8
    inv = (1.0 / (500000.0 ** (np.arange(half, dtype=np.float32) * np.float32(2.0) / np.float32(16)))).astype(np.float32)
    pos = np.zeros((17, 128), np.float32)
    for j in range(16):
        pos[j] = j * 128 + np.arange(128)
    pos[16, :16] = PAST + np.arange(16)
    pos[16, :64] = PAST + (np.arange(64) % 16)
    ang = (pos[:, :, None].astype(np.float32) * inv[None, None, :]).astype(np.float32)
    cos = np.cos(ang).astype(np.float32).transpose(1, 0, 2).copy()
    sin = np.sin(ang).astype(np.float32).transpose(1, 0, 2).copy()
    ident = np.eye(128, dtype=np.float32)
    cm = np.zeros((128, 8), np.float32)
    cm[64:, 0] = NEG
    for s in range(4):
        cm[:, 1 + s] = NEG
        cm[s * 16:(s + 1) * 16, 1 + s] = 0.0
    return cos, sin, ident, cm


_CACHE = {}


def kernel(**inputs):
    f = lambda a: np.ascontiguousarray(np.asarray(a, dtype=np.float32))
    if 'nc' not in _CACHE:
        _CACHE['nc'] = build_program()[0]
    nc = _CACHE['nc']
    cos, sin, ident, cm = _consts()
    shared = {
        "g_mix": f(inputs["g_mix"]), "g_mem": f(inputs["g_mem"]), "g_x": f(inputs["g_x"]),
        "g_ffn": f(inputs["g_ffn"]), "g_final": f(inputs["g_final"]).reshape(1, D), "g_sub": f(inputs["g_sub"]),
        "lam_q1": f(inputs["lam_q1"]), "lam_k1": f(inputs["lam_k1"]), "lam_q2": f(inputs["lam_q2"]), "lam_k2": f(inputs["lam_k2"]),
        "w_in": f(inputs["w_in"])[0], "w_out": f(inputs["w_out"])[0], "w_xq": f(inputs["w_xq"])[0],
        "w_xk": f(inputs["w_xk"])[0], "w_xv": f(inputs["w_xv"])[0], "w_xo": f(inputs["w_xo"])[0],
        "w_up": f(inputs["w_up"])[0], "w_gate": f(inputs["w_gate"])[0], "w_down": f(inputs["w_down"])[0],
        "w_sc": f(inputs["w_sc"])[0], "w_ffconv": f(inputs["w_ffconv"])[0],
        "rope_cos": cos, "rope_sin": sin, "ident": ident, "cmask": cm,
    }
    xpr = f(inputs["x_prompt"]); xsa = f(inputs["x_sample"])
    cak = f(inputs["cache_attn_k"])[0]; cav = f(inputs["cache_attn_v"])[0]
    sscv = f(inputs["state_short_conv"])[0]; sffv = f(inputs["state_ffn_conv"])[0]
    cmkv = f(inputs["cache_mem_k"])[0]; cmvv = f(inputs["cache_mem_v"])[0]; mp = f(inputs["mem_prompt"])
    in_maps = []
    for c in range(NCORES):
        sl = slice(c * NSEQ, (c + 1) * NSEQ)
        m = dict(shared)
        m["xp"] = xpr[sl].reshape(NSEQ * SEQ, D)
        m["xsm"] = xsa[sl].reshape(NSEQ * DEC, D)
        m["ck"] = cak[sl].reshape(NSEQ, PAST, 512)
        m["cv"] = cav[sl].reshape(NSEQ, PAST, 512)
        m["ssc"] = sscv[sl]
        m["sff"] = sffv[sl]
        m["cmk"] = cmkv[sl].reshape(NSEQ, NMEM, D)
        m["cmv"] = cmvv[sl].reshape(NSEQ, NMEM, D)
        m["memp"] = mp[sl]
        in_maps.append(m)
    res = run_bass_kernel_spmd(nc, in_maps, core_ids=list(range(NCORES)))
    R = res.results

    def cat(name, shape):
        return np.concatenate([np.asarray(r[name], dtype=np.float32) for r in R], axis=0).reshape(shape)

    B = NCORES * NSEQ
    return (
        cat("y_p", (B, SEQ, D)),
        cat("y_s", (B, DEC, D)),
        cat("k_p", (B, SEQ, 4, 2, 64))[None],
        cat("v_p", (B, SEQ, 4, 128))[None],
        cat("sc_p", (B, 2, 512))[None],
        cat("ff_p", (B, 2, DFF))[None],
        cat("mk_p", (B, NMEM, 4, 256))[None],
        cat("mv_p", (B, NMEM, 4, 256))[None],
        cat("k_s", (B, DEC, 4, 2, 64))[None],
        cat("v_s", (B, DEC, 4, 128))[None],
        cat("sc_s", (B, 2, 512))[None],
        cat("ff_s", (B, 2, DFF))[None],
    )
```

```python
import math
from contextlib import ExitStack
import numpy as np
import concourse.bass as bass
import concourse.mybir as mybir
from concourse.bass_utils import run_bass_kernel_spmd

F32 = mybir.dt.float32
BF16 = mybir.dt.bfloat16
AF = mybir.ActivationFunctionType
ALU = mybir.AluOpType

NCORES = 8
D = 1024
SEQ = 2048
NSEQ = 4
TT = 512
DEC = 16
PAST = 1024
DFF = 2816
NFC = 22
NMEM = 256
EPS = 1e-6
LAM_INIT = 0.8 - 0.6 * math.exp(-0.3 * 0)
NEG = -30000.0
WINDOW = 2
ENGS = ('pe', 'act', 'dve', 'pool', 'sp')


class Buf:
    __slots__ = ('name', 'lw', 'rd', 'al', 'excl')

    def __init__(self, name, excl=False):
        self.name = name
        self.excl = excl
        self.lw = None
        self.rd = {}
        self.al = [self]


class Op:
    __slots__ = ('eng', 'fn', 'key', 'eidx', 'gidx', 'deps', 'sig', 'cnt', 'dcount')


class Prog:
    def __init__(self, nc):
        self.nc = nc
        self.ops = {e: [] for e in ENGS}
        self.all = []
        self.tag = ''
        self.petags = []

    def add(self, eng, fn, rd=(), wr=(), key=None):
        op = Op()
        op.eng = eng
        op.fn = fn
        op.key = key
        op.eidx = len(self.ops[eng])
        op.gidx = len(self.all)
        op.sig = False
        op.cnt = 0
        op.dcount = 0
        deps = {}
        xw = [b for b in rd if b.excl]
        if xw:
            rd = [b for b in rd if not b.excl]
            wr = list(wr) + [b for b in xw if b not in wr]

        def dep(d):
            if d is None:
                return
            k = ('d', id(d.key), d.eng) if d.key is not None else d.eng
            o = deps.get(k)
            if o is None or o.gidx < d.gidx:
                deps[k] = d

        for b in rd:
            for a in b.al:
                dep(a.lw)
        for b in wr:
            for a in b.al:
                dep(a.lw)
                for r in a.rd.values():
                    dep(r)
        op.deps = list(deps.values())
        mk = ('d', id(key), eng) if key is not None else eng
        for b in rd:
            b.rd[mk] = op
        for b in wr:
            b.lw = op
            b.rd = {}
        self.ops[eng].append(op)
        self.all.append(op)
        return op

    def dma(self, out, in_, rd, wr, key, q='sp'):
        self.add(q, lambda e: e.dma_start(out=out, in_=in_), rd, wr, key=key)

    def mm(self, items, rd, wr):
        tg = self.tag

        def fn(e):
            ins = None
            for it in items:
                self.petags.append(tg)
                o, l, r, st, sp = it[:5]
                if len(it) > 5:
                    ins = e.matmul(o, l, r, start=st, stop=sp, skip_group_check=True)
                else:
                    ins = e.matmul(o, l, r, start=st, stop=sp)
            return ins
        self.add('pe', fn, rd, wr)

    def tr(self, items, ident, rd, wr):
        tg = self.tag

        def fn(e):
            ins = None
            for (o, i, n) in items:
                self.petags.append(tg)
                ins = e.transpose(o, i, ident[0:n, 0:n])
            return ins
        self.add('pe', fn, rd, wr)

    def act(self, out, in_, func, rd, wr, **kw):
        self.add('act', lambda e: e.activation(out, in_, func, **kw), rd, wr)

    def cp(self, eng, out, in_, rd, wr):
        if eng == 'act':
            self.add('act', lambda e: e.copy(out, in_), rd, wr)
        else:
            self.add(eng, lambda e: e.tensor_copy(out, in_), rd, wr)

    def tt(self, eng, out, a, b, op, rd, wr):
        self.add(eng, lambda e: e.tensor_tensor(out, a, b, op), rd, wr)

    def ts(self, eng, out, a, s1, s2, op0, op1, rd, wr):
        if s2 is None:
            self.add(eng, lambda e: e.tensor_scalar(out, a, s1, None, op0), rd, wr)
        else:
            self.add(eng, lambda e: e.tensor_scalar(out, a, s1, s2, op0, op1), rd, wr)

    def stt(self, eng, out, a, s, b, op0, op1, rd, wr):
        self.add(eng, lambda e: e.scalar_tensor_tensor(out, a, s, b, op0, op1), rd, wr)

    def memset(self, eng, ap, val, wr):
        self.add(eng, lambda e: e.memset(ap, val), (), wr)

    def build(self, es):
        nc = self.nc
        import os
        mx = int(os.environ.get("K_MAXOPS", "0"))
        if mx:
            self.all = [o for o in self.all if o.gidx < mx]
            for e in ENGS:
                self.ops[e] = [o for o in self.ops[e] if o.gidx < mx]
        for op in self.all:
            for d in op.deps:
                if d.key is not None:
                    continue
                if d.eng != op.eng or op.key is not None:
                    d.sig = True
                elif op.eng != 'pe' and d.eidx >= op.eidx - WINDOW:
                    d.sig = True
        esem = {}
        for e in ENGS:
            esem[e] = es.enter_context(nc.semaphore("es_" + e))
            c = 0
            for op in self.ops[e]:
                if op.key is None and op.sig:
                    c += 1
                op.cnt = c
        dsem = {}
        dcnt = {}
        for op in self.all:
            if op.key is not None:
                k = (id(op.key), op.eng)
                if k not in dsem:
                    dsem[k] = es.enter_context(nc.semaphore("ds_%d" % len(dsem)))
                    dcnt[k] = 0
                dcnt[k] += 16
                op.dcount = dcnt[k]
        self.nsem = len(dsem) + len(esem)

        def emit(ename, eng):
            waited = {}
            for op in self.ops[ename]:
                need = {}
                for d in op.deps:
                    if d.key is not None:
                        sem, v = dsem[(id(d.key), d.eng)], d.dcount
                    elif d.eng != ename or op.key is not None:
                        sem, v = esem[d.eng], d.cnt
                    elif ename != 'pe' and d.eidx >= op.eidx - WINDOW:
                        sem, v = esem[ename], d.cnt
                    else:
                        continue
                    if need.get(sem.num, (None, 0))[1] < v:
                        need[sem.num] = (sem, v)
                for num, (sem, v) in need.items():
                    if waited.get(num, 0) < v:
                        eng.wait_ge(sem, v)
                        waited[num] = v
                ins = op.fn(eng)
                if op.key is not None:
                    ins.then_inc(dsem[(id(op.key), op.eng)], 16)
                elif op.sig:
                    ins.then_inc(esem[ename], 1)
            if ename == 'sp':
                for k, sem in dsem.items():
                    if waited.get(sem.num, 0) < dcnt[k]:
                        eng.wait_ge(sem, dcnt[k])

        with nc.Block() as block:
            @block.tensor
            def _(e):
                emit('pe', e)

            @block.scalar
            def _(e):
                emit('act', e)

            @block.vector
            def _(e):
                emit('dve', e)

            @block.gpsimd
            def _(e):
                emit('pool', e)

            @block.sync
            def _(e):
                emit('sp', e)


def build_program(phases=99):
    nc = bass.Bass("TRN2", target_bir_lowering=False)
    es = ExitStack()
    es.enter_context(nc.allow_low_precision("bf16 matmul operands, fp32 accumulation"))
    es.enter_context(nc.allow_non_contiguous_dma("small constant / state layouts"))
    P = Prog(nc)

    def din(name, shape):
        return nc.dram_tensor(name, shape, F32, kind="ExternalInput").ap()

    def dout(name, shape):
        return nc.dram_tensor(name, shape, F32, kind="ExternalOutput").ap()

    xp = din("xp", [NSEQ * SEQ, D])
    xsm = din("xsm", [NSEQ * DEC, D])
    ck = din("ck", [NSEQ, PAST, 512])
    cv = din("cv", [NSEQ, PAST, 512])
    ssc = din("ssc", [NSEQ, 2, 512])
    sff = din("sff", [NSEQ, 2, DFF])
    cmk = din("cmk", [NSEQ, NMEM, D])
    cmv = din("cmv", [NSEQ, NMEM, D])
    memp = din("memp", [NSEQ, NMEM, D])
    g_mix = din("g_mix", [1, D]); g_mem = din("g_mem", [1, D]); g_x = din("g_x", [1, D])
    g_ffn = din("g_ffn", [1, D]); g_final = din("g_final", [1, D]); g_sub = din("g_sub", [1, 128])
    lq1 = din("lam_q1", [1, 64]); lk1 = din("lam_k1", [1, 64])
    lq2 = din("lam_q2", [1, 64]); lk2 = din("lam_k2", [1, 64])
    w_in = din("w_in", [D, 3072]); w_out = din("w_out", [D, D])
    w_xq = din("w_xq", [D, D]); w_xk = din("w_xk", [D, D]); w_xv = din("w_xv", [D, D]); w_xo = din("w_xo", [D, D])
    w_up = din("w_up", [D, DFF]); w_gate = din("w_gate", [D, DFF]); w_down = din("w_down", [DFF, D])
    w_sc = din("w_sc", [3, 512]); w_fc = din("w_ffconv", [3, DFF])
    cos_d = din("rope_cos", [128, 17, 8]); sin_d = din("rope_sin", [128, 17, 8])
    ident_d = din("ident", [128, 128]); cmask_d = din("cmask", [128, 8])

    y_p = dout("y_p", [NSEQ * SEQ, D]); y_s = dout("y_s", [NSEQ * DEC, D])
    k_p = dout("k_p", [NSEQ * SEQ, 512]); v_p = dout("v_p", [NSEQ * SEQ, 512])
    sc_p = dout("sc_p", [NSEQ, 2, 512]); ff_p = dout("ff_p", [NSEQ, 2, DFF])
    mk_p = dout("mk_p", [NSEQ * NMEM, D]); mv_p = dout("mv_p", [NSEQ * NMEM, D])
    k_s = dout("k_s", [NSEQ * DEC, 512]); v_s = dout("v_s", [NSEQ * DEC, 512])
    sc_s = dout("sc_s", [NSEQ, 2, 512]); ff_s = dout("ff_s", [NSEQ, 2, DFF])

    NBLK = {'in': 6, 'out': 2, 'xq': 2, 'xk': 2, 'xv': 2, 'xo': 2, 'ug': 11, 'dn': 6}
    scr = {}
    scrb = {}
    for nm, n in NBLK.items():
        scr[nm] = nc.dram_tensor("scr_" + nm, [n, 128, 8 * 512], BF16, kind="ExternalOutput").ap()
        scrb[nm] = Buf("scr_" + nm)

    def sb(name, shape, dt=F32):
        return es.enter_context(nc.sbuf_tensor("sb_" + name, shape, dt))

    xs_t = [sb("xs%d" % i, [128, 4, D]) for i in range(2)]
    xs_b = [Buf("xs%d" % i) for i in range(2)]
    xsub_b = [[Buf("xs%d_%d" % (i, j)) for j in range(4)] for i in range(2)]
    for i in range(2):
        xs_b[i].al = [xs_b[i]] + xsub_b[i]
        for j in range(4):
            xsub_b[i][j].al = [xsub_b[i][j], xs_b[i]]
    hT = sb("hT", [128, 8, TT], BF16); hT_bj = [Buf("hT%d" % j) for j in range(4)]; hT_b = hT_bj
    KT = sb("KT", [128, 4, SEQ], BF16); KT_b = [Buf("KT%d" % i) for i in range(4)]
    V1 = sb("V1", [128, 16, 4, 130], BF16); V1_b = [Buf("V1%d" % i) for i in range(4)]
    ring_t = [sb("ring%d" % i, [128, 8, 512], BF16) for i in range(4)]
    ring_b = [Buf("ring%d" % i) for i in range(4)]
    cst_t = [sb("cst%d" % i, [128, 512]) for i in range(4)]
    cst_b = [Buf("cst%d" % i) for i in range(4)]
    MKT = sb("MKT", [128, 8, NMEM], BF16); MKT_b = Buf("MKT")
    MV = sb("MV", [128, 2, D], BF16); MV_b = Buf("MV")
    ident32 = sb("ident32", [128, 128]); ident = sb("ident", [128, 128], BF16); ident_b = Buf("ident")
    ones = sb("ones", [128, 128], BF16)
    cosT = sb("cosT", [128, 17, 8]); sinT = sb("sinT", [128, 17, 8]); cmask = sb("cmask", [128, 8])
    const_b = Buf("const")
    gcol = sb("gcol", [128, 5, 8])
    gsub1 = sb("gsub1", [128, 1])
    epsc = sb("epsc", [128, 1])
    gfin = sb("gfin", [128, 1, D])
    lamv = sb("lamv", [128, 4, 64]); lamt = sb("lamt", [128, 2, 64]); lams = sb("lams", [128, 4])
    wsc = sb("wsc", [128, 4, 3]); wfc = sb("wfc", [128, NFC, 3])
    uhalo = sb("uhalo", [128, 4, 2]); uhalo_b = Buf("uhalo")
    uphalo = sb("uphalo", [128, NFC, 2]); uphalo_b = Buf("uphalo")
    sphalo = sb("sphalo", [128, NFC, 4, 2]); sphalo_b = Buf("sphalo")
    stat = sb("stat", [128, 128]); stat_b = [Buf("stat%d" % i) for i in range(8)]
    junk = sb("junk", [128, 4, D], mybir.dt.float8e4); junk_b = Buf("junk")
    hb_t = [sb("hb%d" % i, [128, D], BF16) for i in range(2)]
    hb_b = [Buf("hb%d" % i) for i in range(2)]

    OVL = 69 * 1024
    ovl = sb("ovl", [128, OVL // 2], BF16)
    ovl_bufs = []

    def ov(name, off, nbytes, dt, pattern=None, **kw):
        a = ovl[:, off // 2:(off + nbytes) // 2]
        if dt == F32:
            a = a.bitcast(F32)
        if pattern:
            a = a.rearrange(pattern, **kw)
        b = Buf(name)
        ovl_bufs.append((b, off, off + nbytes))
        return a, b

    K = 1024
    qT, qT_b = ov("qT", 0, 4 * K, BF16, "p (h t) -> p h t", h=4)
    catT, catT_b = ov("catT", 4 * K, 8 * K, BF16, "p (k t) -> p k t", k=8)
    cg32, cg32_b = ov("cg32", 12 * K, 8 * K, F32, "p (c t) -> p c t", c=4)
    ubuf, ubuf_b = ov("ubuf", 20 * K, 8 * K + 64, F32)
    kst, kst_b = ov("kst", 29 * K, 8 * K, F32, "p (j c) -> p j c", j=4)
    vst, vst_b = ov("vst", 37 * K, 8 * K, F32, "p (j c) -> p j c", j=4)
    q32x, q32x_b = ov("q32x", 49 * K, 8 * K, F32, "p (j c) -> p j c", j=4)
    qb16x, qb16x_b = ov("qb16x", 57 * K, 4 * K, BF16, "p (j c) -> p j c", j=4)
    kb16x, kb16x_b = ov("kb16x", 61 * K, 4 * K, BF16, "p (j c) -> p j c", j=4)
    rtmp, rtmp_b = ov("rtmp", 65 * K, 4 * K, F32, "p (a j b c) -> p a j b c", a=4, j=4, b=8)
    ptA = []
    for i in range(8):
        ptA.append(ov("pt%d" % i, 49 * K + i * K, K, BF16))
    ptP = []
    for i in range(4):
        ptP.append(ov("ptp%d" % i, 49 * K + 2 * i * K, 2 * K, BF16, "p (m q) -> p m q", m=2))
    t1, t1_b = ov("t1", 57 * K, 2 * K, F32, "p (q e) -> p q e", q=4)
    o32, o32_b = ov("o32", 59 * K, 2 * K, F32, "p (q e) -> p q e", q=4)
    otm, otm_b = ov("otm", 61 * K, K, BF16, "p (q e) -> p q e", q=4)
    ckst, ckst_b = ov("ckst", 29 * K, 16 * K, F32, "p (j c) -> p j c", j=8)
    ckb, ckb_b = ov("ckb", 65 * K, 4 * K, BF16, "p (j c) -> p j c", j=4)
    ckst2, ckst2_b = ov("ckst2", 12 * K, 16 * K, F32, "p (j c) -> p j c", j=8)
    hqT, hqT_b = ov("hqT", 0, 8 * K, BF16, "p (k t) -> p k t", k=8)
    xoT, xoT_b = ov("xoT", 8 * K, 8 * K, BF16, "p (k t) -> p k t", k=8)
    ptB = []
    for i in range(4):
        ptB.append(ov("ptB%d" % i, 16 * K + i * K, K, BF16))
    rdens = []
    for i in range(2):
        rdens.append(ov("rden%d" % i, 20 * K + i * 2 * K, 2 * K, F32))
    mld, mld_b = ov("mld", 24 * K, 8 * K, F32, "p (j c) -> p j c", j=2)
    mst, mst_b = ov("mst", 32 * K, 8 * K, F32, "p (j c) -> p j c", j=2)
    mb16, mb16_b = ov("mb16", 40 * K, 4 * K, BF16, "p (j c) -> p j c", j=2)
    mT, mT_b = ov("mT", 44 * K, 4 * K, BF16, "p (k t) -> p k t", k=8)
    mkb, mkb_b = ov("mkb", 48 * K, 2 * K, BF16)
    gT = ovl[:, 0:11 * K].rearrange("p (k t) -> p k t", k=NFC)
    gT_b3 = [ov("gT%d" % i, i * 8 * K, (8 if i < 2 else 6) * K, BF16)[1] for i in range(3)]
    upb = []
    for i in range(2):
        upb.append(ov("upb%d" % i, 22 * K + i * (2 * K + 64), 2 * K + 64, F32))
    cvb = []
    for i in range(2):
        cvb.append(ov("cvb%d" % i, 27 * K + i * 2 * K, 2 * K, F32))
    slb = []
    for i in range(2):
        slb.append(ov("slb%d" % i, 31 * K + i * 2 * K, 2 * K, F32))
    for (b, lo, hi) in ovl_bufs:
        b.al = [b2 for (b2, lo2, hi2) in ovl_bufs if lo2 < hi and lo < hi2]

    pacc = es.enter_context(nc.psum_tensor("pacc", [128, 2048], F32))
    pb = [pacc[:, i * 512:(i + 1) * 512] for i in range(4)]
    pst = es.enter_context(nc.psum_tensor("pst", [128, 2048], F32))
    pb += [pst[:, i * 512:(i + 1) * 512] for i in range(4)]
    pb_b = [Buf("pb%d" % i, excl=True) for i in range(8)]

    class Rot:
        def __init__(self, ids):
            self.ids = ids
            self.i = 0

        def nxt(self):
            r = self.ids[self.i % len(self.ids)]
            self.i += 1
            return r

    cb = [const_b]
    P.dma(ident32[:, :], ident_d[:, :], (), cb, const_b)
    P.dma(cosT[:, :, :], cos_d[:, :, :], (), cb, const_b)
    P.dma(sinT[:, :, :], sin_d[:, :, :], (), cb, const_b)
    P.dma(cmask[:, :], cmask_d[:, :], (), cb, const_b)
    for i, g in enumerate((g_mix, g_mem, g_x, g_ffn)):
        P.dma(gcol[:, i, :], g[0, :].rearrange("(k p) -> p k", p=128), (), cb, const_b)
    P.dma(gsub1[:, :], g_sub[0, :].rearrange("(p o) -> p o", o=1), (), cb, const_b)
    P.dma(gfin[:, :, :], g_final[0:1, :].partition_broadcast(128), (), cb, const_b)
    for i, l in enumerate((lq1, lk1, lq2, lk2)):
        P.dma(lamv[:, i:i + 1, :], l[0:1, :].partition_broadcast(128), (), cb, const_b)
    for j_ in range(3):
        P.dma(wsc[:, :, j_], w_sc[j_, :].rearrange("(c p) -> p c", p=128), (), cb, const_b)
        P.dma(wfc[:, :, j_], w_fc[j_, :].rearrange("(c p) -> p c", p=128), (), cb, const_b)

    def dma_state_out(dram2, sb3, rd, key):
        for r_ in range(2):
            P.dma(dram2[r_, :].rearrange("(c p) -> p c", p=128), sb3[:, :, r_], rd, (), key)

    def dma_state_in(sb3, dram2, wr, key):
        for r_ in range(2):
            P.dma(sb3[:, :, r_], dram2[r_, :].rearrange("(c p) -> p c", p=128), (), wr, key)
    P.cp('dve', ident[:, :], ident32[:, :], cb, [ident_b])
    P.memset('pool', ones[:, :], 1.0, [ident_b])
    P.memset('pool', epsc[:, :], EPS, cb)
    P.memset('pool', V1[:, :, :, :], 1.0, V1_b)
    P.memset('dve', gcol[:, 4, :], 1.0, cb)
    P.ts('dve', gcol[:, 4, 0:4], gcol[:, 4, 0:4], gsub1[:, 0:1], 1.0 - LAM_INIT, ALU.mult, ALU.mult, cb, cb)
    P.tt('dve', lamt[:, 0, :], lamv[:, 0, :], lamv[:, 1, :], ALU.mult, cb, cb)
    P.tt('dve', lamt[:, 1, :], lamv[:, 2, :], lamv[:, 3, :], ALU.mult, cb, cb)
    P.add('dve', lambda e: e.reduce_sum(lams[:, 0:2], lamt[:, :, :], mybir.AxisListType.X), cb, cb)
    P.act(lams[:, 0:2], lams[:, 0:2], AF.Exp, cb, cb)
    P.tt('dve', lams[:, 2:3], lams[:, 0:1], lams[:, 1:2], ALU.subtract, cb, cb)
    P.ts('dve', lams[:, 3:4], lams[:, 2:3], LAM_INIT, -1.0, ALU.add, ALU.mult, cb, cb)
    neg_lam = lams[:, 3:4]

    converted = set()
    wstate = {'slot': 0, 'cst': 0, 'ptr': 0, 'issued': 0}
    wseq = []
    gidx = {'in': 0, 'xk': 1, 'xv': 1, 'xq': 2, 'ug': 3, 'out': 4}

    def wsrc(nm, b, k):
        r0 = k * 128
        if nm == 'ug':
            return [(0, 256, w_up[r0:r0 + 128, b * 256:(b + 1) * 256]),
                    (256, 256, w_gate[r0:r0 + 128, b * 256:(b + 1) * 256])]
        if nm == 'dn':
            c, kb = b // 3, b % 3
            r0 = kb * 1024 + k * 128
            return [(0, 512, w_down[r0:r0 + 128, c * 512:(c + 1) * 512])]
        w = {'in': w_in, 'out': w_out, 'xq': w_xq, 'xk': w_xk, 'xv': w_xv, 'xo': w_xo}[nm]
        return [(0, 512, w[r0:r0 + 128, b * 512:(b + 1) * 512])]

    def wissue():
        i = wstate['issued']
        if i >= len(wseq):
            return
        nm, b = wseq[i]
        wstate['issued'] += 1
        s = i % 4
        slot, sbuf_ = ring_t[s], ring_b[s]
        if (nm, b) in converted:
            P.dma(slot[:, :, :], scr[nm][b].rearrange("p (k c) -> p k c", k=8), [scrb[nm]], [sbuf_], sbuf_)
        else:
            converted.add((nm, b))
            nk = 6 if (nm == 'dn' and b % 3 == 2) else 8
            for k in range(nk):
                ci = (k % 2) + 2 * ((wstate['cst'] // 2) % 2)
                wstate['cst'] += 1
                qn = 'act' if k % 2 else 'sp'
                for (c0, ncol, src) in wsrc(nm, b, k):
                    P.dma(cst_t[ci][:, c0:c0 + ncol], src, (), [cst_b[ci]], cst_b[ci], q=qn)
                dst = slot[:, k, :]
                if nm in gidx:
                    gs = gcol[:, gidx[nm], k:k + 1]
                    if k % 2:
                        P.act(dst, cst_t[ci][:, :], AF.Copy, [cst_b[ci], const_b], [sbuf_], scale=gs)
                    else:
                        P.ts('dve', dst, cst_t[ci][:, :], gs, None, ALU.mult, None, [cst_b[ci], const_b], [sbuf_])
                else:
                    P.cp('act' if k % 2 else 'dve', dst, cst_t[ci][:, :], [cst_b[ci]], [sbuf_])
            P.dma(scr[nm][b].rearrange("p (k c) -> p k c", k=8), slot[:, :, :], [sbuf_], [scrb[nm]], sbuf_, q='pool')

    def wget(nm, b, held=0):
        i = wstate['ptr']
        assert wseq[i] == (nm, b), (wseq[i], nm, b)
        wstate['ptr'] += 1
        while wstate['issued'] < min(len(wseq), i + 4 - held):
            wissue()
        s = i % 4
        return ring_t[s], ring_b[s]

    class Tile:
        pass

    tiles = []
    for s in range(NSEQ):
        for t in range(SEQ // TT):
            tl = Tile()
            tl.kind = 'p'; tl.s = s; tl.t = t
            tl.NS = 4; tl.PT = 128; tl.NT = 512; tl.G = 1; tl.L = 512
            tl.first = (t == 0); tl.last = (t == SEQ // TT - 1)
            tiles.append(tl)
    tl = Tile()
    tl.kind = 's'; tl.s = 0; tl.t = 0
    tl.NS = 1; tl.PT = 64; tl.NT = 64; tl.G = 4; tl.L = 16
    tl.first = True; tl.last = True
    if phases < 50:
        tiles = tiles[:phases]
    tiles = [tl] + tiles

    for tl in tiles:
        if tl.kind == 'p' and tl.first:
            wseq += [('xk', 0), ('xk', 1), ('xv', 0), ('xv', 1)]
        wseq += [('in', 0), ('in', 1), ('in', 2), ('in', 4), ('in', 5), ('in', 3)]
        wseq += [('out', 0), ('out', 1), ('xq', 0), ('xq', 1), ('xo', 0), ('xo', 1)]
        wseq += [('ug', i) for i in range(11)]
        wseq += [('dn', i) for i in range(6)]

    mmrot = Rot([0, 1, 2, 3, 6, 7])
    tprot = Rot([4, 5])
    strot = Rot([4, 5, 6, 7])
    hbrot = Rot([0, 1])
    statrot = Rot(list(range(8)))

    def newstat():
        i = statrot.nxt()
        return stat[:, i * 16:(i + 1) * 16], stat_b[i]

    def rstd_from_ss(ssap, n, sbuf_, inv_n):
        P.act(ssap, ssap, AF.Ln, [sbuf_, const_b], [sbuf_], bias=epsc[0:ssap.shape[0], 0:1], scale=inv_n)
        P.act(ssap, ssap, AF.Exp, [sbuf_], [sbuf_], scale=-0.5)

    def sq_group(xt, PT, NS, ss, xb, ssb):
        def fn(e):
            ins = None
            for j in range(NS):
                ins = e.activation(junk[0:PT, j, :], xt[0:PT, j, :], AF.Square, accum_out=ss[0:PT, j:j + 1], saturate=False)
            return ins
        P.add('act', fn, [xb], [ssb])

    def norm_to_hT(tl, xt, xb):
        PT, NS = tl.PT, tl.NS
        for j in range(NS):
            ss, ssb = newstat()
            P.act(junk[0:PT, j, :], xt[0:PT, j, :], AF.Square, [xb[j]], [ssb], accum_out=ss[0:PT, 0:1], saturate=False)
            rstd_from_ss(ss[0:PT, 0:1], 1, ssb, 1.0 / D)
            hi = hbrot.nxt()
            P.act(hb_t[hi][0:PT, :], xt[0:PT, j, :], AF.Copy, [xb[j], ssb], [hb_b[hi]], scale=ss[0:PT, 0:1])
            bk = tprot.nxt()
            pv = pb[bk].bitcast(BF16).rearrange("p (k t) -> p k t", k=8)
            P.tr([(pv[:, k, 0:PT], hb_t[hi][0:PT, k * 128:(k + 1) * 128], PT) for k in range(8)],
                 ident, [hb_b[hi], ident_b], [pb_b[bk]])
            P.cp('dve', hT[:, :, j * PT:(j + 1) * PT], pv[:, :, 0:PT], [pb_b[bk]], [hT_bj[j]])

    def proj_resid(tl, nm, src, srcb, xt, xb):
        Ws = [wget(nm, 0), wget(nm, 1, held=1)]
        for j in range(tl.NS):
            for c in range(2):
                W, Wb = Ws[c]
                bk = mmrot.nxt()
                o = pb[bk][0:tl.PT, :]
                P.mm([(o, src[:, k, j * tl.PT:(j + 1) * tl.PT], W[:, k, :], k == 0, k == 7) for k in range(8)],
                     [srcb, Wb], [pb_b[bk]])
                xsl = xt[0:tl.PT, j, c * 512:(c + 1) * 512]
                P.tt('dve', xsl, o, xsl, ALU.add, [pb_b[bk], xb[j]], [xb[j]])

    def proj_tokmajor(tl, W, Wb, src, srcb, evac):
        for j in range(tl.NS):
            bk = mmrot.nxt()
            o = pb[bk][0:tl.PT, :]
            sbj = [srcb[j]] if isinstance(srcb, list) else [srcb]
            P.mm([(o, src[:, k, j * tl.PT:(j + 1) * tl.PT], W[:, k, :], k == 0, k == 7) for k in range(8)],
                 sbj + [Wb], [pb_b[bk]])
            evac(j, o, pb_b[bk])

    def proj_featmajor(tl, W, Wb, c0, nchunk, src, srcb, evac, rot=None):
        rot = rot or mmrot
        for c in range(nchunk):
            bk = rot.nxt()
            o = pb[bk][:, 0:tl.NT]
            sbl = list(srcb) if isinstance(srcb, list) else [srcb]
            P.mm([(o, W[:, k, c0 + c * 128:c0 + (c + 1) * 128], src[:, k, 0:tl.NT], k == 0, k == 7) for k in range(8)],
                 sbl + [Wb], [pb_b[bk]])
            evac(c, o, pb_b[bk])

    def rope(tl, buf4, bufb, jj0):
        PT, NS = tl.PT, tl.NS
        x1 = buf4[0:PT, 0:NS, :, 0:8]
        x2 = buf4[0:PT, 0:NS, :, 8:16]
        c = cosT[0:PT, jj0:jj0 + NS, :].unsqueeze(2).to_broadcast([PT, NS, 8, 8])
        s = sinT[0:PT, jj0:jj0 + NS, :].unsqueeze(2).to_broadcast([PT, NS, 8, 8])
        ta, tb_, tc, td = (rtmp[0:PT, i, 0:NS, :, :] for i in range(4))
        rw = [bufb, rtmp_b, const_b]
        P.tt('dve', ta, x1, c, ALU.mult, rw, [rtmp_b])
        P.tt('dve', tb_, x2, s, ALU.mult, rw, [rtmp_b])
        P.tt('dve', tc, x2, c, ALU.mult, rw, [rtmp_b])
        P.tt('dve', td, x1, s, ALU.mult, rw, [rtmp_b])
        P.tt('dve', x1, ta, tb_, ALU.subtract, [rtmp_b], [bufb])
        P.tt('dve', x2, tc, td, ALU.add, [rtmp_b], [bufb])

    def conv3(eng, out, u3, wv, c, G, L, rd, wr):
        P.ts('dve', out, u3[:, :, 0:L], wv[:, c, 0:1], None, ALU.mult, None, rd, wr)
        P.stt('dve', out, u3[:, :, 1:L + 1], wv[:, c, 1:2], out, ALU.mult, ALU.add, rd + wr, wr)
        P.stt('dve', out, u3[:, :, 2:L + 2], wv[:, c, 2:3], out, ALU.mult, ALU.add, rd + wr, wr)

    def mem_phase(tl):
        s = tl.s
        P.dma(mld[:, :, :], memp[s].rearrange("(j p) d -> p j d", p=128), (), [mld_b], mld_b)
        ss, ssb = newstat()
        sq_group(mld, 128, 2, ss, mld_b, ssb)
        rstd_from_ss(ss[:, 0:2], 2, ssb, 1.0 / D)
        for j in range(2):
            P.act(mb16[:, j, :], mld[:, j, :], AF.Copy, [mld_b, ssb], [mb16_b], scale=ss[:, j:j + 1])
            bk = tprot.nxt()
            pv = pb[bk].bitcast(BF16).rearrange("p (k t) -> p k t", k=8)
            P.tr([(pv[:, k, :], mb16[:, j, k * 128:(k + 1) * 128], 128) for k in range(8)],
                 ident, [mb16_b, ident_b], [pb_b[bk]])
            P.cp('dve', mT[:, :, j * 128:(j + 1) * 128], pv, [pb_b[bk]], [mT_b])
        for which, outd in (('xk', mk_p), ('xv', mv_p)):
            for c in range(2):
                W, Wb = wget(which, c)
                for j in range(2):
                    bk = mmrot.nxt()
                    o = pb[bk][:, :]
                    P.mm([(o, mT[:, k, j * 128:(j + 1) * 128], W[:, k, :], k == 0, k == 7) for k in range(8)],
                         [mT_b, Wb], [pb_b[bk]])
                    P.cp('act', mst[:, j, c * 512:(c + 1) * 512], o, [pb_b[bk]], [mst_b])
                    if which == 'xv':
                        P.cp('dve', MV[:, j, c * 512:(c + 1) * 512], o, [pb_b[bk]], [MV_b])
                    else:
                        P.cp('dve', mkb[:, 0:512], o, [pb_b[bk]], [mkb_b])
                        tk = tprot.nxt()
                        pv = pb[tk].bitcast(BF16).rearrange("p (k t) -> p k t", k=8)
                        P.tr([(pv[:, k, :], mkb[:, k * 128:(k + 1) * 128], 128) for k in range(4)],
                             ident, [mkb_b, ident_b], [pb_b[tk]])
                        P.cp('act', MKT[:, c * 4:(c + 1) * 4, j * 128:(j + 1) * 128], pv[:, 0:4, :], [pb_b[tk]], [MKT_b])
            P.dma(outd[s * NMEM:(s + 1) * NMEM, :].rearrange("(j p) d -> p j d", p=128), mst[:, :, :], [mst_b], (), mst_b, q='pool')

    def mem_sample_load(s):
        P.dma(mld[:, :, :], cmk[s].rearrange("(j p) d -> p j d", p=128), (), [mld_b], mld_b)
        P.dma(mst[:, :, :], cmv[s].rearrange("(j p) d -> p j d", p=128), (), [mst_b], mst_b)

    def mem_sample(s):
        for j in range(2):
            P.cp('dve', mb16[:, j, :], mld[:, j, :], [mld_b], [mb16_b])
            bk = tprot.nxt()
            pv = pb[bk].bitcast(BF16).rearrange("p (k t) -> p k t", k=8)
            P.tr([(pv[:, k, :], mb16[:, j, k * 128:(k + 1) * 128], 128) for k in range(8)],
                 ident, [mb16_b, ident_b], [pb_b[bk]])
            P.cp('act', MKT[:, :, j * 128:(j + 1) * 128], pv, [pb_b[bk]], [MKT_b])
        for j in range(2):
            P.cp('act', MV[:, j, :], mst[:, j, :], [mst_b], [MV_b])
        if s + 1 < NSEQ:
            mem_sample_load(s + 1)

    def phase_A(tl, xt, xb):
        PT, NS, NT, G, L = tl.PT, tl.NS, tl.NT, tl.G, tl.L
        prompt = tl.kind == 'p'
        tq = tl.t if prompt else 0
        kcol0 = tl.t * TT if prompt else PAST
        jj0 = tl.t * 4 if prompt else 16
        W, Wb = wget('in', 0)

        def evq(j, o, ob):
            P.cp('act', q32x[0:PT, j, :], o, [ob], [q32x_b])
        proj_tokmajor(tl, W, Wb, hT, hT_b, evq)
        W, Wb = wget('in', 1)

        def evk(j, o, ob):
            P.cp('act', kst[0:PT, j, :], o, [ob], [kst_b])
        proj_tokmajor(tl, W, Wb, hT, hT_b, evk)
        rope(tl, q32x.rearrange("p j (a d) -> p j a d", a=8), q32x_b, jj0)
        P.cp('dve', qb16x[0:PT, 0:NS, :], q32x[0:PT, 0:NS, :], [q32x_b], [qb16x_b])
        rope(tl, kst.rearrange("p j (a d) -> p j a d", a=8), kst_b, jj0)
        P.cp('dve', kb16x[0:PT, 0:NS, :], kst[0:PT, 0:NS, :], [kst_b], [kb16x_b])
        if prompt:
            r0 = tl.s * SEQ + tl.t * TT
            P.dma(k_p[r0:r0 + TT, :].rearrange("(j p) c -> p j c", p=128), kst[:, :, :], [kst_b], (), kst_b, q='pool')
        else:
            P.dma(k_s[:, :], kst[0:64, 0, :], [kst_b], (), kst_b)
        W, Wb = wget('in', 2)

        def evv(j, o, ob):
            P.cp('act', vst[0:PT, j, :], o, [ob], [vst_b])
            kt = tl.t * 4 + j if prompt else 8
            P.cp('dve', V1[0:PT, kt, :, 0:128], vst[0:PT, j, :].rearrange("p (h e) -> p h e", h=4), [vst_b], [V1_b[tq if prompt else 2]])
        proj_tokmajor(tl, W, Wb, hT, hT_b, evv)
        if prompt:
            P.dma(v_p[r0:r0 + TT, :].rearrange("(j p) c -> p j c", p=128), vst[:, :, :], [vst_b], (), vst_b, q='pool')
        else:
            P.dma(v_s[:, :], vst[0:64, 0, :], [vst_b], (), vst_b)
        u4 = ubuf[:, 0:4 * G * (L + 2)].rearrange("p (c g l) -> p c g l", c=4, g=G)
        if prompt:
            if tl.first:
                P.memset('pool', u4[:, :, 0, 0:2], 0.0, [ubuf_b])
            else:
                P.cp('pool', u4[:, :, 0, 0:2], uhalo[:, :, :], [uhalo_b], [ubuf_b])
        else:
            for s in range(NSEQ):
                dma_state_in(u4[:, :, s, 0:2], ssc[s], [ubuf_b], ubuf_b)
        W, Wb = wget('in', 4)

        def evcg(c, o, ob):
            P.cp('act', cg32[:, c, 0:NT], o, [ob], [cg32_b])
        proj_featmajor(tl, W, Wb, 0, 4, hT, hT_b, evcg)
        for j in range(NS):
            bk = tprot.nxt()
            pv = pb[bk].bitcast(BF16).rearrange("p (k t) -> p k t", k=8)
            P.tr([(pv[:, h, 0:PT], qb16x[0:PT, j, h * 128:(h + 1) * 128], PT) for h in range(4)],
                 ident, [qb16x_b, ident_b], [pb_b[bk]])
            P.cp('dve', qT[:, :, j * PT:(j + 1) * PT], pv[:, 0:4, 0:PT], [pb_b[bk]], [qT_b])
        W, Wb = wget('in', 5)

        def evxh(c, o, ob):
            P.tt('dve', u4[:, c, :, 2:L + 2], cg32[:, c, 0:NT].rearrange("p (g l) -> p g l", g=G),
                 o.rearrange("p (g l) -> p g l", g=G), ALU.mult, [ob, cg32_b], [ubuf_b])
            conv3('pool', cg32[:, c, 0:NT].rearrange("p (g l) -> p g l", g=G), u4[:, c, :, :], wsc, c, G, L,
                  [ubuf_b, const_b], [cg32_b])
        proj_featmajor(tl, W, Wb, 0, 4, hT, hT_b, evxh)
        for j in range(NS):
            bk = tprot.nxt()
            pv = pb[bk].bitcast(BF16).rearrange("p (k t) -> p k t", k=8)
            P.tr([(pv[:, h, 0:PT], kb16x[0:PT, j, h * 128:(h + 1) * 128], PT) for h in range(4)],
                 ident, [kb16x_b, ident_b], [pb_b[bk]])
            P.cp('dve', KT[:, :, kcol0 + j * PT:kcol0 + (j + 1) * PT], pv[:, 0:4, 0:PT], [pb_b[bk]], [KT_b[tq]])
        if prompt:
            if tl.last:
                dma_state_out(sc_p[tl.s], u4[:, :, 0, L:L + 2], [ubuf_b], ubuf_b)
            else:
                P.cp('pool', uhalo[:, :, :], u4[:, :, 0, L:L + 2], [ubuf_b], [uhalo_b])
        else:
            for s in range(NSEQ):
                dma_state_out(sc_s[s], u4[:, :, s, L:L + 2], [ubuf_b], ubuf_b)
        W, Wb = wget('in', 3)

        def evbg(c, o, ob):
            P.tt('dve', catT[:, 4 + c, 0:NT], cg32[:, c, 0:NT], o, ALU.mult, [ob, cg32_b], [catT_b])
        proj_featmajor(tl, W, Wb, 0, 4, hT, hT_b, evbg)
        P.tag = 'A_attn'
        if prompt:
            attn_prompt(tl)
        else:
            attn_sample(tl)
        P.tag = 'A_out'
        proj_resid(tl, 'out', catT, catT_b, xt, xb)

    def attn_post_parts(h):
        A = pacc[:, :].rearrange("p (q c) -> p q c", q=4)[:, :, 0:258].rearrange("p q (m e) -> p q m e", m=2)
        accb = [pb_b[i] for i in range(4)]
        st, stb = newstat()
        rec = st[:, 0:8].rearrange("p (q m) -> p q m", q=4)
        rl = st[:, 8:12]
        rs = st[:, 12:16]

        def part1():
            P.add('dve', lambda e: e.reciprocal(rec, A[:, :, :, 128]), accb, [stb])
            P.ts('dve', rl, rec[:, :, 1], neg_lam, None, ALU.mult, None, [stb, const_b], [stb])
            P.tt('dve', t1[:, :, :], A[:, :, 0, 0:128], rec[:, :, 0:1].to_broadcast([128, 4, 128]), ALU.mult, accb + [stb], [t1_b])
            P.tt('dve', o32[:, :, :], A[:, :, 1, 0:128], rl.unsqueeze(2).to_broadcast([128, 4, 128]), ALU.mult, accb + [stb], [o32_b])
            P.tt('dve', o32[:, :, :], o32[:, :, :], t1[:, :, :], ALU.add, [t1_b, o32_b], [o32_b])

        def part2(bk):
            def fn(e):
                ins = None
                for q in range(4):
                    ins = e.activation(junk[:, q, 0:128], o32[:, q, :], AF.Square, accum_out=rs[:, q:q + 1], saturate=False)
                return ins
            P.add('act', fn, [o32_b], [stb])
            rstd_from_ss(rs, 4, stb, 1.0 / 128)

        def part3(bk):
            P.tt('dve', otm[:, :, :], o32[:, :, :], rs.unsqueeze(2).to_broadcast([128, 4, 128]), ALU.mult, [o32_b, stb], [otm_b])
            pv = pb[bk].bitcast(BF16).rearrange("p (k t) -> p k t", k=8)
            P.tr([(pv[:, q, :], otm[:, q, :], 128) for q in range(4)], ident, [otm_b, ident_b], [pb_b[bk]])
            P.cp('dve', catT[:, h, 0:512].rearrange("p (q t) -> p q t", q=4), pv[:, 0:4, :], [pb_b[bk]], [catT_b])
        return part1, part2, part3

    def attn_prompt(tl):
        t = tl.t
        nkt = t * 4 + 4
        steps = [(h, kt) for h in range(4) for kt in range(nkt)]
        stq = {}
        pti = [0]

        def emit_st(i):
            h, kt = steps[i]
            r = kt - t * 4
            q0 = max(0, r) * 128
            N = 512 - q0
            lst = []
            for m in range(2):
                bk = 4 + 2 * (i % 2) + m
                o = pb[bk][:, 0:N]
                P.mm([(o, KT[m * 64:(m + 1) * 64, h, kt * 128:(kt + 1) * 128], qT[m * 64:(m + 1) * 64, h, q0:512], True, True)],
                     [KT_b[kt // 4], qT_b], [pb_b[bk]])
                lst.append((bk, o))
            stq[i] = (lst, r, q0, N)

        def emit_exp_pv(i):
            h, kt = steps[i]
            lst, r, q0, N = stq.pop(i)
            par = i % 2
            pp, ppb = ptP[pti[0] % 4]
            pti[0] += 1
            stv = pst[:, par * 1024:(par + 1) * 1024].rearrange("p (m q) -> p m q", m=2)
            stb = [pb_b[4 + 2 * par], pb_b[5 + 2 * par]]
            P.act(pp[:, :, q0:512], stv[:, :, 0:N], AF.Exp, stb, [ppb], scale=0.125)
            if r >= 0:
                P.memset('dve', pp[64:128, :, q0:q0 + 64], 0.0, [ppb])
            for qb in range(max(0, r), 4):
                items = []
                for m in range(2):
                    A = pb[qb][:, 0:258].rearrange("p (m e) -> p m e", m=2)
                    items.append((A[:, m, :], pp[:, m, qb * 128:(qb + 1) * 128], V1[:, kt, h, 0:129],
                                  kt == 0 and m == 0, kt == t * 4 + qb, True))
                P.mm(items, [ppb, V1_b[kt // 4]], [pb_b[qb]])

        deferred = []
        emit_st(0)
        for i, (h, kt) in enumerate(steps):
            if i + 1 < len(steps):
                emit_st(i + 1)
            emit_exp_pv(i)
            for d in [d for d in deferred if d[0] <= i]:
                d[1](4 + 2 * (i % 2))
                deferred.remove(d)
            if kt == nkt - 1:
                p1, p2, p3 = attn_post_parts(h)
                p1()
                deferred.append((i + 2, p2))
                deferred.append((i + 4, p3))
        for d in deferred:
            d[1](4)

    def attn_sample(tl):
        ptz, ptz_b = ptA[0][0], ptA[0][1]
        ptn, ptn_b = ptA[2][0], ptA[2][1]
        P.dma(ckst[:, :, :], ck[0].rearrange("(j p) c -> p j c", p=128), (), [ckst_b], ckst_b)
        for s in range(NSEQ):
            P.dma(ckst2[:, :, :], cv[s].rearrange("(j p) c -> p j c", p=128), (), [ckst2_b], ckst2_b)
            for half in range(2):
                P.cp('dve', ckb[:, :, :], ckst[:, half * 4:(half + 1) * 4, :], [ckst_b], [ckb_b])
                for j4 in range(4):
                    j = half * 4 + j4
                    bk = tprot.nxt()
                    pv = pb[bk].bitcast(BF16).rearrange("p (k t) -> p k t", k=8)
                    P.tr([(pv[:, h, :], ckb[:, j4, h * 128:(h + 1) * 128], 128) for h in range(4)],
                         ident, [ckb_b, ident_b], [pb_b[bk]])
                    P.cp('dve', KT[:, :, j * 128:(j + 1) * 128], pv[:, 0:4, :], [pb_b[bk]], [KT_b[0]])
            if s + 1 < NSEQ:
                P.dma(ckst[:, :, :], ck[s + 1].rearrange("(j p) c -> p j c", p=128), (), [ckst_b], ckst_b)
            P.cp('act', V1[:, 0:8, :, 0:128], ckst2[:, :, :].rearrange("p j (h e) -> p j h e", h=4), [ckst2_b], [V1_b[0]])
            for h in range(4):
                pz = ptz[:, 0:256].rearrange("p (k m q) -> p k m q", k=8, m=2)
                pn = ptn[0:64, 0:32].rearrange("p (m q) -> p m q", m=2)
                for m in range(2):
                    bk = strot.nxt()
                    sv = pb[bk][:, 0:128].rearrange("p (k q) -> p k q", k=8)
                    nv = pb[bk][0:64, 128:144]
                    its = [(sv[:, kt, :], KT[m * 64:(m + 1) * 64, h, kt * 128:(kt + 1) * 128],
                            qT[m * 64:(m + 1) * 64, h, s * 16:(s + 1) * 16], True, True) for kt in range(8)]
                    its.append((nv, KT[m * 64:(m + 1) * 64, h, PAST:PAST + 64],
                                qT[m * 64:(m + 1) * 64, h, s * 16:(s + 1) * 16], True, True))
                    P.mm(its, [KT_b[0], qT_b], [pb_b[bk]])
                    P.act(pz[:, :, m, :], sv, AF.Exp, [pb_b[bk]], [ptz_b], scale=0.125)
                    P.act(pn[:, m, :], nv, AF.Exp, [pb_b[bk], const_b], [ptn_b], bias=cmask[0:64, 1 + s:2 + s], scale=0.125)
                A = pb[h][0:16, 0:258].rearrange("p (m e) -> p m e", m=2)
                items = []
                for m in range(2):
                    for kt in range(8):
                        items.append((A[:, m, :], pz[:, kt, m, 0:16], V1[:, kt, h, 0:129], kt == 0, False))
                    items.append((A[:, m, :], pn[:, m, :], V1[0:64, 8, h, 0:129], False, True))
                P.mm(items, [ptz_b, ptn_b, V1_b[0], V1_b[2]], [pb_b[h]])
            for h in range(4):
                attn_post_sample(s, h)

    def attn_post_sample(s, h):
        PT = 16
        bk = h
        A = pb[bk][0:PT, 0:258].rearrange("p (m e) -> p m e", m=2)
        st, stb = newstat()
        P.add('dve', lambda e: e.reciprocal(st[0:PT, 0:2], A[:, :, 128]), [pb_b[bk]], [stb])
        P.ts('dve', st[0:PT, 2:3], st[0:PT, 1:2], neg_lam[0:PT, :], None, ALU.mult, None, [stb, const_b], [stb])
        P.ts('dve', t1[0:PT, 0, :], A[:, 0, 0:128], st[0:PT, 0:1], None, ALU.mult, None, [pb_b[bk], stb], [t1_b])
        P.stt('dve', o32[0:PT, 0, :], A[:, 1, 0:128], st[0:PT, 2:3], t1[0:PT, 0, :], ALU.mult, ALU.add,
              [pb_b[bk], stb, t1_b], [o32_b])
        P.act(junk[0:PT, statrot.i % 4, 0:128], o32[0:PT, 0, :], AF.Square, [o32_b], [stb], accum_out=st[0:PT, 3:4], saturate=False)
        rstd_from_ss(st[0:PT, 3:4], 1, stb, 1.0 / 128)
        P.act(otm[0:PT, 0, :], o32[0:PT, 0, :], AF.Copy, [o32_b, stb], [otm_b], scale=st[0:PT, 3:4])
        tb = strot.nxt()
        pv = pb[tb].bitcast(BF16).rearrange("p (k t) -> p k t", k=8)
        P.tr([(pv[:, 0, 0:PT], otm[0:PT, 0, :], PT)], ident, [otm_b, ident_b], [pb_b[tb]])
        P.cp('dve', catT[:, h, s * 16:(s + 1) * 16], pv[:, 0, 0:PT], [pb_b[tb]], [catT_b])

    def phase_B(tl, xt, xb):
        PT, NS, NT = tl.PT, tl.NS, tl.NT
        norm_to_hT(tl, xt, xb)
        for c in range(2):
            W, Wb = wget('xq', c)

            def evq(cc, o, ob, c=c):
                P.cp('act', hqT[:, c * 4 + cc, 0:NT], o, [ob], [hqT_b])
            proj_featmajor(tl, W, Wb, 0, 4, hT, hT_b, evq)
        if tl.kind == 'p':
            cross_attn(tl, 0, NT)
        else:
            mem_sample_load(0)
            for s in range(NSEQ):
                mem_sample(s)
                cross_attn(tl, s * 16, 16)
        proj_resid(tl, 'xo', xoT, xoT_b, xt, xb)

    xorot = Rot([0, 1, 2, 3])

    cxs = {'i': 0}

    def cross_attn(tl, c0, n):
        scale = 256 ** -0.5
        for h in range(4):
            ci = cxs['i']
            cxs['i'] += 1
            pts = []
            for mt in range(2):
                bk = strot.nxt()
                o = pb[bk][:, 0:n]
                P.mm([(o, MKT[:, h * 2 + dd, mt * 128:(mt + 1) * 128], hqT[:, h * 2 + dd, c0:c0 + n], dd == 0, dd == 1)
                      for dd in range(2)], [MKT_b, hqT_b], [pb_b[bk]])
                pt, ptb = ptB[(ci % 2) * 2 + mt]
                P.act(pt[:, 0:n], o, AF.Exp, [pb_b[bk]], [ptb], scale=scale)
                pts.append((pt, ptb))
            rden, rden_b = rdens[ci % 2]
            dk = 7
            dn = pb[dk][:, 0:n]
            P.mm([(dn, ones[:, :], pts[mt][0][:, 0:n], mt == 0, mt == 1) for mt in range(2)],
                 [pts[0][1], pts[1][1], ident_b], [pb_b[dk]])
            P.add('dve', lambda e, dn=dn, rden=rden: e.reciprocal(rden[:, 0:n], dn), [pb_b[dk]], [rden_b])
            for dd in range(2):
                bk = xorot.nxt()
                o = pb[bk][:, 0:n]
                P.mm([(o, MV[:, mt, h * 256 + dd * 128:h * 256 + (dd + 1) * 128], pts[mt][0][:, 0:n], mt == 0, mt == 1)
                      for mt in range(2)], [MV_b, pts[0][1], pts[1][1]], [pb_b[bk]])
                P.tt('dve', xoT[:, h * 2 + dd, c0:c0 + n], o, rden[:, 0:n], ALU.mult, [pb_b[bk], rden_b], [xoT_b])

    ffrot = Rot([0, 1, 2, 3, 4, 5, 6, 7])

    def phase_C(tl, xt, xb):
        PT, NS, NT, G, L = tl.PT, tl.NS, tl.NT, tl.G, tl.L
        prompt = tl.kind == 'p'
        norm_to_hT(tl, xt, xb)
        if not prompt:
            for s in range(NSEQ):
                dma_state_in(sphalo[:, :, s, :], sff[s], [sphalo_b], sphalo_b)
        elif tl.first:
            P.memset('pool', uphalo[:, :, :], 0.0, [uphalo_b])
        ui = 0
        for i in range(11):
            W, Wb = wget('ug', i)
            for cc in range(2):
                ch = 2 * i + cc
                bu = ffrot.nxt()
                ou = pb[bu][:, 0:NT]
                P.mm([(ou, W[:, k, cc * 128:(cc + 1) * 128], hT[:, k, 0:NT], k == 0, k == 7) for k in range(8)],
                     hT_b + [Wb], [pb_b[bu]])
                bg_ = ffrot.nxt()
                og = pb[bg_][:, 0:NT]
                P.mm([(og, W[:, k, 256 + cc * 128:256 + (cc + 1) * 128], hT[:, k, 0:NT], k == 0, k == 7) for k in range(8)],
                     hT_b + [Wb], [pb_b[bg_]])
                ub, ubb = upb[ui % 2]
                cb_, cbb = cvb[ui % 2]
                sl, slb_ = slb[ui % 2]
                ui += 1
                u3 = ub[:, 0:G * (L + 2)].rearrange("p (g l) -> p g l", g=G)
                if prompt:
                    P.cp('pool', u3[:, 0, 0:2], uphalo[:, ch, :], [uphalo_b], [ubb])
                else:
                    P.cp('pool', u3[:, :, 0:2], sphalo[:, ch, :, :], [sphalo_b], [ubb])
                P.cp('act', u3[:, :, 2:L + 2], ou.rearrange("p (g l) -> p g l", g=G), [pb_b[bu]], [ubb])
                if prompt:
                    P.cp('pool', uphalo[:, ch, :], u3[:, 0, L:L + 2], [ubb], [uphalo_b])
                else:
                    P.cp('pool', sphalo[:, ch, :, :], u3[:, :, L:L + 2], [ubb], [sphalo_b])
                c3 = cb_[:, 0:NT].rearrange("p (g l) -> p g l", g=G)
                conv3('pool', c3, u3, wfc, ch, G, L, [ubb, const_b], [cbb])
                P.act(sl[:, 0:NT], cb_[:, 0:NT], AF.Silu, [cbb], [slb_])
                P.tt('dve', gT[:, ch, 0:NT], sl[:, 0:NT], og, ALU.mult, [pb_b[bg_], slb_], [gT_b3[ch // 8]])
        if prompt:
            if tl.last:
                dma_state_out(ff_p[tl.s], uphalo[:, :, :], [uphalo_b], uphalo_b)
        else:
            for s in range(NSEQ):
                dma_state_out(ff_s[s], sphalo[:, :, s, :], [sphalo_b], sphalo_b)
        P.tag = 'C_down'
        for c in range(2):
            banks = [c * 4 + j for j in range(NS)]
            for kb in range(3):
                W, Wb = wget('dn', c * 3 + kb)
                nk = 8 if kb < 2 else 6
                for j in range(NS):
                    o = pb[banks[j]][0:PT, :]
                    P.mm([(o, gT[:, kb * 8 + k, j * PT:(j + 1) * PT], W[:, k, :], kb == 0 and k == 0, kb == 2 and k == nk - 1)
                          for k in range(nk)], [gT_b3[kb], Wb], [pb_b[banks[j]]])
            for j in range(NS):
                xsl = xt[0:PT, j, c * 512:(c + 1) * 512]
                P.tt('dve', xsl, pb[banks[j]][0:PT, :], xsl, ALU.add, [pb_b[banks[j]], xb[j]], [xb[j]])

    def final_norm(tl, xt, xb, xtile_b):
        PT, NS = tl.PT, tl.NS
        for j in range(NS):
            ss, ssb = newstat()
            P.act(junk[0:PT, j, :], xt[0:PT, j, :], AF.Square, [xb[j]], [ssb], accum_out=ss[0:PT, 0:1], saturate=False)
            rstd_from_ss(ss[0:PT, 0:1], 1, ssb, 1.0 / D)
            P.stt('dve', xt[0:PT, j, :], xt[0:PT, j, :], ss[0:PT, 0:1], gfin[0:PT, 0, :], ALU.mult, ALU.mult,
                  [xb[j], ssb, const_b], [xb[j]])
        if tl.kind == 'p':
            r0 = tl.s * SEQ + tl.t * TT
            P.dma(y_p[r0:r0 + TT, :].rearrange("(j p) d -> p j d", p=128), xt[:, :, :], [xtile_b], (), xtile_b, q='pool')
        else:
            P.dma(y_s[:, :], xt[0:64, 0, :], [xtile_b], (), xtile_b)

    def load_x(i):
        tl = tiles[i]
        xt, xb = xs_t[i % 2], xs_b[i % 2]
        if tl.kind == 'p':
            r0 = tl.s * SEQ + tl.t * TT
            P.dma(xt[:, :, :], xp[r0:r0 + TT, :].rearrange("(j p) d -> p j d", p=128), (), [xb], xb)
        else:
            P.dma(xt[0:64, 0, :], xsm[:, :], (), [xb], xb)

    load_x(0)
    P.tag = 'A_inproj'
    norm_to_hT(tiles[0], xs_t[0], xsub_b[0])
    for i, tl in enumerate(tiles):
        xt, xb, xtb = xs_t[i % 2], xsub_b[i % 2], xs_b[i % 2]
        if i + 1 < len(tiles):
            load_x(i + 1)
        if tl.kind == 'p' and tl.first:
            P.tag = 'mem'
            mem_phase(tl)
        P.tag = 'A_inproj'
        phase_A(tl, xt, xb)
        P.tag = 'B'
        phase_B(tl, xt, xb)
        P.tag = 'C_upgate'
        phase_C(tl, xt, xb)
        if i + 1 < len(tiles):
            P.tag = 'A_inproj'
            norm_to_hT(tiles[i + 1], xs_t[(i + 1) % 2], xsub_b[(i + 1) % 2])
        final_norm(tl, xt, xb, xtb)
    assert wstate['ptr'] == len(wseq)
    P.sbuf_left = nc.sbuf_bytes_remaining
    P.build(es)
    es.close()
    return nc, P


def _consts():
    half = 8
    inv = (1.0 / (500000.0 ** (np.arange(half, dtype=np.float32) * np.float32(2.0) / np.float32(16)))).astype(np.float32)
    pos = np.zeros((17, 128), np.float32)
    for j in range(16):
        pos[j] = j * 128 + np.arange(128)
    pos[16, :16] = PAST + np.arange(16)
    pos[16, :64] = PAST + (np.arange(64) % 16)
    ang = (pos[:, :, None].astype(np.float32) * inv[None, None, :]).astype(np.float32)
    cos = np.cos(ang).astype(np.float32).transpose(1, 0, 2).copy()
    sin = np.sin(ang).astype(np.float32).transpose(1, 0, 2).copy()
    ident = np.eye(128, dtype=np.float32)
    cm = np.zeros((128, 8), np.float32)
    cm[64:, 0] = NEG
    for s in range(4):
        cm[:, 1 + s] = NEG
        cm[s * 16:(s + 1) * 16, 1 + s] = 0.0
    return cos, sin, ident, cm


_CACHE = {}


def kernel(**inputs):
    f = lambda a: np.ascontiguousarray(np.asarray(a, dtype=np.float32))
    if 'nc' not in _CACHE:
        _CACHE['nc'] = build_program()[0]
    nc = _CACHE['nc']
    cos, sin, ident, cm = _consts()
    shared = {
        "g_mix": f(inputs["g_mix"]), "g_mem": f(inputs["g_mem"]), "g_x": f(inputs["g_x"]),
        "g_ffn": f(inputs["g_ffn"]), "g_final": f(inputs["g_final"]).reshape(1, D), "g_sub": f(inputs["g_sub"]),
        "lam_q1": f(inputs["lam_q1"]), "lam_k1": f(inputs["lam_k1"]), "lam_q2": f(inputs["lam_q2"]), "lam_k2": f(inputs["lam_k2"]),
        "w_in": f(inputs["w_in"])[0], "w_out": f(inputs["w_out"])[0], "w_xq": f(inputs["w_xq"])[0],
        "w_xk": f(inputs["w_xk"])[0], "w_xv": f(inputs["w_xv"])[0], "w_xo": f(inputs["w_xo"])[0],
        "w_up": f(inputs["w_up"])[0], "w_gate": f(inputs["w_gate"])[0], "w_down": f(inputs["w_down"])[0],
        "w_sc": f(inputs["w_sc"])[0], "w_ffconv": f(inputs["w_ffconv"])[0],
        "rope_cos": cos, "rope_sin": sin, "ident": ident, "cmask": cm,
    }
    xpr = f(inputs["x_prompt"]); xsa = f(inputs["x_sample"])
    cak = f(inputs["cache_attn_k"])[0]; cav = f(inputs["cache_attn_v"])[0]
    sscv = f(inputs["state_short_conv"])[0]; sffv = f(inputs["state_ffn_conv"])[0]
    cmkv = f(inputs["cache_mem_k"])[0]; cmvv = f(inputs["cache_mem_v"])[0]; mp = f(inputs["mem_prompt"])
    in_maps = []
    for c in range(NCORES):
        sl = slice(c * NSEQ, (c + 1) * NSEQ)
        m = dict(shared)
        m["xp"] = xpr[sl].reshape(NSEQ * SEQ, D)
        m["xsm"] = xsa[sl].reshape(NSEQ * DEC, D)
        m["ck"] = cak[sl].reshape(NSEQ, PAST, 512)
        m["cv"] = cav[sl].reshape(NSEQ, PAST, 512)
        m["ssc"] = sscv[sl]
        m["sff"] = sffv[sl]
        m["cmk"] = cmkv[sl].reshape(NSEQ, NMEM, D)
        m["cmv"] = cmvv[sl].reshape(NSEQ, NMEM, D)
        m["memp"] = mp[sl]
        in_maps.append(m)
    res = run_bass_kernel_spmd(nc, in_maps, core_ids=list(range(NCORES)))
    R = res.results

    def cat(name, shape):
        return np.concatenate([np.asarray(r[name], dtype=np.float32) for r in R], axis=0).reshape(shape)

    B = NCORES * NSEQ
    return (
        cat("y_p", (B, SEQ, D)),
        cat("y_s", (B, DEC, D)),
        cat("k_p", (B, SEQ, 4, 2, 64))[None],
        cat("v_p", (B, SEQ, 4, 128))[None],
        cat("sc_p", (B, 2, 512))[None],
        cat("ff_p", (B, 2, DFF))[None],
        cat("mk_p", (B, NMEM, 4, 256))[None],
        cat("mv_p", (B, NMEM, 4, 256))[None],
        cat("k_s", (B, DEC, 4, 2, 64))[None],
        cat("v_s", (B, DEC, 4, 128))[None],
        cat("sc_s", (B, 2, 512))[None],
        cat("ff_s", (B, 2, DFF))[None],
    )
```

```python
import math
from contextlib import ExitStack
import numpy as np
import concourse.bass as bass
import concourse.mybir as mybir
from concourse.bass_utils import run_bass_kernel_spmd

F32 = mybir.dt.float32
BF16 = mybir.dt.bfloat16
AF = mybir.ActivationFunctionType
ALU = mybir.AluOpType

NCORES = 8
D = 1024
SEQ = 2048
NSEQ = 4
TT = 512
DEC = 16
PAST = 1024
DFF = 2816
NFC = 22
NMEM = 256
EPS = 1e-6
LAM_INIT = 0.8 - 0.6 * math.exp(-0.3 * 0)
NEG = -30000.0
WINDOW = 2
ENGS = ('pe', 'act', 'dve', 'pool', 'sp')


class Buf:
    __slots__ = ('name', 'lw', 'rd', 'al', 'excl')

    def __init__(self, name, excl=False):
        self.name = name
        self.excl = excl
        self.lw = None
        self.rd = {}
        self.al = [self]


class Op:
    __slots__ = ('eng', 'fn', 'key', 'eidx', 'gidx', 'deps', 'sig', 'cnt', 'dcount')


class Prog:
    def __init__(self, nc):
        self.nc = nc
        self.ops = {e: [] for e in ENGS}
        self.all = []
        self.tag = ''
        self.petags = []

    def add(self, eng, fn, rd=(), wr=(), key=None):
        op = Op()
        op.eng = eng
        op.fn = fn
        op.key = key
        op.eidx = len(self.ops[eng])
        op.gidx = len(self.all)
        op.sig = False
        op.cnt = 0
        op.dcount = 0
        deps = {}
        xw = [b for b in rd if b.excl]
        if xw:
            rd = [b for b in rd if not b.excl]
            wr = list(wr) + [b for b in xw if b not in wr]

        def dep(d):
            if d is None:
                return
            k = ('d', id(d.key), d.eng) if d.key is not None else d.eng
            o = deps.get(k)
            if o is None or o.gidx < d.gidx:
                deps[k] = d

        for b in rd:
            for a in b.al:
                dep(a.lw)
        for b in wr:
            for a in b.al:
                dep(a.lw)
                for r in a.rd.values():
                    dep(r)
        op.deps = list(deps.values())
        mk = ('d', id(key), eng) if key is not None else eng
        for b in rd:
            b.rd[mk] = op
        for b in wr:
            b.lw = op
            b.rd = {}
        self.ops[eng].append(op)
        self.all.append(op)
        return op

    def dma(self, out, in_, rd, wr, key, q='sp'):
        self.add(q, lambda e: e.dma_start(out=out, in_=in_), rd, wr, key=key)

    def mm(self, items, rd, wr):
        tg = self.tag

        def fn(e):
            ins = None
            for it in items:
                self.petags.append(tg)
                o, l, r, st, sp = it[:5]
                if len(it) > 5:
                    ins = e.matmul(o, l, r, start=st, stop=sp, skip_group_check=True)
                else:
                    ins = e.matmul(o, l, r, start=st, stop=sp)
            return ins
        self.add('pe', fn, rd, wr)

    def tr(self, items, ident, rd, wr):
        tg = self.tag

        def fn(e):
            ins = None
            for (o, i, n) in items:
                self.petags.append(tg)
                ins = e.transpose(o, i, ident[0:n, 0:n])
            return ins
        self.add('pe', fn, rd, wr)

    def act(self, out, in_, func, rd, wr, **kw):
        self.add('act', lambda e: e.activation(out, in_, func, **kw), rd, wr)

    def cp(self, eng, out, in_, rd, wr):
        if eng == 'act':
            self.add('act', lambda e: e.copy(out, in_), rd, wr)
        else:
            self.add(eng, lambda e: e.tensor_copy(out, in_), rd, wr)

    def tt(self, eng, out, a, b, op, rd, wr):
        self.add(eng, lambda e: e.tensor_tensor(out, a, b, op), rd, wr)

    def ts(self, eng, out, a, s1, s2, op0, op1, rd, wr):
        if s2 is None:
            self.add(eng, lambda e: e.tensor_scalar(out, a, s1, None, op0), rd, wr)
        else:
            self.add(eng, lambda e: e.tensor_scalar(out, a, s1, s2, op0, op1), rd, wr)

    def stt(self, eng, out, a, s, b, op0, op1, rd, wr):
        self.add(eng, lambda e: e.scalar_tensor_tensor(out, a, s, b, op0, op1), rd, wr)

    def memset(self, eng, ap, val, wr):
        self.add(eng, lambda e: e.memset(ap, val), (), wr)

    def build(self, es):
        nc = self.nc
        import os
        mx = int(os.environ.get("K_MAXOPS", "0"))
        if mx:
            self.all = [o for o in self.all if o.gidx < mx]
            for e in ENGS:
                self.ops[e] = [o for o in self.ops[e] if o.gidx < mx]
        for op in self.all:
            for d in op.deps:
                if d.key is not None:
                    continue
                if d.eng != op.eng or op.key is not None:
                    d.sig = True
                elif op.eng != 'pe' and d.eidx >= op.eidx - WINDOW:
                    d.sig = True
        esem = {}
        for e in ENGS:
            esem[e] = es.enter_context(nc.semaphore("es_" + e))
            c = 0
            for op in self.ops[e]:
                if op.key is None and op.sig:
                    c += 1
                op.cnt = c
        dsem = {}
        dcnt = {}
        for op in self.all:
            if op.key is not None:
                k = (id(op.key), op.eng)
                if k not in dsem:
                    dsem[k] = es.enter_context(nc.semaphore("ds_%d" % len(dsem)))
                    dcnt[k] = 0
                dcnt[k] += 16
                op.dcount = dcnt[k]
        self.nsem = len(dsem) + len(esem)

        def emit(ename, eng):
            waited = {}
            for op in self.ops[ename]:
                need = {}
                for d in op.deps:
                    if d.key is not None:
                        sem, v = dsem[(id(d.key), d.eng)], d.dcount
                    elif d.eng != ename or op.key is not None:
                        sem, v = esem[d.eng], d.cnt
                    elif ename != 'pe' and d.eidx >= op.eidx - WINDOW:
                        sem, v = esem[ename], d.cnt
                    else:
                        continue
                    if need.get(sem.num, (None, 0))[1] < v:
                        need[sem.num] = (sem, v)
                for num, (sem, v) in need.items():
                    if waited.get(num, 0) < v:
                        eng.wait_ge(sem, v)
                        waited[num] = v
                ins = op.fn(eng)
                if op.key is not None:
                    ins.then_inc(dsem[(id(op.key), op.eng)], 16)
                elif op.sig:
                    ins.then_inc(esem[ename], 1)
            if ename == 'sp':
                for k, sem in dsem.items():
                    if waited.get(sem.num, 0) < dcnt[k]:
                        eng.wait_ge(sem, dcnt[k])

        with nc.Block() as block:
            @block.tensor
            def _(e):
                emit('pe', e)

            @block.scalar
            def _(e):
                emit('act', e)

            @block.vector
            def _(e):
                emit('dve', e)

            @block.gpsimd
            def _(e):
                emit('pool', e)

            @block.sync
            def _(e):
                emit('sp', e)


def build_program(phases=99):
    nc = bass.Bass("TRN2", target_bir_lowering=False)
    es = ExitStack()
    es.enter_context(nc.allow_low_precision("bf16 matmul operands, fp32 accumulation"))
    es.enter_context(nc.allow_non_contiguous_dma("small constant / state layouts"))
    P = Prog(nc)

    def din(name, shape):
        return nc.dram_tensor(name, shape, F32, kind="ExternalInput").ap()

    def dout(name, shape):
        return nc.dram_tensor(name, shape, F32, kind="ExternalOutput").ap()

    xp = din("xp", [NSEQ * SEQ, D])
    xsm = din("xsm", [NSEQ * DEC, D])
    ck = din("ck", [NSEQ, PAST, 512])
    cv = din("cv", [NSEQ, PAST, 512])
    ssc = din("ssc", [NSEQ, 2, 512])
    sff = din("sff", [NSEQ, 2, DFF])
    cmk = din("cmk", [NSEQ, NMEM, D])
    cmv = din("cmv", [NSEQ, NMEM, D])
    memp = din("memp", [NSEQ, NMEM, D])
    g_mix = din("g_mix", [1, D]); g_mem = din("g_mem", [1, D]); g_x = din("g_x", [1, D])
    g_ffn = din("g_ffn", [1, D]); g_final = din("g_final", [1, D]); g_sub = din("g_sub", [1, 128])
    lq1 = din("lam_q1", [1, 64]); lk1 = din("lam_k1", [1, 64])
    lq2 = din("lam_q2", [1, 64]); lk2 = din("lam_k2", [1, 64])
    w_in = din("w_in", [D, 3072]); w_out = din("w_out", [D, D])
    w_xq = din("w_xq", [D, D]); w_xk = din("w_xk", [D, D]); w_xv = din("w_xv", [D, D]); w_xo = din("w_xo", [D, D])
    w_up = din("w_up", [D, DFF]); w_gate = din("w_gate", [D, DFF]); w_down = din("w_down", [DFF, D])
    w_sc = din("w_sc", [3, 512]); w_fc = din("w_ffconv", [3, DFF])
    cos_d = din("rope_cos", [128, 17, 8]); sin_d = din("rope_sin", [128, 17, 8])
    ident_d = din("ident", [128, 128]); cmask_d = din("cmask", [128, 8])

    y_p = dout("y_p", [NSEQ * SEQ, D]); y_s = dout("y_s", [NSEQ * DEC, D])
    k_p = dout("k_p", [NSEQ * SEQ, 512]); v_p = dout("v_p", [NSEQ * SEQ, 512])
    sc_p = dout("sc_p", [NSEQ, 2, 512]); ff_p = dout("ff_p", [NSEQ, 2, DFF])
    mk_p = dout("mk_p", [NSEQ * NMEM, D]); mv_p = dout("mv_p", [NSEQ * NMEM, D])
    k_s = dout("k_s", [NSEQ * DEC, 512]); v_s = dout("v_s", [NSEQ * DEC, 512])
    sc_s = dout("sc_s", [NSEQ, 2, 512]); ff_s = dout("ff_s", [NSEQ, 2, DFF])

    NBLK = {'in': 6, 'out': 2, 'xq': 2, 'xk': 2, 'xv': 2, 'xo': 2, 'ug': 11, 'dn': 6}
    scr = {}
    scrb = {}
    for nm, n in NBLK.items():
        scr[nm] = nc.dram_tensor("scr_" + nm, [n, 128, 8 * 512], BF16, kind="ExternalOutput").ap()
        scrb[nm] = Buf("scr_" + nm)

    def sb(name, shape, dt=F32):
        return es.enter_context(nc.sbuf_tensor("sb_" + name, shape, dt))

    xs_t = [sb("xs%d" % i, [128, 4, D]) for i in range(2)]
    xs_b = [Buf("xs%d" % i) for i in range(2)]
    xsub_b = [[Buf("xs%d_%d" % (i, j)) for j in range(4)] for i in range(2)]
    for i in range(2):
        xs_b[i].al = [xs_b[i]] + xsub_b[i]
        for j in range(4):
            xsub_b[i][j].al = [xsub_b[i][j], xs_b[i]]
    hT = sb("hT", [128, 8, TT], BF16); hT_bj = [Buf("hT%d" % j) for j in range(4)]; hT_b = hT_bj
    KT = sb("KT", [128, 4, SEQ], BF16); KT_b = [Buf("KT%d" % i) for i in range(4)]
    V1 = sb("V1", [128, 16, 4, 130], BF16); V1_b = [Buf("V1%d" % i) for i in range(4)]
    ring_t = [sb("ring%d" % i, [128, 8, 512], BF16) for i in range(4)]
    ring_b = [Buf("ring%d" % i) for i in range(4)]
    cst_t = [sb("cst%d" % i, [128, 512]) for i in range(4)]
    cst_b = [Buf("cst%d" % i) for i in range(4)]
    MKT = sb("MKT", [128, 8, NMEM], BF16); MKT_b = Buf("MKT")
    MV = sb("MV", [128, 2, D], BF16); MV_b = Buf("MV")
    ident32 = sb("ident32", [128, 128]); ident = sb("ident", [128, 128], BF16); ident_b = Buf("ident")
    ones = sb("ones", [128, 128], BF16)
    cosT = sb("cosT", [128, 17, 8]); sinT = sb("sinT", [128, 17, 8]); cmask = sb("cmask", [128, 8])
    const_b = Buf("const")
    gcol = sb("gcol", [128, 5, 8])
    gsub1 = sb("gsub1", [128, 1])
    epsc = sb("epsc", [128, 1])
    gfin = sb("gfin", [128, 1, D])
    lamv = sb("lamv", [128, 4, 64]); lamt = sb("lamt", [128, 2, 64]); lams = sb("lams", [128, 4])
    wsc = sb("wsc", [128, 4, 3]); wfc = sb("wfc", [128, NFC, 3])
    uhalo = sb("uhalo", [128, 4, 2]); uhalo_b = Buf("uhalo")
    uphalo = sb("uphalo", [128, NFC, 2]); uphalo_b = Buf("uphalo")
    sphalo = sb("sphalo", [128, NFC, 4, 2]); sphalo_b = Buf("sphalo")
    stat = sb("stat", [128, 128]); stat_b = [Buf("stat%d" % i) for i in range(8)]
    junk = sb("junk", [128, 4, D], mybir.dt.float8e4); junk_b = Buf("junk")
    hb_t = [sb("hb%d" % i, [128, D], BF16) for i in range(2)]
    hb_b = [Buf("hb%d" % i) for i in range(2)]

    OVL = 69 * 1024
    ovl = sb("ovl", [128, OVL // 2], BF16)
    ovl_bufs = []

    def ov(name, off, nbytes, dt, pattern=None, **kw):
        a = ovl[:, off // 2:(off + nbytes) // 2]
        if dt == F32:
            a = a.bitcast(F32)
        if pattern:
            a = a.rearrange(pattern, **kw)
        b = Buf(name)
        ovl_bufs.append((b, off, off + nbytes))
        return a, b

    K = 1024
    qT, qT_b = ov("qT", 0, 4 * K, BF16, "p (h t) -> p h t", h=4)
    catT = ovl[:, 2 * K:6 * K].rearrange("p (k t) -> p k t", k=8)
    catT_bh = [ov("catT_h%d" % h, 4 * K + h * K, K, BF16)[1] for h in range(4)]
    catT_sc = ov("catT_sc", 8 * K, 4 * K, BF16)[1]
    cg32, cg32_b = ov("cg32", 12 * K, 8 * K, F32, "p (c t) -> p c t", c=4)
    ubuf, ubuf_b = ov("ubuf", 20 * K, 8 * K + 64, F32)
    kst, kst_b = ov("kst", 29 * K, 8 * K, F32, "p (j c) -> p j c", j=4)
    vst, vst_b = ov("vst", 37 * K, 8 * K, F32, "p (j c) -> p j c", j=4)
    q32x, q32x_b = ov("q32x", 49 * K, 8 * K, F32, "p (j c) -> p j c", j=4)
    qb16x, qb16x_b = ov("qb16x", 57 * K, 4 * K, BF16, "p (j c) -> p j c", j=4)
    kb16x, kb16x_b = ov("kb16x", 61 * K, 4 * K, BF16, "p (j c) -> p j c", j=4)
    rtmp, rtmp_b = ov("rtmp", 65 * K, 4 * K, F32, "p (a j b c) -> p a j b c", a=4, j=4, b=8)
    ptA = []
    for i in range(8):
        ptA.append(ov("pt%d" % i, 49 * K + i * K, K, BF16))
    ptP = []
    for i in range(4):
        ptP.append(ov("ptp%d" % i, 49 * K + 2 * i * K, 2 * K, BF16, "p (m q) -> p m q", m=2))
    t1, t1_b = ov("t1", 57 * K, 2 * K, F32, "p (q e) -> p q e", q=4)
    o32, o32_b = ov("o32", 59 * K, 2 * K, F32, "p (q e) -> p q e", q=4)
    otm, otm_b = ov("otm", 61 * K, K, BF16, "p (q e) -> p q e", q=4)
    ckst, ckst_b = ov("ckst", 29 * K, 16 * K, F32, "p (j c) -> p j c", j=8)
    ckb, ckb_b = ov("ckb", 65 * K, 4 * K, BF16, "p (j c) -> p j c", j=4)
    ckst2, ckst2_b = ov("ckst2", 12 * K, 16 * K, F32, "p (j c) -> p j c", j=8)
    hqT, hqT_b = ov("hqT", 0, 8 * K, BF16, "p (k t) -> p k t", k=8)
    xoT, xoT_b = ov("xoT", 8 * K, 8 * K, BF16, "p (k t) -> p k t", k=8)
    ptB = []
    for i in range(4):
        ptB.append(ov("ptB%d" % i, 16 * K + i * K, K, BF16))
    rdens = []
    for i in range(2):
        rdens.append(ov("rden%d" % i, 20 * K + i * 2 * K, 2 * K, F32))
    mld, mld_b = ov("mld", 24 * K, 8 * K, F32, "p (j c) -> p j c", j=2)
    mst, mst_b = ov("mst", 32 * K, 8 * K, F32, "p (j c) -> p j c", j=2)
    mb16, mb16_b = ov("mb16", 40 * K, 4 * K, BF16, "p (j c) -> p j c", j=2)
    mT, mT_b = ov("mT", 44 * K, 4 * K, BF16, "p (k t) -> p k t", k=8)
    mkb, mkb_b = ov("mkb", 48 * K, 2 * K, BF16)
    gT = ovl[:, 0:11 * K].rearrange("p (k t) -> p k t", k=NFC)
    gT_b3 = [ov("gT%d" % i, i * 8 * K, (8 if i < 2 else 6) * K, BF16)[1] for i in range(3)]
    upb = []
    for i in range(2):
        upb.append(ov("upb%d" % i, 22 * K + i * (2 * K + 64), 2 * K + 64, F32))
    cvb = []
    for i in range(2):
        cvb.append(ov("cvb%d" % i, 27 * K + i * 2 * K, 2 * K, F32))
    slb = []
    for i in range(2):
        slb.append(ov("slb%d" % i, 31 * K + i * 2 * K, 2 * K, F32))
    for (b, lo, hi) in ovl_bufs:
        b.al = [b2 for (b2, lo2, hi2) in ovl_bufs if lo2 < hi and lo < hi2]

    pacc = es.enter_context(nc.psum_tensor("pacc", [128, 2048], F32))
    pb = [pacc[:, i * 512:(i + 1) * 512] for i in range(4)]
    pst = es.enter_context(nc.psum_tensor("pst", [128, 2048], F32))
    pb += [pst[:, i * 512:(i + 1) * 512] for i in range(4)]
    pb_b = [Buf("pb%d" % i, excl=True) for i in range(8)]

    class Rot:
        def __init__(self, ids):
            self.ids = ids
            self.i = 0

        def nxt(self):
            r = self.ids[self.i % len(self.ids)]
            self.i += 1
            return r

    cb = [const_b]
    P.dma(ident32[:, :], ident_d[:, :], (), cb, const_b)
    P.dma(cosT[:, :, :], cos_d[:, :, :], (), cb, const_b)
    P.dma(sinT[:, :, :], sin_d[:, :, :], (), cb, const_b)
    P.dma(cmask[:, :], cmask_d[:, :], (), cb, const_b)
    for i, g in enumerate((g_mix, g_mem, g_x, g_ffn)):
        P.dma(gcol[:, i, :], g[0, :].rearrange("(k p) -> p k", p=128), (), cb, const_b)
    P.dma(gsub1[:, :], g_sub[0, :].rearrange("(p o) -> p o", o=1), (), cb, const_b)
    P.dma(gfin[:, :, :], g_final[0:1, :].partition_broadcast(128), (), cb, const_b)
    for i, l in enumerate((lq1, lk1, lq2, lk2)):
        P.dma(lamv[:, i:i + 1, :], l[0:1, :].partition_broadcast(128), (), cb, const_b)
    for j_ in range(3):
        P.dma(wsc[:, :, j_], w_sc[j_, :].rearrange("(c p) -> p c", p=128), (), cb, const_b)
        P.dma(wfc[:, :, j_], w_fc[j_, :].rearrange("(c p) -> p c", p=128), (), cb, const_b)

    def dma_state_out(dram2, sb3, rd, key):
        for r_ in range(2):
            P.dma(dram2[r_, :].rearrange("(c p) -> p c", p=128), sb3[:, :, r_], rd, (), key)

    def dma_state_in(sb3, dram2, wr, key):
        for r_ in range(2):
            P.dma(sb3[:, :, r_], dram2[r_, :].rearrange("(c p) -> p c", p=128), (), wr, key)
    P.cp('dve', ident[:, :], ident32[:, :], cb, [ident_b])
    P.memset('pool', ones[:, :], 1.0, [ident_b])
    P.memset('pool', epsc[:, :], EPS, cb)
    P.memset('pool', V1[:, :, :, :], 1.0, V1_b)
    P.memset('dve', gcol[:, 4, :], 1.0, cb)
    P.ts('dve', gcol[:, 4, 0:4], gcol[:, 4, 0:4], gsub1[:, 0:1], 1.0 - LAM_INIT, ALU.mult, ALU.mult, cb, cb)
    P.tt('dve', lamt[:, 0, :], lamv[:, 0, :], lamv[:, 1, :], ALU.mult, cb, cb)
    P.tt('dve', lamt[:, 1, :], lamv[:, 2, :], lamv[:, 3, :], ALU.mult, cb, cb)
    P.add('dve', lambda e: e.reduce_sum(lams[:, 0:2], lamt[:, :, :], mybir.AxisListType.X), cb, cb)
    P.act(lams[:, 0:2], lams[:, 0:2], AF.Exp, cb, cb)
    P.tt('dve', lams[:, 2:3], lams[:, 0:1], lams[:, 1:2], ALU.subtract, cb, cb)
    P.ts('dve', lams[:, 3:4], lams[:, 2:3], LAM_INIT, -1.0, ALU.add, ALU.mult, cb, cb)
    neg_lam = lams[:, 3:4]

    converted = set()
    wstate = {'slot': 0, 'cst': 0, 'ptr': 0, 'issued': 0}
    wseq = []
    gidx = {'in': 0, 'xk': 1, 'xv': 1, 'xq': 2, 'ug': 3, 'out': 4}

    def wsrc(nm, b, k):
        r0 = k * 128
        if nm == 'ug':
            return [(0, 256, w_up[r0:r0 + 128, b * 256:(b + 1) * 256]),
                    (256, 256, w_gate[r0:r0 + 128, b * 256:(b + 1) * 256])]
        if nm == 'dn':
            c, kb = b // 3, b % 3
            r0 = kb * 1024 + k * 128
            return [(0, 512, w_down[r0:r0 + 128, c * 512:(c + 1) * 512])]
        w = {'in': w_in, 'out': w_out, 'xq': w_xq, 'xk': w_xk, 'xv': w_xv, 'xo': w_xo}[nm]
        return [(0, 512, w[r0:r0 + 128, b * 512:(b + 1) * 512])]

    def wissue():
        i = wstate['issued']
        if i >= len(wseq):
            return
        nm, b = wseq[i]
        wstate['issued'] += 1
        s = i % 4
        slot, sbuf_ = ring_t[s], ring_b[s]
        if (nm, b) in converted:
            P.dma(slot[:, :, :], scr[nm][b].rearrange("p (k c) -> p k c", k=8), [scrb[nm]], [sbuf_], sbuf_)
        else:
            converted.add((nm, b))
            nk = 6 if (nm == 'dn' and b % 3 == 2) else 8
            for k in range(nk):
                ci = wstate['cst'] % 4
                wstate['cst'] += 1
                for (c0, ncol, src) in wsrc(nm, b, k):
                    P.dma(cst_t[ci][:, c0:c0 + ncol], src, (), [cst_b[ci]], cst_b[ci])
                dst = slot[:, k, :]
                if nm in gidx:
                    gs = gcol[:, gidx[nm], k:k + 1]
                    if k % 2:
                        P.act(dst, cst_t[ci][:, :], AF.Copy, [cst_b[ci], const_b], [sbuf_], scale=gs)
                    else:
                        P.ts('dve', dst, cst_t[ci][:, :], gs, None, ALU.mult, None, [cst_b[ci], const_b], [sbuf_])
                else:
                    P.cp('act' if k % 2 else 'dve', dst, cst_t[ci][:, :], [cst_b[ci]], [sbuf_])
            P.dma(scr[nm][b].rearrange("p (k c) -> p k c", k=8), slot[:, :, :], [sbuf_], [scrb[nm]], sbuf_, q='pool')

    def wget(nm, b, held=0):
        i = wstate['ptr']
        assert wseq[i] == (nm, b), (wseq[i], nm, b)
        wstate['ptr'] += 1
        while wstate['issued'] < min(len(wseq), i + 4 - held):
            wissue()
        s = i % 4
        return ring_t[s], ring_b[s]

    class Tile:
        pass

    tiles = []
    for s in range(NSEQ):
        for t in range(SEQ // TT):
            tl = Tile()
            tl.kind = 'p'; tl.s = s; tl.t = t
            tl.NS = 4; tl.PT = 128; tl.NT = 512; tl.G = 1; tl.L = 512
            tl.first = (t == 0); tl.last = (t == SEQ // TT - 1)
            tiles.append(tl)
    tl = Tile()
    tl.kind = 's'; tl.s = 0; tl.t = 0
    tl.NS = 1; tl.PT = 64; tl.NT = 64; tl.G = 4; tl.L = 16
    tl.first = True; tl.last = True
    if phases < 50:
        tiles = tiles[:phases]
    tiles = [tl] + tiles

    for tl in tiles:
        if tl.kind == 'p' and tl.first:
            wseq += [('xk', 0), ('xk', 1), ('xv', 0), ('xv', 1)]
        wseq += [('in', 0), ('in', 1), ('in', 2), ('in', 4), ('in', 5), ('in', 3)]
        wseq += [('out', 0), ('out', 1), ('xq', 0), ('xq', 1), ('xo', 0), ('xo', 1)]
        wseq += [('ug', i) for i in range(11)]
        wseq += [('dn', i) for i in range(6)]

    mmrot = Rot([0, 1, 2, 3, 6, 7])
    tprot = Rot([4, 5])
    strot = Rot([4, 5, 6, 7])
    hbrot = Rot([0, 1])
    statrot = Rot(list(range(8)))

    def newstat():
        i = statrot.nxt()
        return stat[:, i * 16:(i + 1) * 16], stat_b[i]

    def rstd_from_ss(ssap, n, sbuf_, inv_n):
        P.act(ssap, ssap, AF.Ln, [sbuf_, const_b], [sbuf_], bias=epsc[0:ssap.shape[0], 0:1], scale=inv_n)
        P.act(ssap, ssap, AF.Exp, [sbuf_], [sbuf_], scale=-0.5)

    def sq_group(xt, PT, NS, ss, xb, ssb):
        def fn(e):
            ins = None
            for j in range(NS):
                ins = e.activation(junk[0:PT, j, :], xt[0:PT, j, :], AF.Square, accum_out=ss[0:PT, j:j + 1], saturate=False)
            return ins
        P.add('act', fn, [xb], [ssb])

    def norm_to_hT(tl, xt, xb):
        PT, NS = tl.PT, tl.NS
        for j in range(NS):
            ss, ssb = newstat()
            P.act(junk[0:PT, j, :], xt[0:PT, j, :], AF.Square, [xb[j]], [ssb], accum_out=ss[0:PT, 0:1], saturate=False)
            rstd_from_ss(ss[0:PT, 0:1], 1, ssb, 1.0 / D)
            hi = hbrot.nxt()
            P.act(hb_t[hi][0:PT, :], xt[0:PT, j, :], AF.Copy, [xb[j], ssb], [hb_b[hi]], scale=ss[0:PT, 0:1])
            bk = tprot.nxt()
            pv = pb[bk].bitcast(BF16).rearrange("p (k t) -> p k t", k=8)
            P.tr([(pv[:, k, 0:PT], hb_t[hi][0:PT, k * 128:(k + 1) * 128], PT) for k in range(8)],
                 ident, [hb_b[hi], ident_b], [pb_b[bk]])
            P.cp('dve', hT[:, :, j * PT:(j + 1) * PT], pv[:, :, 0:PT], [pb_b[bk]], [hT_bj[j]])

    def proj_out(tl, xt, xb):
        Ws = [wget('out', 0), wget('out', 1, held=1)]
        korder = [4, 5, 6, 7, 0, 1, 2, 3]
        groups = [(j, c) for j in range(tl.NS) for c in range(2)]
        early = groups[:6]
        pend = []
        early_tok = [catT_sc, catT_bh[0], catT_bh[1], catT_bh[2]]

        def finish(j, c, bk):
            W, Wb = Ws[c]
            o = pb[bk][0:tl.PT, :]
            P.mm([(o, catT[:, 3, j * tl.PT:(j + 1) * tl.PT], W[:, 3, :], False, True)], [catT_bh[3], Wb], [pb_b[bk]])
            xsl = xt[0:tl.PT, j, c * 512:(c + 1) * 512]
            P.tt('dve', xsl, o, xsl, ALU.add, [pb_b[bk], xb[j]], [xb[j]])

        for gi, (j, c) in enumerate(groups):
            if gi == len(early):
                for (j2, c2, bk2) in pend:
                    finish(j2, c2, bk2)
                pend = []
            W, Wb = Ws[c]
            bk = mmrot.nxt()
            o = pb[bk][0:tl.PT, :]
            P.mm([(o, catT[:, k, j * tl.PT:(j + 1) * tl.PT], W[:, k, :], k == 4, False) for k in korder[:7]],
                 early_tok + [Wb], [pb_b[bk]])
            if gi < len(early):
                pend.append((j, c, bk))
            else:
                finish(j, c, bk)
        for (j2, c2, bk2) in pend:
            finish(j2, c2, bk2)

    def proj_resid(tl, nm, src, srcb, xt, xb):
        Ws = [wget(nm, 0), wget(nm, 1, held=1)]
        for j in range(tl.NS):
            for c in range(2):
                W, Wb = Ws[c]
                bk = mmrot.nxt()
                o = pb[bk][0:tl.PT, :]
                P.mm([(o, src[:, k, j * tl.PT:(j + 1) * tl.PT], W[:, k, :], k == 0, k == 7) for k in range(8)],
                     [srcb, Wb], [pb_b[bk]])
                xsl = xt[0:tl.PT, j, c * 512:(c + 1) * 512]
                P.tt('dve', xsl, o, xsl, ALU.add, [pb_b[bk], xb[j]], [xb[j]])

    def proj_tokmajor(tl, W, Wb, src, srcb, evac):
        for j in range(tl.NS):
            bk = mmrot.nxt()
            o = pb[bk][0:tl.PT, :]
            sbj = [srcb[j]] if isinstance(srcb, list) else [srcb]
            P.mm([(o, src[:, k, j * tl.PT:(j + 1) * tl.PT], W[:, k, :], k == 0, k == 7) for k in range(8)],
                 sbj + [Wb], [pb_b[bk]])
            evac(j, o, pb_b[bk])

    def proj_featmajor(tl, W, Wb, c0, nchunk, src, srcb, evac, rot=None):
        rot = rot or mmrot
        for c in range(nchunk):
            bk = rot.nxt()
            o = pb[bk][:, 0:tl.NT]
            sbl = list(srcb) if isinstance(srcb, list) else [srcb]
            P.mm([(o, W[:, k, c0 + c * 128:c0 + (c + 1) * 128], src[:, k, 0:tl.NT], k == 0, k == 7) for k in range(8)],
                 sbl + [Wb], [pb_b[bk]])
            evac(c, o, pb_b[bk])

    def rope(tl, buf4, bufb, jj0):
        PT, NS = tl.PT, tl.NS
        x1 = buf4[0:PT, 0:NS, :, 0:8]
        x2 = buf4[0:PT, 0:NS, :, 8:16]
        c = cosT[0:PT, jj0:jj0 + NS, :].unsqueeze(2).to_broadcast([PT, NS, 8, 8])
        s = sinT[0:PT, jj0:jj0 + NS, :].unsqueeze(2).to_broadcast([PT, NS, 8, 8])
        ta, tb_, tc, td = (rtmp[0:PT, i, 0:NS, :, :] for i in range(4))
        rw = [bufb, rtmp_b, const_b]
        P.tt('dve', ta, x1, c, ALU.mult, rw, [rtmp_b])
        P.tt('dve', tb_, x2, s, ALU.mult, rw, [rtmp_b])
        P.tt('dve', tc, x2, c, ALU.mult, rw, [rtmp_b])
        P.tt('dve', td, x1, s, ALU.mult, rw, [rtmp_b])
        P.tt('dve', x1, ta, tb_, ALU.subtract, [rtmp_b], [bufb])
        P.tt('dve', x2, tc, td, ALU.add, [rtmp_b], [bufb])

    def conv3(eng, out, u3, wv, c, G, L, rd, wr):
        P.ts('dve', out, u3[:, :, 0:L], wv[:, c, 0:1], None, ALU.mult, None, rd, wr)
        P.stt('dve', out, u3[:, :, 1:L + 1], wv[:, c, 1:2], out, ALU.mult, ALU.add, rd + wr, wr)
        P.stt('dve', out, u3[:, :, 2:L + 2], wv[:, c, 2:3], out, ALU.mult, ALU.add, rd + wr, wr)

    def mem_phase(tl):
        s = tl.s
        P.dma(mld[:, :, :], memp[s].rearrange("(j p) d -> p j d", p=128), (), [mld_b], mld_b)
        ss, ssb = newstat()
        sq_group(mld, 128, 2, ss, mld_b, ssb)
        rstd_from_ss(ss[:, 0:2], 2, ssb, 1.0 / D)
        for j in range(2):
            P.act(mb16[:, j, :], mld[:, j, :], AF.Copy, [mld_b, ssb], [mb16_b], scale=ss[:, j:j + 1])
            bk = tprot.nxt()
            pv = pb[bk].bitcast(BF16).rearrange("p (k t) -> p k t", k=8)
            P.tr([(pv[:, k, :], mb16[:, j, k * 128:(k + 1) * 128], 128) for k in range(8)],
                 ident, [mb16_b, ident_b], [pb_b[bk]])
            P.cp('dve', mT[:, :, j * 128:(j + 1) * 128], pv, [pb_b[bk]], [mT_b])
        for which, outd in (('xk', mk_p), ('xv', mv_p)):
            for c in range(2):
                W, Wb = wget(which, c)
                for j in range(2):
                    bk = mmrot.nxt()
                    o = pb[bk][:, :]
                    P.mm([(o, mT[:, k, j * 128:(j + 1) * 128], W[:, k, :], k == 0, k == 7) for k in range(8)],
                         [mT_b, Wb], [pb_b[bk]])
                    P.cp('act', mst[:, j, c * 512:(c + 1) * 512], o, [pb_b[bk]], [mst_b])
                    if which == 'xv':
                        P.cp('dve', MV[:, j, c * 512:(c + 1) * 512], o, [pb_b[bk]], [MV_b])
                    else:
                        P.cp('dve', mkb[:, 0:512], o, [pb_b[bk]], [mkb_b])
                        tk = tprot.nxt()
                        pv = pb[tk].bitcast(BF16).rearrange("p (k t) -> p k t", k=8)
                        P.tr([(pv[:, k, :], mkb[:, k * 128:(k + 1) * 128], 128) for k in range(4)],
                             ident, [mkb_b, ident_b], [pb_b[tk]])
                        P.cp('act', MKT[:, c * 4:(c + 1) * 4, j * 128:(j + 1) * 128], pv[:, 0:4, :], [pb_b[tk]], [MKT_b])
            P.dma(outd[s * NMEM:(s + 1) * NMEM, :].rearrange("(j p) d -> p j d", p=128), mst[:, :, :], [mst_b], (), mst_b, q='pool')

    def mem_sample_load(s):
        P.dma(mld[:, :, :], cmk[s].rearrange("(j p) d -> p j d", p=128), (), [mld_b], mld_b)
        P.dma(mst[:, :, :], cmv[s].rearrange("(j p) d -> p j d", p=128), (), [mst_b], mst_b)

    def mem_sample(s):
        for j in range(2):
            P.cp('dve', mb16[:, j, :], mld[:, j, :], [mld_b], [mb16_b])
            bk = tprot.nxt()
            pv = pb[bk].bitcast(BF16).rearrange("p (k t) -> p k t", k=8)
            P.tr([(pv[:, k, :], mb16[:, j, k * 128:(k + 1) * 128], 128) for k in range(8)],
                 ident, [mb16_b, ident_b], [pb_b[bk]])
            P.cp('act', MKT[:, :, j * 128:(j + 1) * 128], pv, [pb_b[bk]], [MKT_b])
        for j in range(2):
            P.cp('act', MV[:, j, :], mst[:, j, :], [mst_b], [MV_b])
        if s + 1 < NSEQ:
            mem_sample_load(s + 1)

    def phase_A(tl, xt, xb):
        PT, NS, NT, G, L = tl.PT, tl.NS, tl.NT, tl.G, tl.L
        prompt = tl.kind == 'p'
        tq = tl.t if prompt else 0
        kcol0 = tl.t * TT if prompt else PAST
        jj0 = tl.t * 4 if prompt else 16
        W, Wb = wget('in', 0)

        def evq(j, o, ob):
            P.cp('act', q32x[0:PT, j, :], o, [ob], [q32x_b])
        proj_tokmajor(tl, W, Wb, hT, hT_b, evq)
        W, Wb = wget('in', 1)

        def evk(j, o, ob):
            P.cp('act', kst[0:PT, j, :], o, [ob], [kst_b])
        proj_tokmajor(tl, W, Wb, hT, hT_b, evk)
        rope(tl, q32x.rearrange("p j (a d) -> p j a d", a=8), q32x_b, jj0)
        P.cp('dve', qb16x[0:PT, 0:NS, :], q32x[0:PT, 0:NS, :], [q32x_b], [qb16x_b])
        rope(tl, kst.rearrange("p j (a d) -> p j a d", a=8), kst_b, jj0)
        P.cp('dve', kb16x[0:PT, 0:NS, :], kst[0:PT, 0:NS, :], [kst_b], [kb16x_b])
        if prompt:
            r0 = tl.s * SEQ + tl.t * TT
            P.dma(k_p[r0:r0 + TT, :].rearrange("(j p) c -> p j c", p=128), kst[:, :, :], [kst_b], (), kst_b, q='pool')
        else:
            P.dma(k_s[:, :], kst[0:64, 0, :], [kst_b], (), kst_b)
        W, Wb = wget('in', 2)

        def evv(j, o, ob):
            P.cp('act', vst[0:PT, j, :], o, [ob], [vst_b])
            kt = tl.t * 4 + j if prompt else 8
            P.cp('dve', V1[0:PT, kt, :, 0:128], vst[0:PT, j, :].rearrange("p (h e) -> p h e", h=4), [vst_b], [V1_b[tq if prompt else 2]])
        proj_tokmajor(tl, W, Wb, hT, hT_b, evv)
        if prompt:
            P.dma(v_p[r0:r0 + TT, :].rearrange("(j p) c -> p j c", p=128), vst[:, :, :], [vst_b], (), vst_b, q='pool')
        else:
            P.dma(v_s[:, :], vst[0:64, 0, :], [vst_b], (), vst_b)
        u4 = ubuf[:, 0:4 * G * (L + 2)].rearrange("p (c g l) -> p c g l", c=4, g=G)
        if prompt:
            if tl.first:
                P.memset('pool', u4[:, :, 0, 0:2], 0.0, [ubuf_b])
            else:
                P.cp('pool', u4[:, :, 0, 0:2], uhalo[:, :, :], [uhalo_b], [ubuf_b])
        else:
            for s in range(NSEQ):
                dma_state_in(u4[:, :, s, 0:2], ssc[s], [ubuf_b], ubuf_b)
        W, Wb = wget('in', 4)

        def evcg(c, o, ob):
            P.cp('act', cg32[:, c, 0:NT], o, [ob], [cg32_b])
        proj_featmajor(tl, W, Wb, 0, 4, hT, hT_b, evcg)
        for j in range(NS):
            bk = tprot.nxt()
            pv = pb[bk].bitcast(BF16).rearrange("p (k t) -> p k t", k=8)
            P.tr([(pv[:, h, 0:PT], qb16x[0:PT, j, h * 128:(h + 1) * 128], PT) for h in range(4)],
                 ident, [qb16x_b, ident_b], [pb_b[bk]])
            P.cp('dve', qT[:, :, j * PT:(j + 1) * PT], pv[:, 0:4, 0:PT], [pb_b[bk]], [qT_b])
        W, Wb = wget('in', 5)

        def evxh(c, o, ob):
            P.tt('dve', u4[:, c, :, 2:L + 2], cg32[:, c, 0:NT].rearrange("p (g l) -> p g l", g=G),
                 o.rearrange("p (g l) -> p g l", g=G), ALU.mult, [ob, cg32_b], [ubuf_b])
            conv3('pool', cg32[:, c, 0:NT].rearrange("p (g l) -> p g l", g=G), u4[:, c, :, :], wsc, c, G, L,
                  [ubuf_b, const_b], [cg32_b])
        proj_featmajor(tl, W, Wb, 0, 4, hT, hT_b, evxh)
        for j in range(NS):
            bk = tprot.nxt()
            pv = pb[bk].bitcast(BF16).rearrange("p (k t) -> p k t", k=8)
            P.tr([(pv[:, h, 0:PT], kb16x[0:PT, j, h * 128:(h + 1) * 128], PT) for h in range(4)],
                 ident, [kb16x_b, ident_b], [pb_b[bk]])
            P.cp('dve', KT[:, :, kcol0 + j * PT:kcol0 + (j + 1) * PT], pv[:, 0:4, 0:PT], [pb_b[bk]], [KT_b[tq]])
        if prompt:
            if tl.last:
                dma_state_out(sc_p[tl.s], u4[:, :, 0, L:L + 2], [ubuf_b], ubuf_b)
            else:
                P.cp('pool', uhalo[:, :, :], u4[:, :, 0, L:L + 2], [ubuf_b], [uhalo_b])
        else:
            for s in range(NSEQ):
                dma_state_out(sc_s[s], u4[:, :, s, L:L + 2], [ubuf_b], ubuf_b)
        W, Wb = wget('in', 3)

        def evbg(c, o, ob):
            P.tt('dve', catT[:, 4 + c, 0:NT], cg32[:, c, 0:NT], o, ALU.mult, [ob, cg32_b], [catT_sc])
        proj_featmajor(tl, W, Wb, 0, 4, hT, hT_b, evbg)
        P.tag = 'A_attn'
        if prompt:
            attn_prompt(tl)
        else:
            attn_sample(tl)
        P.tag = 'A_out'
        proj_out(tl, xt, xb)

    def attn_post_parts(h):
        A = pacc[:, :].rearrange("p (q c) -> p q c", q=4)[:, :, 0:258].rearrange("p q (m e) -> p q m e", m=2)
        accb = [pb_b[i] for i in range(4)]
        st, stb = newstat()
        rec = st[:, 0:8].rearrange("p (q m) -> p q m", q=4)
        rl = st[:, 8:12]
        rs = st[:, 12:16]

        def part1():
            P.add('dve', lambda e: e.reciprocal(rec, A[:, :, :, 128]), accb, [stb])
            P.ts('dve', rl, rec[:, :, 1], neg_lam, None, ALU.mult, None, [stb, const_b], [stb])
            P.tt('dve', t1[:, :, :], A[:, :, 0, 0:128], rec[:, :, 0:1].to_broadcast([128, 4, 128]), ALU.mult, accb + [stb], [t1_b])
            P.tt('dve', o32[:, :, :], A[:, :, 1, 0:128], rl.unsqueeze(2).to_broadcast([128, 4, 128]), ALU.mult, accb + [stb], [o32_b])
            P.tt('dve', o32[:, :, :], o32[:, :, :], t1[:, :, :], ALU.add, [t1_b, o32_b], [o32_b])

        def part2(bk):
            def fn(e):
                ins = None
                for q in range(4):
                    ins = e.activation(junk[:, q, 0:128], o32[:, q, :], AF.Square, accum_out=rs[:, q:q + 1], saturate=False)
                return ins
            P.add('act', fn, [o32_b], [stb])
            rstd_from_ss(rs, 4, stb, 1.0 / 128)

        def part3(bk):
            P.tt('dve', otm[:, :, :], o32[:, :, :], rs.unsqueeze(2).to_broadcast([128, 4, 128]), ALU.mult, [o32_b, stb], [otm_b])
            pv = pb[bk].bitcast(BF16).rearrange("p (k t) -> p k t", k=8)
            P.tr([(pv[:, q, :], otm[:, q, :], 128) for q in range(4)], ident, [otm_b, ident_b], [pb_b[bk]])
            P.cp('dve', catT[:, h, 0:512].rearrange("p (q t) -> p q t", q=4), pv[:, 0:4, :], [pb_b[bk]], [catT_bh[h]])
        return part1, part2, part3

    def attn_prompt(tl):
        t = tl.t
        nkt = t * 4 + 4
        steps = [(h, kt) for h in range(4) for kt in range(nkt)]
        stq = {}
        pti = [0]

        def emit_st(i):
            h, kt = steps[i]
            r = kt - t * 4
            q0 = max(0, r) * 128
            N = 512 - q0
            lst = []
            for m in range(2):
                bk = 4 + 2 * (i % 2) + m
                o = pb[bk][:, 0:N]
                P.mm([(o, KT[m * 64:(m + 1) * 64, h, kt * 128:(kt + 1) * 128], qT[m * 64:(m + 1) * 64, h, q0:512], True, True)],
                     [KT_b[kt // 4], qT_b], [pb_b[bk]])
                lst.append((bk, o))
            stq[i] = (lst, r, q0, N)

        def emit_exp_pv(i):
            h, kt = steps[i]
            lst, r, q0, N = stq.pop(i)
            par = i % 2
            pp, ppb = ptP[pti[0] % 4]
            pti[0] += 1
            stv = pst[:, par * 1024:(par + 1) * 1024].rearrange("p (m q) -> p m q", m=2)
            stb = [pb_b[4 + 2 * par], pb_b[5 + 2 * par]]
            P.act(pp[:, :, q0:512], stv[:, :, 0:N], AF.Exp, stb, [ppb], scale=0.125)
            if r >= 0:
                P.memset('dve', pp[64:128, :, q0:q0 + 64], 0.0, [ppb])
            for qb in range(max(0, r), 4):
                items = []
                for m in range(2):
                    A = pb[qb][:, 0:258].rearrange("p (m e) -> p m e", m=2)
                    items.append((A[:, m, :], pp[:, m, qb * 128:(qb + 1) * 128], V1[:, kt, h, 0:129],
                                  kt == 0 and m == 0, kt == t * 4 + qb, True))
                P.mm(items, [ppb, V1_b[kt // 4]], [pb_b[qb]])

        deferred = []
        emit_st(0)
        for i, (h, kt) in enumerate(steps):
            if i + 1 < len(steps):
                emit_st(i + 1)
            emit_exp_pv(i)
            for d in [d for d in deferred if d[0] <= i]:
                d[1](4 + 2 * (i % 2))
                deferred.remove(d)
            if kt == nkt - 1:
                p1, p2, p3 = attn_post_parts(h)
                p1()
                deferred.append((i + 2, p2))
                deferred.append((i + 4, p3))
        for d in deferred:
            d[1](4)

    def attn_sample(tl):
        ptz, ptz_b = ptA[0][0], ptA[0][1]
        ptn, ptn_b = ptA[2][0], ptA[2][1]
        P.dma(ckst[:, :, :], ck[0].rearrange("(j p) c -> p j c", p=128), (), [ckst_b], ckst_b)
        for s in range(NSEQ):
            P.dma(ckst2[:, :, :], cv[s].rearrange("(j p) c -> p j c", p=128), (), [ckst2_b], ckst2_b)
            for half in range(2):
                P.cp('dve', ckb[:, :, :], ckst[:, half * 4:(half + 1) * 4, :], [ckst_b], [ckb_b])
                for j4 in range(4):
                    j = half * 4 + j4
                    bk = tprot.nxt()
                    pv = pb[bk].bitcast(BF16).rearrange("p (k t) -> p k t", k=8)
                    P.tr([(pv[:, h, :], ckb[:, j4, h * 128:(h + 1) * 128], 128) for h in range(4)],
                         ident, [ckb_b, ident_b], [pb_b[bk]])
                    P.cp('dve', KT[:, :, j * 128:(j + 1) * 128], pv[:, 0:4, :], [pb_b[bk]], [KT_b[0]])
            if s + 1 < NSEQ:
                P.dma(ckst[:, :, :], ck[s + 1].rearrange("(j p) c -> p j c", p=128), (), [ckst_b], ckst_b)
            P.cp('act', V1[:, 0:8, :, 0:128], ckst2[:, :, :].rearrange("p j (h e) -> p j h e", h=4), [ckst2_b], [V1_b[0]])
            for h in range(4):
                pz = ptz[:, 0:256].rearrange("p (k m q) -> p k m q", k=8, m=2)
                pn = ptn[0:64, 0:32].rearrange("p (m q) -> p m q", m=2)
                for m in range(2):
                    bk = strot.nxt()
                    sv = pb[bk][:, 0:128].rearrange("p (k q) -> p k q", k=8)
                    nv = pb[bk][0:64, 128:144]
                    its = [(sv[:, kt, :], KT[m * 64:(m + 1) * 64, h, kt * 128:(kt + 1) * 128],
                            qT[m * 64:(m + 1) * 64, h, s * 16:(s + 1) * 16], True, True) for kt in range(8)]
                    its.append((nv, KT[m * 64:(m + 1) * 64, h, PAST:PAST + 64],
                                qT[m * 64:(m + 1) * 64, h, s * 16:(s + 1) * 16], True, True))
                    P.mm(its, [KT_b[0], qT_b], [pb_b[bk]])
                    P.act(pz[:, :, m, :], sv, AF.Exp, [pb_b[bk]], [ptz_b], scale=0.125)
                    P.act(pn[:, m, :], nv, AF.Exp, [pb_b[bk], const_b], [ptn_b], bias=cmask[0:64, 1 + s:2 + s], scale=0.125)
                A = pb[h][0:16, 0:258].rearrange("p (m e) -> p m e", m=2)
                items = []
                for m in range(2):
                    for kt in range(8):
                        items.append((A[:, m, :], pz[:, kt, m, 0:16], V1[:, kt, h, 0:129], kt == 0, False))
                    items.append((A[:, m, :], pn[:, m, :], V1[0:64, 8, h, 0:129], False, True))
                P.mm(items, [ptz_b, ptn_b, V1_b[0], V1_b[2]], [pb_b[h]])
            for h in range(4):
                attn_post_sample(s, h)

    def attn_post_sample(s, h):
        PT = 16
        bk = h
        A = pb[bk][0:PT, 0:258].rearrange("p (m e) -> p m e", m=2)
        st, stb = newstat()
        P.add('dve', lambda e: e.reciprocal(st[0:PT, 0:2], A[:, :, 128]), [pb_b[bk]], [stb])
        P.ts('dve', st[0:PT, 2:3], st[0:PT, 1:2], neg_lam[0:PT, :], None, ALU.mult, None, [stb, const_b], [stb])
        P.ts('dve', t1[0:PT, 0, :], A[:, 0, 0:128], st[0:PT, 0:1], None, ALU.mult, None, [pb_b[bk], stb], [t1_b])
        P.stt('dve', o32[0:PT, 0, :], A[:, 1, 0:128], st[0:PT, 2:3], t1[0:PT, 0, :], ALU.mult, ALU.add,
              [pb_b[bk], stb, t1_b], [o32_b])
        P.act(junk[0:PT, statrot.i % 4, 0:128], o32[0:PT, 0, :], AF.Square, [o32_b], [stb], accum_out=st[0:PT, 3:4], saturate=False)
        rstd_from_ss(st[0:PT, 3:4], 1, stb, 1.0 / 128)
        P.act(otm[0:PT, 0, :], o32[0:PT, 0, :], AF.Copy, [o32_b, stb], [otm_b], scale=st[0:PT, 3:4])
        tb = strot.nxt()
        pv = pb[tb].bitcast(BF16).rearrange("p (k t) -> p k t", k=8)
        P.tr([(pv[:, 0, 0:PT], otm[0:PT, 0, :], PT)], ident, [otm_b, ident_b], [pb_b[tb]])
        P.cp('dve', catT[:, h, s * 16:(s + 1) * 16], pv[:, 0, 0:PT], [pb_b[tb]], [catT_bh[h]])

    def phase_B(tl, xt, xb):
        PT, NS, NT = tl.PT, tl.NS, tl.NT
        norm_to_hT(tl, xt, xb)
        for c in range(2):
            W, Wb = wget('xq', c)

            def evq(cc, o, ob, c=c):
                P.cp('act', hqT[:, c * 4 + cc, 0:NT], o, [ob], [hqT_b])
            proj_featmajor(tl, W, Wb, 0, 4, hT, hT_b, evq)
        if tl.kind == 'p':
            cross_attn(tl, 0, NT)
        else:
            mem_sample_load(0)
            for s in range(NSEQ):
                mem_sample(s)
                cross_attn(tl, s * 16, 16)
        proj_resid(tl, 'xo', xoT, xoT_b, xt, xb)

    xorot = Rot([0, 1, 2, 3])

    cxs = {'i': 0}

    def cross_attn(tl, c0, n):
        scale = 256 ** -0.5
        for h in range(4):
            ci = cxs['i']
            cxs['i'] += 1
            pts = []
            for mt in range(2):
                bk = strot.nxt()
                o = pb[bk][:, 0:n]
                P.mm([(o, MKT[:, h * 2 + dd, mt * 128:(mt + 1) * 128], hqT[:, h * 2 + dd, c0:c0 + n], dd == 0, dd == 1)
                      for dd in range(2)], [MKT_b, hqT_b], [pb_b[bk]])
                pt, ptb = ptB[(ci % 2) * 2 + mt]
                P.act(pt[:, 0:n], o, AF.Exp, [pb_b[bk]], [ptb], scale=scale)
                pts.append((pt, ptb))
            rden, rden_b = rdens[ci % 2]
            dk = 7
            dn = pb[dk][:, 0:n]
            P.mm([(dn, ones[:, :], pts[mt][0][:, 0:n], mt == 0, mt == 1) for mt in range(2)],
                 [pts[0][1], pts[1][1], ident_b], [pb_b[dk]])
            P.add('dve', lambda e, dn=dn, rden=rden: e.reciprocal(rden[:, 0:n], dn), [pb_b[dk]], [rden_b])
            for dd in range(2):
                bk = xorot.nxt()
                o = pb[bk][:, 0:n]
                P.mm([(o, MV[:, mt, h * 256 + dd * 128:h * 256 + (dd + 1) * 128], pts[mt][0][:, 0:n], mt == 0, mt == 1)
                      for mt in range(2)], [MV_b, pts[0][1], pts[1][1]], [pb_b[bk]])
                P.tt('dve', xoT[:, h * 2 + dd, c0:c0 + n], o, rden[:, 0:n], ALU.mult, [pb_b[bk], rden_b], [xoT_b])

    ffrot = Rot([0, 1, 2, 3, 4, 5, 6, 7])

    def phase_C(tl, xt, xb):
        PT, NS, NT, G, L = tl.PT, tl.NS, tl.NT, tl.G, tl.L
        prompt = tl.kind == 'p'
        norm_to_hT(tl, xt, xb)
        if not prompt:
            for s in range(NSEQ):
                dma_state_in(sphalo[:, :, s, :], sff[s], [sphalo_b], sphalo_b)
        elif tl.first:
            P.memset('pool', uphalo[:, :, :], 0.0, [uphalo_b])
        ui = 0
        for i in range(11):
            W, Wb = wget('ug', i)
            for cc in range(2):
                ch = 2 * i + cc
                bu = ffrot.nxt()
                ou = pb[bu][:, 0:NT]
                P.mm([(ou, W[:, k, cc * 128:(cc + 1) * 128], hT[:, k, 0:NT], k == 0, k == 7) for k in range(8)],
                     hT_b + [Wb], [pb_b[bu]])
                bg_ = ffrot.nxt()
                og = pb[bg_][:, 0:NT]
                P.mm([(og, W[:, k, 256 + cc * 128:256 + (cc + 1) * 128], hT[:, k, 0:NT], k == 0, k == 7) for k in range(8)],
                     hT_b + [Wb], [pb_b[bg_]])
                ub, ubb = upb[ui % 2]
                cb_, cbb = cvb[ui % 2]
                sl, slb_ = slb[ui % 2]
                ui += 1
                u3 = ub[:, 0:G * (L + 2)].rearrange("p (g l) -> p g l", g=G)
                if prompt:
                    P.cp('pool', u3[:, 0, 0:2], uphalo[:, ch, :], [uphalo_b], [ubb])
                else:
                    P.cp('pool', u3[:, :, 0:2], sphalo[:, ch, :, :], [sphalo_b], [ubb])
                P.cp('act', u3[:, :, 2:L + 2], ou.rearrange("p (g l) -> p g l", g=G), [pb_b[bu]], [ubb])
                if prompt:
                    P.cp('pool', uphalo[:, ch, :], u3[:, 0, L:L + 2], [ubb], [uphalo_b])
                else:
                    P.cp('pool', sphalo[:, ch, :, :], u3[:, :, L:L + 2], [ubb], [sphalo_b])
                c3 = cb_[:, 0:NT].rearrange("p (g l) -> p g l", g=G)
                conv3('pool', c3, u3, wfc, ch, G, L, [ubb, const_b], [cbb])
                P.act(sl[:, 0:NT], cb_[:, 0:NT], AF.Silu, [cbb], [slb_])
                P.tt('dve', gT[:, ch, 0:NT], sl[:, 0:NT], og, ALU.mult, [pb_b[bg_], slb_], [gT_b3[ch // 8]])
        if prompt:
            if tl.last:
                dma_state_out(ff_p[tl.s], uphalo[:, :, :], [uphalo_b], uphalo_b)
        else:
            for s in range(NSEQ):
                dma_state_out(ff_s[s], sphalo[:, :, s, :], [sphalo_b], sphalo_b)
        P.tag = 'C_down'
        for c in range(2):
            banks = [c * 4 + j for j in range(NS)]
            for kb in range(3):
                W, Wb = wget('dn', c * 3 + kb)
                nk = 8 if kb < 2 else 6
                for j in range(NS):
                    o = pb[banks[j]][0:PT, :]
                    P.mm([(o, gT[:, kb * 8 + k, j * PT:(j + 1) * PT], W[:, k, :], kb == 0 and k == 0, kb == 2 and k == nk - 1)
                          for k in range(nk)], [gT_b3[kb], Wb], [pb_b[banks[j]]])
            for j in range(NS):
                xsl = xt[0:PT, j, c * 512:(c + 1) * 512]
                P.tt('dve', xsl, pb[banks[j]][0:PT, :], xsl, ALU.add, [pb_b[banks[j]], xb[j]], [xb[j]])

    def final_norm(tl, xt, xb, xtile_b):
        PT, NS = tl.PT, tl.NS
        for j in range(NS):
            ss, ssb = newstat()
            P.act(junk[0:PT, j, :], xt[0:PT, j, :], AF.Square, [xb[j]], [ssb], accum_out=ss[0:PT, 0:1], saturate=False)
            rstd_from_ss(ss[0:PT, 0:1], 1, ssb, 1.0 / D)
            P.stt('dve', xt[0:PT, j, :], xt[0:PT, j, :], ss[0:PT, 0:1], gfin[0:PT, 0, :], ALU.mult, ALU.mult,
                  [xb[j], ssb, const_b], [xb[j]])
        if tl.kind == 'p':
            r0 = tl.s * SEQ + tl.t * TT
            P.dma(y_p[r0:r0 + TT, :].rearrange("(j p) d -> p j d", p=128), xt[:, :, :], [xtile_b], (), xtile_b, q='pool')
        else:
            P.dma(y_s[:, :], xt[0:64, 0, :], [xtile_b], (), xtile_b)

    def load_x(i):
        tl = tiles[i]
        xt, xb = xs_t[i % 2], xs_b[i % 2]
        if tl.kind == 'p':
            r0 = tl.s * SEQ + tl.t * TT
            P.dma(xt[:, :, :], xp[r0:r0 + TT, :].rearrange("(j p) d -> p j d", p=128), (), [xb], xb)
        else:
            P.dma(xt[0:64, 0, :], xsm[:, :], (), [xb], xb)

    load_x(0)
    P.tag = 'A_inproj'
    norm_to_hT(tiles[0], xs_t[0], xsub_b[0])
    for i, tl in enumerate(tiles):
        xt, xb, xtb = xs_t[i % 2], xsub_b[i % 2], xs_b[i % 2]
        if i + 1 < len(tiles):
            load_x(i + 1)
        if tl.kind == 'p' and tl.first:
            P.tag = 'mem'
            mem_phase(tl)
        P.tag = 'A_inproj'
        phase_A(tl, xt, xb)
        P.tag = 'B'
        phase_B(tl, xt, xb)
        P.tag = 'C_upgate'
        phase_C(tl, xt, xb)
        if i + 1 < len(tiles):
            P.tag = 'A_inproj'
            norm_to_hT(tiles[i + 1], xs_t[(i + 1) % 2], xsub_b[(i + 1) % 2])
        final_norm(tl, xt, xb, xtb)
    assert wstate['ptr'] == len(wseq)
    P.sbuf_left = nc.sbuf_bytes_remaining
    P.build(es)
    es.close()
    return nc, P


def _consts():
    half = 8
    inv = (1.0 / (500000.0 ** (np.arange(half, dtype=np.float32) * np.float32(2.0) / np.float32(16)))).astype(np.float32)
    pos = np.zeros((17, 128), np.float32)
    for j in range(16):
        pos[j] = j * 128 + np.arange(128)
    pos[16, :16] = PAST + np.arange(16)
    pos[16, :64] = PAST + (np.arange(64) % 16)
    ang = (pos[:, :, None].astype(np.float32) * inv[None, None, :]).astype(np.float32)
    cos = np.cos(ang).astype(np.float32).transpose(1, 0, 2).copy()
    sin = np.sin(ang).astype(np.float32).transpose(1, 0, 2).copy()
    ident = np.eye(128, dtype=np.float32)
    cm = np.zeros((128, 8), np.float32)
    cm[64:, 0] = NEG
    for s in range(4):
        cm[:, 1 + s] = NEG
        cm[s * 16:(s + 1) * 16, 1 + s] = 0.0
    return cos, sin, ident, cm


_CACHE = {}


def kernel(**inputs):
    f = lambda a: np.ascontiguousarray(np.asarray(a, dtype=np.float32))
    if 'nc' not in _CACHE:
        _CACHE['nc'] = build_program()[0]
    nc = _CACHE['nc']
    cos, sin, ident, cm = _consts()
    shared = {
        "g_mix": f(inputs["g_mix"]), "g_mem": f(inputs["g_mem"]), "g_x": f(inputs["g_x"]),
        "g_ffn": f(inputs["g_ffn"]), "g_final": f(inputs["g_final"]).reshape(1, D), "g_sub": f(inputs["g_sub"]),
        "lam_q1": f(inputs["lam_q1"]), "lam_k1": f(inputs["lam_k1"]), "lam_q2": f(inputs["lam_q2"]), "lam_k2": f(inputs["lam_k2"]),
        "w_in": f(inputs["w_in"])[0], "w_out": f(inputs["w_out"])[0], "w_xq": f(inputs["w_xq"])[0],
        "w_xk": f(inputs["w_xk"])[0], "w_xv": f(inputs["w_xv"])[0], "w_xo": f(inputs["w_xo"])[0],
        "w_up": f(inputs["w_up"])[0], "w_gate": f(inputs["w_gate"])[0], "w_down": f(inputs["w_down"])[0],
        "w_sc": f(inputs["w_sc"])[0], "w_ffconv": f(inputs["w_ffconv"])[0],
        "rope_cos": cos, "rope_sin": sin, "ident": ident, "cmask": cm,
    }
    xpr = f(inputs["x_prompt"]); xsa = f(inputs["x_sample"])
    cak = f(inputs["cache_attn_k"])[0]; cav = f(inputs["cache_attn_v"])[0]
    sscv = f(inputs["state_short_conv"])[0]; sffv = f(inputs["state_ffn_conv"])[0]
    cmkv = f(inputs["cache_mem_k"])[0]; cmvv = f(inputs["cache_mem_v"])[0]; mp = f(inputs["mem_prompt"])
    in_maps = []
    for c in range(NCORES):
        sl = slice(c * NSEQ, (c + 1) * NSEQ)
        m = dict(shared)
        m["xp"] = xpr[sl].reshape(NSEQ * SEQ, D)
        m["xsm"] = xsa[sl].reshape(NSEQ * DEC, D)
        m["ck"] = cak[sl].reshape(NSEQ, PAST, 512)
        m["cv"] = cav[sl].reshape(NSEQ, PAST, 512)
        m["ssc"] = sscv[sl]
        m["sff"] = sffv[sl]
        m["cmk"] = cmkv[sl].reshape(NSEQ, NMEM, D)
        m["cmv"] = cmvv[sl].reshape(NSEQ, NMEM, D)
        m["memp"] = mp[sl]
        in_maps.append(m)
    res = run_bass_kernel_spmd(nc, in_maps, core_ids=list(range(NCORES)))
    R = res.results

    def cat(name, shape):
        return np.concatenate([np.asarray(r[name], dtype=np.float32) for r in R], axis=0).reshape(shape)

    B = NCORES * NSEQ
    return (
        cat("y_p", (B, SEQ, D)),
        cat("y_s", (B, DEC, D)),
        cat("k_p", (B, SEQ, 4, 2, 64))[None],
        cat("v_p", (B, SEQ, 4, 128))[None],
        cat("sc_p", (B, 2, 512))[None],
        cat("ff_p", (B, 2, DFF))[None],
        cat("mk_p", (B, NMEM, 4, 256))[None],
        cat("mv_p", (B, NMEM, 4, 256))[None],
        cat("k_s", (B, DEC, 4, 2, 64))[None],
        cat("v_s", (B, DEC, 4, 128))[None],
        cat("sc_s", (B, 2, 512))[None],
        cat("ff_s", (B, 2, DFF))[None],
    )
```

```python
import math
from contextlib import ExitStack
import numpy as np
import concourse.bass as bass
import concourse.mybir as mybir
from concourse.bass_utils import run_bass_kernel_spmd

F32 = mybir.dt.float32
BF16 = mybir.dt.bfloat16
AF = mybir.ActivationFunctionType
ALU = mybir.AluOpType

NCORES = 8
D = 1024
SEQ = 2048
NSEQ = 4
TT = 512
DEC = 16
PAST = 1024
DFF = 2816
NFC = 22
NMEM = 256
EPS = 1e-6
LAM_INIT = 0.8 - 0.6 * math.exp(-0.3 * 0)
NEG = -30000.0
WINDOW = 2
ENGS = ('pe', 'act', 'dve', 'pool', 'sp')


class Buf:
    __slots__ = ('name', 'lw', 'rd', 'al', 'excl')

    def __init__(self, name, excl=False):
        self.name = name
        self.excl = excl
        self.lw = None
        self.rd = {}
        self.al = [self]


class Op:
    __slots__ = ('eng', 'fn', 'key', 'eidx', 'gidx', 'deps', 'sig', 'cnt', 'dcount')


class Prog:
    def __init__(self, nc):
        self.nc = nc
        self.ops = {e: [] for e in ENGS}
        self.all = []
        self.tag = ''
        self.petags = []

    def add(self, eng, fn, rd=(), wr=(), key=None):
        op = Op()
        op.eng = eng
        op.fn = fn
        op.key = key
        op.eidx = len(self.ops[eng])
        op.gidx = len(self.all)
        op.sig = False
        op.cnt = 0
        op.dcount = 0
        deps = {}
        xw = [b for b in rd if b.excl]
        if xw:
            rd = [b for b in rd if not b.excl]
            wr = list(wr) + [b for b in xw if b not in wr]

        def dep(d):
            if d is None:
                return
            k = ('d', id(d.key), d.eng) if d.key is not None else d.eng
            o = deps.get(k)
            if o is None or o.gidx < d.gidx:
                deps[k] = d

        for b in rd:
            for a in b.al:
                dep(a.lw)
        for b in wr:
            for a in b.al:
                dep(a.lw)
                for r in a.rd.values():
                    dep(r)
        op.deps = list(deps.values())
        mk = ('d', id(key), eng) if key is not None else eng
        for b in rd:
            b.rd[mk] = op
        for b in wr:
            b.lw = op
            b.rd = {}
        self.ops[eng].append(op)
        self.all.append(op)
        return op

    def dma(self, out, in_, rd, wr, key, q='sp'):
        self.add(q, lambda e: e.dma_start(out=out, in_=in_), rd, wr, key=key)

    def mm(self, items, rd, wr):
        tg = self.tag

        def fn(e):
            ins = None
            for it in items:
                self.petags.append(tg)
                o, l, r, st, sp = it[:5]
                if len(it) > 5:
                    ins = e.matmul(o, l, r, start=st, stop=sp, skip_group_check=True)
                else:
                    ins = e.matmul(o, l, r, start=st, stop=sp)
            return ins
        self.add('pe', fn, rd, wr)

    def tr(self, items, ident, rd, wr):
        tg = self.tag

        def fn(e):
            ins = None
            for (o, i, n) in items:
                self.petags.append(tg)
                ins = e.transpose(o, i, ident[0:n, 0:n])
            return ins
        self.add('pe', fn, rd, wr)

    def act(self, out, in_, func, rd, wr, **kw):
        self.add('act', lambda e: e.activation(out, in_, func, **kw), rd, wr)

    def cp(self, eng, out, in_, rd, wr):
        if eng == 'act':
            self.add('act', lambda e: e.copy(out, in_), rd, wr)
        else:
            self.add(eng, lambda e: e.tensor_copy(out, in_), rd, wr)

    def tt(self, eng, out, a, b, op, rd, wr):
        self.add(eng, lambda e: e.tensor_tensor(out, a, b, op), rd, wr)

    def ts(self, eng, out, a, s1, s2, op0, op1, rd, wr):
        if s2 is None:
            self.add(eng, lambda e: e.tensor_scalar(out, a, s1, None, op0), rd, wr)
        else:
            self.add(eng, lambda e: e.tensor_scalar(out, a, s1, s2, op0, op1), rd, wr)

    def stt(self, eng, out, a, s, b, op0, op1, rd, wr):
        self.add(eng, lambda e: e.scalar_tensor_tensor(out, a, s, b, op0, op1), rd, wr)

    def memset(self, eng, ap, val, wr):
        self.add(eng, lambda e: e.memset(ap, val), (), wr)

    def build(self, es):
        nc = self.nc
        import os
        mx = int(os.environ.get("K_MAXOPS", "0"))
        if mx:
            self.all = [o for o in self.all if o.gidx < mx]
            for e in ENGS:
                self.ops[e] = [o for o in self.ops[e] if o.gidx < mx]
        for op in self.all:
            for d in op.deps:
                if d.key is not None:
                    continue
                if d.eng != op.eng or op.key is not None:
                    d.sig = True
                elif op.eng != 'pe' and d.eidx >= op.eidx - WINDOW:
                    d.sig = True
        esem = {}
        for e in ENGS:
            esem[e] = es.enter_context(nc.semaphore("es_" + e))
            c = 0
            for op in self.ops[e]:
                if op.key is None and op.sig:
                    c += 1
                op.cnt = c
        dsem = {}
        dcnt = {}
        for op in self.all:
            if op.key is not None:
                k = (id(op.key), op.eng)
                if k not in dsem:
                    dsem[k] = es.enter_context(nc.semaphore("ds_%d" % len(dsem)))
                    dcnt[k] = 0
                dcnt[k] += 16
                op.dcount = dcnt[k]
        self.nsem = len(dsem) + len(esem)

        def emit(ename, eng):
            waited = {}
            for op in self.ops[ename]:
                need = {}
                for d in op.deps:
                    if d.key is not None:
                        sem, v = dsem[(id(d.key), d.eng)], d.dcount
                    elif d.eng != ename or op.key is not None:
                        sem, v = esem[d.eng], d.cnt
                    elif ename != 'pe' and d.eidx >= op.eidx - WINDOW:
                        sem, v = esem[ename], d.cnt
                    else:
                        continue
                    if need.get(sem.num, (None, 0))[1] < v:
                        need[sem.num] = (sem, v)
                for num, (sem, v) in need.items():
                    if waited.get(num, 0) < v:
                        eng.wait_ge(sem, v)
                        waited[num] = v
                ins = op.fn(eng)
                if op.key is not None:
                    ins.then_inc(dsem[(id(op.key), op.eng)], 16)
                elif op.sig:
                    ins.then_inc(esem[ename], 1)
            if ename == 'sp':
                for k, sem in dsem.items():
                    if waited.get(sem.num, 0) < dcnt[k]:
                        eng.wait_ge(sem, dcnt[k])

        with nc.Block() as block:
            @block.tensor
            def _(e):
                emit('pe', e)

            @block.scalar
            def _(e):
                emit('act', e)

            @block.vector
            def _(e):
                emit('dve', e)

            @block.gpsimd
            def _(e):
                emit('pool', e)

            @block.sync
            def _(e):
                emit('sp', e)


def build_program(phases=99):
    nc = bass.Bass("TRN2", target_bir_lowering=False)
    es = ExitStack()
    es.enter_context(nc.allow_low_precision("bf16 matmul operands, fp32 accumulation"))
    es.enter_context(nc.allow_non_contiguous_dma("small constant / state layouts"))
    P = Prog(nc)

    def din(name, shape):
        return nc.dram_tensor(name, shape, F32, kind="ExternalInput").ap()

    def dout(name, shape):
        return nc.dram_tensor(name, shape, F32, kind="ExternalOutput").ap()

    xp = din("xp", [NSEQ * SEQ, D])
    xsm = din("xsm", [NSEQ * DEC, D])
    ck = din("ck", [NSEQ, PAST, 512])
    cv = din("cv", [NSEQ, PAST, 512])
    ssc = din("ssc", [NSEQ, 2, 512])
    sff = din("sff", [NSEQ, 2, DFF])
    cmk = din("cmk", [NSEQ, NMEM, D])
    cmv = din("cmv", [NSEQ, NMEM, D])
    memp = din("memp", [NSEQ, NMEM, D])
    g_mix = din("g_mix", [1, D]); g_mem = din("g_mem", [1, D]); g_x = din("g_x", [1, D])
    g_ffn = din("g_ffn", [1, D]); g_final = din("g_final", [1, D]); g_sub = din("g_sub", [1, 128])
    lq1 = din("lam_q1", [1, 64]); lk1 = din("lam_k1", [1, 64])
    lq2 = din("lam_q2", [1, 64]); lk2 = din("lam_k2", [1, 64])
    w_in = din("w_in", [D, 3072]); w_out = din("w_out", [D, D])
    w_xq = din("w_xq", [D, D]); w_xk = din("w_xk", [D, D]); w_xv = din("w_xv", [D, D]); w_xo = din("w_xo", [D, D])
    w_up = din("w_up", [D, DFF]); w_gate = din("w_gate", [D, DFF]); w_down = din("w_down", [DFF, D])
    w_sc = din("w_sc", [3, 512]); w_fc = din("w_ffconv", [3, DFF])
    cos_d = din("rope_cos", [128, 17, 8]); sin_d = din("rope_sin", [128, 17, 8])
    ident_d = din("ident", [128, 128]); cmask_d = din("cmask", [128, 8])

    y_p = dout("y_p", [NSEQ * SEQ, D]); y_s = dout("y_s", [NSEQ * DEC, D])
    k_p = dout("k_p", [NSEQ * SEQ, 512]); v_p = dout("v_p", [NSEQ * SEQ, 512])
    sc_p = dout("sc_p", [NSEQ, 2, 512]); ff_p = dout("ff_p", [NSEQ, 2, DFF])
    mk_p = dout("mk_p", [NSEQ * NMEM, D]); mv_p = dout("mv_p", [NSEQ * NMEM, D])
    k_s = dout("k_s", [NSEQ * DEC, 512]); v_s = dout("v_s", [NSEQ * DEC, 512])
    sc_s = dout("sc_s", [NSEQ, 2, 512]); ff_s = dout("ff_s", [NSEQ, 2, DFF])

    NBLK = {'in': 6, 'out': 2, 'xq': 2, 'xk': 2, 'xv': 2, 'xo': 2, 'ug': 11, 'dn': 6}
    scr = {}
    scrb = {}
    for nm, n in NBLK.items():
        scr[nm] = nc.dram_tensor("scr_" + nm, [n, 128, 8 * 512], BF16, kind="ExternalOutput").ap()
        scrb[nm] = Buf("scr_" + nm)

    def sb(name, shape, dt=F32):
        return es.enter_context(nc.sbuf_tensor("sb_" + name, shape, dt))

    xs_t = [sb("xs%d" % i, [128, 4, D]) for i in range(2)]
    xs_b = [Buf("xs%d" % i) for i in range(2)]
    xsub_b = [[Buf("xs%d_%d" % (i, j)) for j in range(4)] for i in range(2)]
    for i in range(2):
        xs_b[i].al = [xs_b[i]] + xsub_b[i]
        for j in range(4):
            xsub_b[i][j].al = [xsub_b[i][j], xs_b[i]]
    hT = sb("hT", [128, 8, TT], BF16); hT_bj = [Buf("hT%d" % j) for j in range(4)]; hT_b = hT_bj
    KT = sb("KT", [128, 4, SEQ], BF16); KT_b = [Buf("KT%d" % i) for i in range(4)]
    V1 = sb("V1", [128, 16, 4, 130], BF16); V1_b = [Buf("V1%d" % i) for i in range(4)]
    ring_t = [sb("ring%d" % i, [128, 8, 512], BF16) for i in range(4)]
    ring_b = [Buf("ring%d" % i) for i in range(4)]
    cst_t = [sb("cst%d" % i, [128, 512]) for i in range(4)]
    cst_b = [Buf("cst%d" % i) for i in range(4)]
    MKT = sb("MKT", [128, 8, NMEM], BF16); MKT_b = Buf("MKT")
    MV = sb("MV", [128, 2, D], BF16); MV_b = Buf("MV")
    ident32 = sb("ident32", [128, 128]); ident = sb("ident", [128, 128], BF16); ident_b = Buf("ident")
    ones = sb("ones", [128, 128], BF16)
    cosT = sb("cosT", [128, 17, 8]); sinT = sb("sinT", [128, 17, 8]); cmask = sb("cmask", [128, 8])
    const_b = Buf("const")
    gcol = sb("gcol", [128, 5, 8])
    gsub1 = sb("gsub1", [128, 1])
    epsc = sb("epsc", [128, 1])
    gfin = sb("gfin", [128, 1, D])
    lamv = sb("lamv", [128, 4, 64]); lamt = sb("lamt", [128, 2, 64]); lams = sb("lams", [128, 4])
    wsc = sb("wsc", [128, 4, 3]); wfc = sb("wfc", [128, NFC, 3])
    uhalo = sb("uhalo", [128, 4, 2]); uhalo_b = Buf("uhalo")
    uphalo = sb("uphalo", [128, NFC, 2]); uphalo_b = Buf("uphalo")
    sphalo = sb("sphalo", [128, NFC, 4, 2]); sphalo_b = Buf("sphalo")
    stat = sb("stat", [128, 128]); stat_b = [Buf("stat%d" % i) for i in range(8)]
    junk = sb("junk", [128, 4, D], mybir.dt.float8e4); junk_b = Buf("junk")
    hb_t = [sb("hb%d" % i, [128, D], BF16) for i in range(2)]
    hb_b = [Buf("hb%d" % i) for i in range(2)]

    OVL = 69 * 1024
    ovl = sb("ovl", [128, OVL // 2], BF16)
    ovl_bufs = []

    def ov(name, off, nbytes, dt, pattern=None, **kw):
        a = ovl[:, off // 2:(off + nbytes) // 2]
        if dt == F32:
            a = a.bitcast(F32)
        if pattern:
            a = a.rearrange(pattern, **kw)
        b = Buf(name)
        ovl_bufs.append((b, off, off + nbytes))
        return a, b

    K = 1024
    qT, qT_b = ov("qT", 0, 4 * K, BF16, "p (h t) -> p h t", h=4)
    catT, catT_b = ov("catT", 4 * K, 8 * K, BF16, "p (k t) -> p k t", k=8)
    cg32, cg32_b = ov("cg32", 12 * K, 8 * K, F32, "p (c t) -> p c t", c=4)
    ubuf, ubuf_b = ov("ubuf", 20 * K, 8 * K + 64, F32)
    kst, kst_b = ov("kst", 29 * K, 8 * K, F32, "p (j c) -> p j c", j=4)
    vst, vst_b = ov("vst", 37 * K, 8 * K, F32, "p (j c) -> p j c", j=4)
    q32x, q32x_b = ov("q32x", 49 * K, 8 * K, F32, "p (j c) -> p j c", j=4)
    qb16x, qb16x_b = ov("qb16x", 57 * K, 4 * K, BF16, "p (j c) -> p j c", j=4)
    kb16x, kb16x_b = ov("kb16x", 61 * K, 4 * K, BF16, "p (j c) -> p j c", j=4)
    rtmp, rtmp_b = ov("rtmp", 65 * K, 4 * K, F32, "p (a j b c) -> p a j b c", a=4, j=4, b=8)
    ptA = []
    for i in range(8):
        ptA.append(ov("pt%d" % i, 49 * K + i * K, K, BF16))
    ptP = []
    for i in range(4):
        ptP.append(ov("ptp%d" % i, 49 * K + 2 * i * K, 2 * K, BF16, "p (m q) -> p m q", m=2))
    t1, t1_b = ov("t1", 57 * K, 2 * K, F32, "p (q e) -> p q e", q=4)
    o32, o32_b = ov("o32", 59 * K, 2 * K, F32, "p (q e) -> p q e", q=4)
    otm, otm_b = ov("otm", 61 * K, K, BF16, "p (q e) -> p q e", q=4)
    ckst, ckst_b = ov("ckst", 29 * K, 16 * K, F32, "p (j c) -> p j c", j=8)
    ckb, ckb_b = ov("ckb", 65 * K, 4 * K, BF16, "p (j c) -> p j c", j=4)
    ckst2, ckst2_b = ov("ckst2", 12 * K, 16 * K, F32, "p (j c) -> p j c", j=8)
    hqT, hqT_b = ov("hqT", 0, 8 * K, BF16, "p (k t) -> p k t", k=8)
    xoT, xoT_b = ov("xoT", 8 * K, 8 * K, BF16, "p (k t) -> p k t", k=8)
    ptB = []
    for i in range(4):
        ptB.append(ov("ptB%d" % i, 16 * K + i * K, K, BF16))
    rdens = []
    for i in range(2):
        rdens.append(ov("rden%d" % i, 20 * K + i * 2 * K, 2 * K, F32))
    mld, mld_b = ov("mld", 36 * K, 8 * K, F32, "p (j c) -> p j c", j=2)
    mst, mst_b = ov("mst", 44 * K, 8 * K, F32, "p (j c) -> p j c", j=2)
    mb16, mb16_b = ov("mb16", 52 * K, 4 * K, BF16, "p (j c) -> p j c", j=2)
    mT, mT_b = ov("mT", 56 * K, 4 * K, BF16, "p (k t) -> p k t", k=8)
    mkb, mkb_b = ov("mkb", 60 * K, 2 * K, BF16)
    gT = ovl[:, 0:11 * K].rearrange("p (k t) -> p k t", k=NFC)
    gT_b3 = [ov("gT%d" % i, i * 8 * K, (8 if i < 2 else 6) * K, BF16)[1] for i in range(3)]
    upb = []
    for i in range(2):
        upb.append(ov("upb%d" % i, 22 * K + i * (2 * K + 64), 2 * K + 64, F32))
    cvb = []
    for i in range(2):
        cvb.append(ov("cvb%d" % i, 27 * K + i * 2 * K, 2 * K, F32))
    slb = []
    for i in range(2):
        slb.append(ov("slb%d" % i, 31 * K + i * 2 * K, 2 * K, F32))
    for (b, lo, hi) in ovl_bufs:
        b.al = [b2 for (b2, lo2, hi2) in ovl_bufs if lo2 < hi and lo < hi2]

    pacc = es.enter_context(nc.psum_tensor("pacc", [128, 2048], F32))
    pb = [pacc[:, i * 512:(i + 1) * 512] for i in range(4)]
    pst = es.enter_context(nc.psum_tensor("pst", [128, 2048], F32))
    pb += [pst[:, i * 512:(i + 1) * 512] for i in range(4)]
    pb_b = [Buf("pb%d" % i, excl=True) for i in range(8)]

    class Rot:
        def __init__(self, ids):
            self.ids = ids
            self.i = 0

        def nxt(self):
            r = self.ids[self.i % len(self.ids)]
            self.i += 1
            return r

    cb = [const_b]
    P.dma(ident32[:, :], ident_d[:, :], (), cb, const_b)
    P.dma(cosT[:, :, :], cos_d[:, :, :], (), cb, const_b)
    P.dma(sinT[:, :, :], sin_d[:, :, :], (), cb, const_b)
    P.dma(cmask[:, :], cmask_d[:, :], (), cb, const_b)
    for i, g in enumerate((g_mix, g_mem, g_x, g_ffn)):
        P.dma(gcol[:, i, :], g[0, :].rearrange("(k p) -> p k", p=128), (), cb, const_b)
    P.dma(gsub1[:, :], g_sub[0, :].rearrange("(p o) -> p o", o=1), (), cb, const_b)
    P.dma(gfin[:, :, :], g_final[0:1, :].partition_broadcast(128), (), cb, const_b)
    for i, l in enumerate((lq1, lk1, lq2, lk2)):
        P.dma(lamv[:, i:i + 1, :], l[0:1, :].partition_broadcast(128), (), cb, const_b)
    for j_ in range(3):
        P.dma(wsc[:, :, j_], w_sc[j_, :].rearrange("(c p) -> p c", p=128), (), cb, const_b)
        P.dma(wfc[:, :, j_], w_fc[j_, :].rearrange("(c p) -> p c", p=128), (), cb, const_b)

    def dma_state_out(dram2, sb3, rd, key):
        for r_ in range(2):
            P.dma(dram2[r_, :].rearrange("(c p) -> p c", p=128), sb3[:, :, r_], rd, (), key)

    def dma_state_in(sb3, dram2, wr, key):
        for r_ in range(2):
            P.dma(sb3[:, :, r_], dram2[r_, :].rearrange("(c p) -> p c", p=128), (), wr, key)
    P.cp('dve', ident[:, :], ident32[:, :], cb, [ident_b])
    P.memset('pool', ones[:, :], 1.0, [ident_b])
    P.memset('pool', epsc[:, :], EPS, cb)
    P.memset('pool', V1[:, :, :, :], 1.0, V1_b)
    P.memset('dve', gcol[:, 4, :], 1.0, cb)
    P.ts('dve', gcol[:, 4, 0:4], gcol[:, 4, 0:4], gsub1[:, 0:1], 1.0 - LAM_INIT, ALU.mult, ALU.mult, cb, cb)
    P.tt('dve', lamt[:, 0, :], lamv[:, 0, :], lamv[:, 1, :], ALU.mult, cb, cb)
    P.tt('dve', lamt[:, 1, :], lamv[:, 2, :], lamv[:, 3, :], ALU.mult, cb, cb)
    P.add('dve', lambda e: e.reduce_sum(lams[:, 0:2], lamt[:, :, :], mybir.AxisListType.X), cb, cb)
    P.act(lams[:, 0:2], lams[:, 0:2], AF.Exp, cb, cb)
    P.tt('dve', lams[:, 2:3], lams[:, 0:1], lams[:, 1:2], ALU.subtract, cb, cb)
    P.ts('dve', lams[:, 3:4], lams[:, 2:3], LAM_INIT, -1.0, ALU.add, ALU.mult, cb, cb)
    neg_lam = lams[:, 3:4]

    converted = set()
    wstate = {'slot': 0, 'cst': 0, 'ptr': 0, 'issued': 0}
    wseq = []
    gidx = {'in': 0, 'xk': 1, 'xv': 1, 'xq': 2, 'ug': 3, 'out': 4}

    def wsrc(nm, b, k):
        r0 = k * 128
        if nm == 'ug':
            return [(0, 256, w_up[r0:r0 + 128, b * 256:(b + 1) * 256]),
                    (256, 256, w_gate[r0:r0 + 128, b * 256:(b + 1) * 256])]
        if nm == 'dn':
            c, kb = b // 3, b % 3
            r0 = kb * 1024 + k * 128
            return [(0, 512, w_down[r0:r0 + 128, c * 512:(c + 1) * 512])]
        w = {'in': w_in, 'out': w_out, 'xq': w_xq, 'xk': w_xk, 'xv': w_xv, 'xo': w_xo}[nm]
        return [(0, 512, w[r0:r0 + 128, b * 512:(b + 1) * 512])]

    def wissue():
        i = wstate['issued']
        if i >= len(wseq):
            return
        nm, b = wseq[i]
        wstate['issued'] += 1
        s = i % 4
        slot, sbuf_ = ring_t[s], ring_b[s]
        if (nm, b) in converted:
            P.dma(slot[:, :, :], scr[nm][b].rearrange("p (k c) -> p k c", k=8), [scrb[nm]], [sbuf_], sbuf_)
        else:
            converted.add((nm, b))
            nk = 6 if (nm == 'dn' and b % 3 == 2) else 8
            for k in range(nk):
                ci = wstate['cst'] % 4
                wstate['cst'] += 1
                for (c0, ncol, src) in wsrc(nm, b, k):
                    P.dma(cst_t[ci][:, c0:c0 + ncol], src, (), [cst_b[ci]], cst_b[ci])
                dst = slot[:, k, :]
                if nm in gidx:
                    gs = gcol[:, gidx[nm], k:k + 1]
                    if k % 2:
                        P.act(dst, cst_t[ci][:, :], AF.Copy, [cst_b[ci], const_b], [sbuf_], scale=gs)
                    else:
                        P.ts('dve', dst, cst_t[ci][:, :], gs, None, ALU.mult, None, [cst_b[ci], const_b], [sbuf_])
                else:
                    P.cp('act' if k % 2 else 'dve', dst, cst_t[ci][:, :], [cst_b[ci]], [sbuf_])
            P.dma(scr[nm][b].rearrange("p (k c) -> p k c", k=8), slot[:, :, :], [sbuf_], [scrb[nm]], sbuf_, q='pool')

    def wget(nm, b, held=0):
        i = wstate['ptr']
        assert wseq[i] == (nm, b), (wseq[i], nm, b)
        wstate['ptr'] += 1
        while wstate['issued'] < min(len(wseq), i + 4 - held):
            wissue()
        s = i % 4
        return ring_t[s], ring_b[s]

    class Tile:
        pass

    tiles = []
    for s in range(NSEQ):
        for t in range(SEQ // TT):
            tl = Tile()
            tl.kind = 'p'; tl.s = s; tl.t = t
            tl.NS = 4; tl.PT = 128; tl.NT = 512; tl.G = 1; tl.L = 512
            tl.first = (t == 0); tl.last = (t == SEQ // TT - 1)
            tiles.append(tl)
    tl = Tile()
    tl.kind = 's'; tl.s = 0; tl.t = 0
    tl.NS = 1; tl.PT = 64; tl.NT = 64; tl.G = 4; tl.L = 16
    tl.first = True; tl.last = True
    if phases < 50:
        tiles = tiles[:phases]
    tiles = [tl] + tiles

    def hoist_mem(i):
        tl = tiles[i]
        return tl.kind == 'p' and tl.last and i + 1 < len(tiles) and tiles[i + 1].kind == 'p'

    first_prompt = [i for i, tl in enumerate(tiles) if tl.kind == 'p'][:1]
    for i, tl in enumerate(tiles):
        if first_prompt and i == first_prompt[0]:
            wseq += [('xk', 0), ('xk', 1), ('xv', 0), ('xv', 1)]
        wseq += [('in', 0), ('in', 1), ('in', 2), ('in', 4), ('in', 5), ('in', 3)]
        wseq += [('out', 0), ('out', 1), ('xq', 0), ('xq', 1), ('xo', 0), ('xo', 1)]
        if hoist_mem(i):
            wseq += [('xk', 0), ('xk', 1), ('xv', 0), ('xv', 1)]
        wseq += [('ug', i_) for i_ in range(11)]
        wseq += [('dn', i_) for i_ in range(6)]

    mmrot = Rot([0, 1, 2, 3, 6, 7])
    tprot = Rot([4, 5])
    strot = Rot([4, 5, 6, 7])
    hbrot = Rot([0, 1])
    statrot = Rot(list(range(8)))

    def newstat():
        i = statrot.nxt()
        return stat[:, i * 16:(i + 1) * 16], stat_b[i]

    def rstd_from_ss(ssap, n, sbuf_, inv_n):
        P.act(ssap, ssap, AF.Ln, [sbuf_, const_b], [sbuf_], bias=epsc[0:ssap.shape[0], 0:1], scale=inv_n)
        P.act(ssap, ssap, AF.Exp, [sbuf_], [sbuf_], scale=-0.5)

    def sq_group(xt, PT, NS, ss, xb, ssb):
        def fn(e):
            ins = None
            for j in range(NS):
                ins = e.activation(junk[0:PT, j, :], xt[0:PT, j, :], AF.Square, accum_out=ss[0:PT, j:j + 1], saturate=False)
            return ins
        P.add('act', fn, [xb], [ssb])

    def norm_to_hT(tl, xt, xb):
        PT, NS = tl.PT, tl.NS
        for j in range(NS):
            ss, ssb = newstat()
            P.act(junk[0:PT, j, :], xt[0:PT, j, :], AF.Square, [xb[j]], [ssb], accum_out=ss[0:PT, 0:1], saturate=False)
            rstd_from_ss(ss[0:PT, 0:1], 1, ssb, 1.0 / D)
            hi = hbrot.nxt()
            P.act(hb_t[hi][0:PT, :], xt[0:PT, j, :], AF.Copy, [xb[j], ssb], [hb_b[hi]], scale=ss[0:PT, 0:1])
            bk = tprot.nxt()
            pv = pb[bk].bitcast(BF16).rearrange("p (k t) -> p k t", k=8)
            P.tr([(pv[:, k, 0:PT], hb_t[hi][0:PT, k * 128:(k + 1) * 128], PT) for k in range(8)],
                 ident, [hb_b[hi], ident_b], [pb_b[bk]])
            P.cp('dve', hT[:, :, j * PT:(j + 1) * PT], pv[:, :, 0:PT], [pb_b[bk]], [hT_bj[j]])

    def proj_resid(tl, nm, src, srcb, xt, xb):
        Ws = [wget(nm, 0), wget(nm, 1, held=1)]
        for j in range(tl.NS):
            for c in range(2):
                W, Wb = Ws[c]
                bk = mmrot.nxt()
                o = pb[bk][0:tl.PT, :]
                P.mm([(o, src[:, k, j * tl.PT:(j + 1) * tl.PT], W[:, k, :], k == 0, k == 7) for k in range(8)],
                     [srcb, Wb], [pb_b[bk]])
                xsl = xt[0:tl.PT, j, c * 512:(c + 1) * 512]
                P.tt('dve', xsl, o, xsl, ALU.add, [pb_b[bk], xb[j]], [xb[j]])

    def proj_tokmajor(tl, W, Wb, src, srcb, evac):
        for j in range(tl.NS):
            bk = mmrot.nxt()
            o = pb[bk][0:tl.PT, :]
            sbj = [srcb[j]] if isinstance(srcb, list) else [srcb]
            P.mm([(o, src[:, k, j * tl.PT:(j + 1) * tl.PT], W[:, k, :], k == 0, k == 7) for k in range(8)],
                 sbj + [Wb], [pb_b[bk]])
            evac(j, o, pb_b[bk])

    def proj_featmajor(tl, W, Wb, c0, nchunk, src, srcb, evac, rot=None):
        rot = rot or mmrot
        for c in range(nchunk):
            bk = rot.nxt()
            o = pb[bk][:, 0:tl.NT]
            sbl = list(srcb) if isinstance(srcb, list) else [srcb]
            P.mm([(o, W[:, k, c0 + c * 128:c0 + (c + 1) * 128], src[:, k, 0:tl.NT], k == 0, k == 7) for k in range(8)],
                 sbl + [Wb], [pb_b[bk]])
            evac(c, o, pb_b[bk])

    def rope(tl, buf4, bufb, jj0):
        PT, NS = tl.PT, tl.NS
        x1 = buf4[0:PT, 0:NS, :, 0:8]
        x2 = buf4[0:PT, 0:NS, :, 8:16]
        c = cosT[0:PT, jj0:jj0 + NS, :].unsqueeze(2).to_broadcast([PT, NS, 8, 8])
        s = sinT[0:PT, jj0:jj0 + NS, :].unsqueeze(2).to_broadcast([PT, NS, 8, 8])
        ta, tb_, tc, td = (rtmp[0:PT, i, 0:NS, :, :] for i in range(4))
        rw = [bufb, rtmp_b, const_b]
        P.tt('dve', ta, x1, c, ALU.mult, rw, [rtmp_b])
        P.tt('dve', tb_, x2, s, ALU.mult, rw, [rtmp_b])
        P.tt('dve', tc, x2, c, ALU.mult, rw, [rtmp_b])
        P.tt('dve', td, x1, s, ALU.mult, rw, [rtmp_b])
        P.tt('dve', x1, ta, tb_, ALU.subtract, [rtmp_b], [bufb])
        P.tt('dve', x2, tc, td, ALU.add, [rtmp_b], [bufb])

    def conv3(eng, out, u3, wv, c, G, L, rd, wr):
        P.ts('dve', out, u3[:, :, 0:L], wv[:, c, 0:1], None, ALU.mult, None, rd, wr)
        P.stt('dve', out, u3[:, :, 1:L + 1], wv[:, c, 1:2], out, ALU.mult, ALU.add, rd + wr, wr)
        P.stt('dve', out, u3[:, :, 2:L + 2], wv[:, c, 2:3], out, ALU.mult, ALU.add, rd + wr, wr)

    def mem_phase(tl):
        s = tl.s
        P.dma(mld[:, :, :], memp[s].rearrange("(j p) d -> p j d", p=128), (), [mld_b], mld_b)
        ss, ssb = newstat()
        sq_group(mld, 128, 2, ss, mld_b, ssb)
        rstd_from_ss(ss[:, 0:2], 2, ssb, 1.0 / D)
        for j in range(2):
            P.act(mb16[:, j, :], mld[:, j, :], AF.Copy, [mld_b, ssb], [mb16_b], scale=ss[:, j:j + 1])
            bk = tprot.nxt()
            pv = pb[bk].bitcast(BF16).rearrange("p (k t) -> p k t", k=8)
            P.tr([(pv[:, k, :], mb16[:, j, k * 128:(k + 1) * 128], 128) for k in range(8)],
                 ident, [mb16_b, ident_b], [pb_b[bk]])
            P.cp('dve', mT[:, :, j * 128:(j + 1) * 128], pv, [pb_b[bk]], [mT_b])
        for which, outd in (('xk', mk_p), ('xv', mv_p)):
            for c in range(2):
                W, Wb = wget(which, c)
                for j in range(2):
                    bk = mmrot.nxt()
                    o = pb[bk][:, :]
                    P.mm([(o, mT[:, k, j * 128:(j + 1) * 128], W[:, k, :], k == 0, k == 7) for k in range(8)],
                         [mT_b, Wb], [pb_b[bk]])
                    P.cp('act', mst[:, j, c * 512:(c + 1) * 512], o, [pb_b[bk]], [mst_b])
                    if which == 'xv':
                        P.cp('dve', MV[:, j, c * 512:(c + 1) * 512], o, [pb_b[bk]], [MV_b])
                    else:
                        P.cp('dve', mkb[:, 0:512], o, [pb_b[bk]], [mkb_b])
                        tk = tprot.nxt()
                        pv = pb[tk].bitcast(BF16).rearrange("p (k t) -> p k t", k=8)
                        P.tr([(pv[:, k, :], mkb[:, k * 128:(k + 1) * 128], 128) for k in range(4)],
                             ident, [mkb_b, ident_b], [pb_b[tk]])
                        P.cp('act', MKT[:, c * 4:(c + 1) * 4, j * 128:(j + 1) * 128], pv[:, 0:4, :], [pb_b[tk]], [MKT_b])
            P.dma(outd[s * NMEM:(s + 1) * NMEM, :].rearrange("(j p) d -> p j d", p=128), mst[:, :, :], [mst_b], (), mst_b, q='pool')

    def mem_sample_load(s):
        P.dma(mld[:, :, :], cmk[s].rearrange("(j p) d -> p j d", p=128), (), [mld_b], mld_b)
        P.dma(mst[:, :, :], cmv[s].rearrange("(j p) d -> p j d", p=128), (), [mst_b], mst_b)

    def mem_sample(s):
        for j in range(2):
            P.cp('dve', mb16[:, j, :], mld[:, j, :], [mld_b], [mb16_b])
            bk = tprot.nxt()
            pv = pb[bk].bitcast(BF16).rearrange("p (k t) -> p k t", k=8)
            P.tr([(pv[:, k, :], mb16[:, j, k * 128:(k + 1) * 128], 128) for k in range(8)],
                 ident, [mb16_b, ident_b], [pb_b[bk]])
            P.cp('act', MKT[:, :, j * 128:(j + 1) * 128], pv, [pb_b[bk]], [MKT_b])
        for j in range(2):
            P.cp('act', MV[:, j, :], mst[:, j, :], [mst_b], [MV_b])
        if s + 1 < NSEQ:
            mem_sample_load(s + 1)

    def phase_A(tl, xt, xb):
        PT, NS, NT, G, L = tl.PT, tl.NS, tl.NT, tl.G, tl.L
        prompt = tl.kind == 'p'
        tq = tl.t if prompt else 0
        kcol0 = tl.t * TT if prompt else PAST
        jj0 = tl.t * 4 if prompt else 16
        W, Wb = wget('in', 0)

        def evq(j, o, ob):
            P.cp('act', q32x[0:PT, j, :], o, [ob], [q32x_b])
        proj_tokmajor(tl, W, Wb, hT, hT_b, evq)
        W, Wb = wget('in', 1)

        def evk(j, o, ob):
            P.cp('act', kst[0:PT, j, :], o, [ob], [kst_b])
        proj_tokmajor(tl, W, Wb, hT, hT_b, evk)
        rope(tl, q32x.rearrange("p j (a d) -> p j a d", a=8), q32x_b, jj0)
        P.cp('dve', qb16x[0:PT, 0:NS, :], q32x[0:PT, 0:NS, :], [q32x_b], [qb16x_b])
        rope(tl, kst.rearrange("p j (a d) -> p j a d", a=8), kst_b, jj0)
        P.cp('dve', kb16x[0:PT, 0:NS, :], kst[0:PT, 0:NS, :], [kst_b], [kb16x_b])
        if prompt:
            r0 = tl.s * SEQ + tl.t * TT
            P.dma(k_p[r0:r0 + TT, :].rearrange("(j p) c -> p j c", p=128), kst[:, :, :], [kst_b], (), kst_b, q='pool')
        else:
            P.dma(k_s[:, :], kst[0:64, 0, :], [kst_b], (), kst_b)
        W, Wb = wget('in', 2)

        def evv(j, o, ob):
            P.cp('act', vst[0:PT, j, :], o, [ob], [vst_b])
            kt = tl.t * 4 + j if prompt else 8
            P.cp('dve', V1[0:PT, kt, :, 0:128], vst[0:PT, j, :].rearrange("p (h e) -> p h e", h=4), [vst_b], [V1_b[tq if prompt else 2]])
        proj_tokmajor(tl, W, Wb, hT, hT_b, evv)
        if prompt:
            P.dma(v_p[r0:r0 + TT, :].rearrange("(j p) c -> p j c", p=128), vst[:, :, :], [vst_b], (), vst_b, q='pool')
        else:
            P.dma(v_s[:, :], vst[0:64, 0, :], [vst_b], (), vst_b)
        u4 = ubuf[:, 0:4 * G * (L + 2)].rearrange("p (c g l) -> p c g l", c=4, g=G)
        if prompt:
            if tl.first:
                P.memset('pool', u4[:, :, 0, 0:2], 0.0, [ubuf_b])
            else:
                P.cp('pool', u4[:, :, 0, 0:2], uhalo[:, :, :], [uhalo_b], [ubuf_b])
        else:
            for s in range(NSEQ):
                dma_state_in(u4[:, :, s, 0:2], ssc[s], [ubuf_b], ubuf_b)
        W, Wb = wget('in', 4)

        def evcg(c, o, ob):
            P.cp('act', cg32[:, c, 0:NT], o, [ob], [cg32_b])
        proj_featmajor(tl, W, Wb, 0, 4, hT, hT_b, evcg)
        for j in range(NS):
            bk = tprot.nxt()
            pv = pb[bk].bitcast(BF16).rearrange("p (k t) -> p k t", k=8)
            P.tr([(pv[:, h, 0:PT], qb16x[0:PT, j, h * 128:(h + 1) * 128], PT) for h in range(4)],
                 ident, [qb16x_b, ident_b], [pb_b[bk]])
            P.cp('dve', qT[:, :, j * PT:(j + 1) * PT], pv[:, 0:4, 0:PT], [pb_b[bk]], [qT_b])
        W, Wb = wget('in', 5)

        def evxh(c, o, ob):
            P.tt('dve', u4[:, c, :, 2:L + 2], cg32[:, c, 0:NT].rearrange("p (g l) -> p g l", g=G),
                 o.rearrange("p (g l) -> p g l", g=G), ALU.mult, [ob, cg32_b], [ubuf_b])
            conv3('pool', cg32[:, c, 0:NT].rearrange("p (g l) -> p g l", g=G), u4[:, c, :, :], wsc, c, G, L,
                  [ubuf_b, const_b], [cg32_b])
        proj_featmajor(tl, W, Wb, 0, 4, hT, hT_b, evxh)
        for j in range(NS):
            bk = tprot.nxt()
            pv = pb[bk].bitcast(BF16).rearrange("p (k t) -> p k t", k=8)
            P.tr([(pv[:, h, 0:PT], kb16x[0:PT, j, h * 128:(h + 1) * 128], PT) for h in range(4)],
                 ident, [kb16x_b, ident_b], [pb_b[bk]])
            P.cp('dve', KT[:, :, kcol0 + j * PT:kcol0 + (j + 1) * PT], pv[:, 0:4, 0:PT], [pb_b[bk]], [KT_b[tq]])
        if prompt:
            if tl.last:
                dma_state_out(sc_p[tl.s], u4[:, :, 0, L:L + 2], [ubuf_b], ubuf_b)
            else:
                P.cp('pool', uhalo[:, :, :], u4[:, :, 0, L:L + 2], [ubuf_b], [uhalo_b])
        else:
            for s in range(NSEQ):
                dma_state_out(sc_s[s], u4[:, :, s, L:L + 2], [ubuf_b], ubuf_b)
        W, Wb = wget('in', 3)

        def evbg(c, o, ob):
            P.tt('dve', catT[:, 4 + c, 0:NT], cg32[:, c, 0:NT], o, ALU.mult, [ob, cg32_b], [catT_b])
        proj_featmajor(tl, W, Wb, 0, 4, hT, hT_b, evbg)
        P.tag = 'A_attn'
        if prompt:
            attn_prompt(tl)
        else:
            attn_sample(tl)
        P.tag = 'A_out'
        proj_resid(tl, 'out', catT, catT_b, xt, xb)

    def attn_post_parts(h):
        A = pacc[:, :].rearrange("p (q c) -> p q c", q=4)[:, :, 0:258].rearrange("p q (m e) -> p q m e", m=2)
        accb = [pb_b[i] for i in range(4)]
        st, stb = newstat()
        rec = st[:, 0:8].rearrange("p (q m) -> p q m", q=4)
        rl = st[:, 8:12]
        rs = st[:, 12:16]

        def part1():
            P.add('dve', lambda e: e.reciprocal(rec, A[:, :, :, 128]), accb, [stb])
            P.ts('dve', rl, rec[:, :, 1], neg_lam, None, ALU.mult, None, [stb, const_b], [stb])
            P.tt('dve', t1[:, :, :], A[:, :, 0, 0:128], rec[:, :, 0:1].to_broadcast([128, 4, 128]), ALU.mult, accb + [stb], [t1_b])
            P.tt('dve', o32[:, :, :], A[:, :, 1, 0:128], rl.unsqueeze(2).to_broadcast([128, 4, 128]), ALU.mult, accb + [stb], [o32_b])
            P.tt('dve', o32[:, :, :], o32[:, :, :], t1[:, :, :], ALU.add, [t1_b, o32_b], [o32_b])

        def part2(bk):
            def fn(e):
                ins = None
                for q in range(4):
                    ins = e.activation(junk[:, q, 0:128], o32[:, q, :], AF.Square, accum_out=rs[:, q:q + 1], saturate=False)
                return ins
            P.add('act', fn, [o32_b], [stb])
            rstd_from_ss(rs, 4, stb, 1.0 / 128)

        def part3(bk):
            P.tt('dve', otm[:, :, :], o32[:, :, :], rs.unsqueeze(2).to_broadcast([128, 4, 128]), ALU.mult, [o32_b, stb], [otm_b])
            pv = pb[bk].bitcast(BF16).rearrange("p (k t) -> p k t", k=8)
            P.tr([(pv[:, q, :], otm[:, q, :], 128) for q in range(4)], ident, [otm_b, ident_b], [pb_b[bk]])
            P.cp('dve', catT[:, h, 0:512].rearrange("p (q t) -> p q t", q=4), pv[:, 0:4, :], [pb_b[bk]], [catT_b])
        return part1, part2, part3

    def attn_prompt(tl):
        t = tl.t
        nkt = t * 4 + 4
        steps = [(h, kt) for h in range(4) for kt in range(nkt)]
        stq = {}
        pti = [0]

        def emit_st(i):
            h, kt = steps[i]
            r = kt - t * 4
            q0 = max(0, r) * 128
            N = 512 - q0
            lst = []
            for m in range(2):
                bk = 4 + 2 * (i % 2) + m
                o = pb[bk][:, 0:N]
                P.mm([(o, KT[m * 64:(m + 1) * 64, h, kt * 128:(kt + 1) * 128], qT[m * 64:(m + 1) * 64, h, q0:512], True, True)],
                     [KT_b[kt // 4], qT_b], [pb_b[bk]])
                lst.append((bk, o))
            stq[i] = (lst, r, q0, N)

        def emit_exp_pv(i):
            h, kt = steps[i]
            lst, r, q0, N = stq.pop(i)
            par = i % 2
            pp, ppb = ptP[pti[0] % 4]
            pti[0] += 1
            stv = pst[:, par * 1024:(par + 1) * 1024].rearrange("p (m q) -> p m q", m=2)
            stb = [pb_b[4 + 2 * par], pb_b[5 + 2 * par]]
            P.act(pp[:, :, q0:512], stv[:, :, 0:N], AF.Exp, stb, [ppb], scale=0.125)
            if r >= 0:
                P.memset('dve', pp[64:128, :, q0:q0 + 64], 0.0, [ppb])
            for qb in range(max(0, r), 4):
                items = []
                for m in range(2):
                    A = pb[qb][:, 0:258].rearrange("p (m e) -> p m e", m=2)
                    items.append((A[:, m, :], pp[:, m, qb * 128:(qb + 1) * 128], V1[:, kt, h, 0:129],
                                  kt == 0 and m == 0, kt == t * 4 + qb, True))
                P.mm(items, [ppb, V1_b[kt // 4]], [pb_b[qb]])

        deferred = []
        emit_st(0)
        for i, (h, kt) in enumerate(steps):
            if i + 1 < len(steps):
                emit_st(i + 1)
            emit_exp_pv(i)
            for d in [d for d in deferred if d[0] <= i]:
                d[1](4 + 2 * (i % 2))
                deferred.remove(d)
            if kt == nkt - 1:
                p1, p2, p3 = attn_post_parts(h)
                p1()
                deferred.append((i + 2, p2))
                deferred.append((i + 4, p3))
        for d in deferred:
            d[1](4)

    def attn_sample(tl):
        ptz, ptz_b = ptA[0][0], ptA[0][1]
        ptn, ptn_b = ptA[2][0], ptA[2][1]
        P.dma(ckst[:, :, :], ck[0].rearrange("(j p) c -> p j c", p=128), (), [ckst_b], ckst_b)
        for s in range(NSEQ):
            P.dma(ckst2[:, :, :], cv[s].rearrange("(j p) c -> p j c", p=128), (), [ckst2_b], ckst2_b)
            for half in range(2):
                P.cp('dve', ckb[:, :, :], ckst[:, half * 4:(half + 1) * 4, :], [ckst_b], [ckb_b])
                for j4 in range(4):
                    j = half * 4 + j4
                    bk = tprot.nxt()
                    pv = pb[bk].bitcast(BF16).rearrange("p (k t) -> p k t", k=8)
                    P.tr([(pv[:, h, :], ckb[:, j4, h * 128:(h + 1) * 128], 128) for h in range(4)],
                         ident, [ckb_b, ident_b], [pb_b[bk]])
                    P.cp('dve', KT[:, :, j * 128:(j + 1) * 128], pv[:, 0:4, :], [pb_b[bk]], [KT_b[0]])
            if s + 1 < NSEQ:
                P.dma(ckst[:, :, :], ck[s + 1].rearrange("(j p) c -> p j c", p=128), (), [ckst_b], ckst_b)
            P.cp('act', V1[:, 0:8, :, 0:128], ckst2[:, :, :].rearrange("p j (h e) -> p j h e", h=4), [ckst2_b], [V1_b[0]])
            for h in range(4):
                pz = ptz[:, 0:256].rearrange("p (k m q) -> p k m q", k=8, m=2)
                pn = ptn[0:64, 0:32].rearrange("p (m q) -> p m q", m=2)
                for m in range(2):
                    bk = strot.nxt()
                    sv = pb[bk][:, 0:128].rearrange("p (k q) -> p k q", k=8)
                    nv = pb[bk][0:64, 128:144]
                    its = [(sv[:, kt, :], KT[m * 64:(m + 1) * 64, h, kt * 128:(kt + 1) * 128],
                            qT[m * 64:(m + 1) * 64, h, s * 16:(s + 1) * 16], True, True) for kt in range(8)]
                    its.append((nv, KT[m * 64:(m + 1) * 64, h, PAST:PAST + 64],
                                qT[m * 64:(m + 1) * 64, h, s * 16:(s + 1) * 16], True, True))
                    P.mm(its, [KT_b[0], qT_b], [pb_b[bk]])
                    P.act(pz[:, :, m, :], sv, AF.Exp, [pb_b[bk]], [ptz_b], scale=0.125)
                    P.act(pn[:, m, :], nv, AF.Exp, [pb_b[bk], const_b], [ptn_b], bias=cmask[0:64, 1 + s:2 + s], scale=0.125)
                A = pb[h][0:16, 0:258].rearrange("p (m e) -> p m e", m=2)
                items = []
                for m in range(2):
                    for kt in range(8):
                        items.append((A[:, m, :], pz[:, kt, m, 0:16], V1[:, kt, h, 0:129], kt == 0, False))
                    items.append((A[:, m, :], pn[:, m, :], V1[0:64, 8, h, 0:129], False, True))
                P.mm(items, [ptz_b, ptn_b, V1_b[0], V1_b[2]], [pb_b[h]])
            for h in range(4):
                attn_post_sample(s, h)

    def attn_post_sample(s, h):
        PT = 16
        bk = h
        A = pb[bk][0:PT, 0:258].rearrange("p (m e) -> p m e", m=2)
        st, stb = newstat()
        P.add('dve', lambda e: e.reciprocal(st[0:PT, 0:2], A[:, :, 128]), [pb_b[bk]], [stb])
        P.ts('dve', st[0:PT, 2:3], st[0:PT, 1:2], neg_lam[0:PT, :], None, ALU.mult, None, [stb, const_b], [stb])
        P.ts('dve', t1[0:PT, 0, :], A[:, 0, 0:128], st[0:PT, 0:1], None, ALU.mult, None, [pb_b[bk], stb], [t1_b])
        P.stt('dve', o32[0:PT, 0, :], A[:, 1, 0:128], st[0:PT, 2:3], t1[0:PT, 0, :], ALU.mult, ALU.add,
              [pb_b[bk], stb, t1_b], [o32_b])
        P.act(junk[0:PT, statrot.i % 4, 0:128], o32[0:PT, 0, :], AF.Square, [o32_b], [stb], accum_out=st[0:PT, 3:4], saturate=False)
        rstd_from_ss(st[0:PT, 3:4], 1, stb, 1.0 / 128)
        P.act(otm[0:PT, 0, :], o32[0:PT, 0, :], AF.Copy, [o32_b, stb], [otm_b], scale=st[0:PT, 3:4])
        tb = strot.nxt()
        pv = pb[tb].bitcast(BF16).rearrange("p (k t) -> p k t", k=8)
        P.tr([(pv[:, 0, 0:PT], otm[0:PT, 0, :], PT)], ident, [otm_b, ident_b], [pb_b[tb]])
        P.cp('dve', catT[:, h, s * 16:(s + 1) * 16], pv[:, 0, 0:PT], [pb_b[tb]], [catT_b])

    def phase_B(tl, xt, xb):
        PT, NS, NT = tl.PT, tl.NS, tl.NT
        norm_to_hT(tl, xt, xb)
        for c in range(2):
            W, Wb = wget('xq', c)

            def evq(cc, o, ob, c=c):
                P.cp('act', hqT[:, c * 4 + cc, 0:NT], o, [ob], [hqT_b])
            proj_featmajor(tl, W, Wb, 0, 4, hT, hT_b, evq)
        if tl.kind == 'p':
            cross_attn(tl, 0, NT)
        else:
            mem_sample_load(0)
            for s in range(NSEQ):
                mem_sample(s)
                cross_attn(tl, s * 16, 16)
        proj_resid(tl, 'xo', xoT, xoT_b, xt, xb)

    xorot = Rot([0, 1, 2, 3])

    cxs = {'i': 0}

    def cross_attn(tl, c0, n):
        scale = 256 ** -0.5
        for h in range(4):
            ci = cxs['i']
            cxs['i'] += 1
            pts = []
            for mt in range(2):
                bk = strot.nxt()
                o = pb[bk][:, 0:n]
                P.mm([(o, MKT[:, h * 2 + dd, mt * 128:(mt + 1) * 128], hqT[:, h * 2 + dd, c0:c0 + n], dd == 0, dd == 1)
                      for dd in range(2)], [MKT_b, hqT_b], [pb_b[bk]])
                pt, ptb = ptB[(ci % 2) * 2 + mt]
                P.act(pt[:, 0:n], o, AF.Exp, [pb_b[bk]], [ptb], scale=scale)
                pts.append((pt, ptb))
            rden, rden_b = rdens[ci % 2]
            dk = 7
            dn = pb[dk][:, 0:n]
            P.mm([(dn, ones[:, :], pts[mt][0][:, 0:n], mt == 0, mt == 1) for mt in range(2)],
                 [pts[0][1], pts[1][1], ident_b], [pb_b[dk]])
            P.add('dve', lambda e, dn=dn, rden=rden: e.reciprocal(rden[:, 0:n], dn), [pb_b[dk]], [rden_b])
            for dd in range(2):
                bk = xorot.nxt()
                o = pb[bk][:, 0:n]
                P.mm([(o, MV[:, mt, h * 256 + dd * 128:h * 256 + (dd + 1) * 128], pts[mt][0][:, 0:n], mt == 0, mt == 1)
                      for mt in range(2)], [MV_b, pts[0][1], pts[1][1]], [pb_b[bk]])
                P.tt('dve', xoT[:, h * 2 + dd, c0:c0 + n], o, rden[:, 0:n], ALU.mult, [pb_b[bk], rden_b], [xoT_b])

    ffrot = Rot([0, 1, 2, 3, 4, 5, 6, 7])

    def phase_C(tl, xt, xb):
        PT, NS, NT, G, L = tl.PT, tl.NS, tl.NT, tl.G, tl.L
        prompt = tl.kind == 'p'
        norm_to_hT(tl, xt, xb)
        if not prompt:
            for s in range(NSEQ):
                dma_state_in(sphalo[:, :, s, :], sff[s], [sphalo_b], sphalo_b)
        elif tl.first:
            P.memset('pool', uphalo[:, :, :], 0.0, [uphalo_b])
        ui = 0
        for i in range(11):
            W, Wb = wget('ug', i)
            for cc in range(2):
                ch = 2 * i + cc
                bu = ffrot.nxt()
                ou = pb[bu][:, 0:NT]
                P.mm([(ou, W[:, k, cc * 128:(cc + 1) * 128], hT[:, k, 0:NT], k == 0, k == 7) for k in range(8)],
                     hT_b + [Wb], [pb_b[bu]])
                bg_ = ffrot.nxt()
                og = pb[bg_][:, 0:NT]
                P.mm([(og, W[:, k, 256 + cc * 128:256 + (cc + 1) * 128], hT[:, k, 0:NT], k == 0, k == 7) for k in range(8)],
                     hT_b + [Wb], [pb_b[bg_]])
                ub, ubb = upb[ui % 2]
                cb_, cbb = cvb[ui % 2]
                sl, slb_ = slb[ui % 2]
                ui += 1
                u3 = ub[:, 0:G * (L + 2)].rearrange("p (g l) -> p g l", g=G)
                if prompt:
                    P.cp('pool', u3[:, 0, 0:2], uphalo[:, ch, :], [uphalo_b], [ubb])
                else:
                    P.cp('pool', u3[:, :, 0:2], sphalo[:, ch, :, :], [sphalo_b], [ubb])
                P.cp('act', u3[:, :, 2:L + 2], ou.rearrange("p (g l) -> p g l", g=G), [pb_b[bu]], [ubb])
                if prompt:
                    P.cp('pool', uphalo[:, ch, :], u3[:, 0, L:L + 2], [ubb], [uphalo_b])
                else:
                    P.cp('pool', sphalo[:, ch, :, :], u3[:, :, L:L + 2], [ubb], [sphalo_b])
                c3 = cb_[:, 0:NT].rearrange("p (g l) -> p g l", g=G)
                conv3('pool', c3, u3, wfc, ch, G, L, [ubb, const_b], [cbb])
                P.act(sl[:, 0:NT], cb_[:, 0:NT], AF.Silu, [cbb], [slb_])
                P.tt('dve', gT[:, ch, 0:NT], sl[:, 0:NT], og, ALU.mult, [pb_b[bg_], slb_], [gT_b3[ch // 8]])
        if prompt:
            if tl.last:
                dma_state_out(ff_p[tl.s], uphalo[:, :, :], [uphalo_b], uphalo_b)
        else:
            for s in range(NSEQ):
                dma_state_out(ff_s[s], sphalo[:, :, s, :], [sphalo_b], sphalo_b)
        P.tag = 'C_down'
        for c in range(2):
            banks = [c * 4 + j for j in range(NS)]
            for kb in range(3):
                W, Wb = wget('dn', c * 3 + kb)
                nk = 8 if kb < 2 else 6
                for j in range(NS):
                    o = pb[banks[j]][0:PT, :]
                    P.mm([(o, gT[:, kb * 8 + k, j * PT:(j + 1) * PT], W[:, k, :], kb == 0 and k == 0, kb == 2 and k == nk - 1)
                          for k in range(nk)], [gT_b3[kb], Wb], [pb_b[banks[j]]])
            for j in range(NS):
                xsl = xt[0:PT, j, c * 512:(c + 1) * 512]
                P.tt('dve', xsl, pb[banks[j]][0:PT, :], xsl, ALU.add, [pb_b[banks[j]], xb[j]], [xb[j]])

    def final_norm(tl, xt, xb, xtile_b):
        PT, NS = tl.PT, tl.NS
        for j in range(NS):
            ss, ssb = newstat()
            P.act(junk[0:PT, j, :], xt[0:PT, j, :], AF.Square, [xb[j]], [ssb], accum_out=ss[0:PT, 0:1], saturate=False)
            rstd_from_ss(ss[0:PT, 0:1], 1, ssb, 1.0 / D)
            P.stt('dve', xt[0:PT, j, :], xt[0:PT, j, :], ss[0:PT, 0:1], gfin[0:PT, 0, :], ALU.mult, ALU.mult,
                  [xb[j], ssb, const_b], [xb[j]])
        if tl.kind == 'p':
            r0 = tl.s * SEQ + tl.t * TT
            P.dma(y_p[r0:r0 + TT, :].rearrange("(j p) d -> p j d", p=128), xt[:, :, :], [xtile_b], (), xtile_b, q='pool')
        else:
            P.dma(y_s[:, :], xt[0:64, 0, :], [xtile_b], (), xtile_b)

    def load_x(i):
        tl = tiles[i]
        xt, xb = xs_t[i % 2], xs_b[i % 2]
        if tl.kind == 'p':
            r0 = tl.s * SEQ + tl.t * TT
            P.dma(xt[:, :, :], xp[r0:r0 + TT, :].rearrange("(j p) d -> p j d", p=128), (), [xb], xb)
        else:
            P.dma(xt[0:64, 0, :], xsm[:, :], (), [xb], xb)

    load_x(0)
    P.tag = 'A_inproj'
    norm_to_hT(tiles[0], xs_t[0], xsub_b[0])
    for i, tl in enumerate(tiles):
        xt, xb, xtb = xs_t[i % 2], xsub_b[i % 2], xs_b[i % 2]
        if i + 1 < len(tiles):
            load_x(i + 1)
        if first_prompt and i == first_prompt[0]:
            P.tag = 'mem'
            mem_phase(tl)
        P.tag = 'A_inproj'
        phase_A(tl, xt, xb)
        P.tag = 'B'
        phase_B(tl, xt, xb)
        if hoist_mem(i):
            P.tag = 'mem'
            mem_phase(tiles[i + 1])
        P.tag = 'C_upgate'
        phase_C(tl, xt, xb)
        if i + 1 < len(tiles):
            P.tag = 'A_inproj'
            norm_to_hT(tiles[i + 1], xs_t[(i + 1) % 2], xsub_b[(i + 1) % 2])
        final_norm(tl, xt, xb, xtb)
    assert wstate['ptr'] == len(wseq)
    P.sbuf_left = nc.sbuf_bytes_remaining
    P.build(es)
    es.close()
    return nc, P


def _consts():
    half = 8
    inv = (1.0 / (500000.0 ** (np.arange(half, dtype=np.float32) * np.float32(2.0) / np.float32(16)))).astype(np.float32)
    pos = np.zeros((17, 128), np.float32)
    for j in range(16):
        pos[j] = j * 128 + np.arange(128)
    pos[16, :16] = PAST + np.arange(16)
    pos[16, :64] = PAST + (np.arange(64) % 16)
    ang = (pos[:, :, None].astype(np.float32) * inv[None, None, :]).astype(np.float32)
    cos = np.cos(ang).astype(np.float32).transpose(1, 0, 2).copy()
    sin = np.sin(ang).astype(np.float32).transpose(1, 0, 2).copy()
    ident = np.eye(128, dtype=np.float32)
    cm = np.zeros((128, 8), np.float32)
    cm[64:, 0] = NEG
    for s in range(4):
        cm[:, 1 + s] = NEG
        cm[s * 16:(s + 1) * 16, 1 + s] = 0.0
    return cos, sin, ident, cm


_CACHE = {}


def kernel(**inputs):
    f = lambda a: np.ascontiguousarray(np.asarray(a, dtype=np.float32))
    if 'nc' not in _CACHE:
        _CACHE['nc'] = build_program()[0]
    nc = _CACHE['nc']
    cos, sin, ident, cm = _consts()
    shared = {
        "g_mix": f(inputs["g_mix"]), "g_mem": f(inputs["g_mem"]), "g_x": f(inputs["g_x"]),
        "g_ffn": f(inputs["g_ffn"]), "g_final": f(inputs["g_final"]).reshape(1, D), "g_sub": f(inputs["g_sub"]),
        "lam_q1": f(inputs["lam_q1"]), "lam_k1": f(inputs["lam_k1"]), "lam_q2": f(inputs["lam_q2"]), "lam_k2": f(inputs["lam_k2"]),
        "w_in": f(inputs["w_in"])[0], "w_out": f(inputs["w_out"])[0], "w_xq": f(inputs["w_xq"])[0],
        "w_xk": f(inputs["w_xk"])[0], "w_xv": f(inputs["w_xv"])[0], "w_xo": f(inputs["w_xo"])[0],
        "w_up": f(inputs["w_up"])[0], "w_gate": f(inputs["w_gate"])[0], "w_down": f(inputs["w_down"])[0],
        "w_sc": f(inputs["w_sc"])[0], "w_ffconv": f(inputs["w_ffconv"])[0],
        "rope_cos": cos, "rope_sin": sin, "ident": ident, "cmask": cm,
    }
    xpr = f(inputs["x_prompt"]); xsa = f(inputs["x_sample"])
    cak = f(inputs["cache_attn_k"])[0]; cav = f(inputs["cache_attn_v"])[0]
    sscv = f(inputs["state_short_conv"])[0]; sffv = f(inputs["state_ffn_conv"])[0]
    cmkv = f(inputs["cache_mem_k"])[0]; cmvv = f(inputs["cache_mem_v"])[0]; mp = f(inputs["mem_prompt"])
    in_maps = []
    for c in range(NCORES):
        sl = slice(c * NSEQ, (c + 1) * NSEQ)
        m = dict(shared)
        m["xp"] = xpr[sl].reshape(NSEQ * SEQ, D)
        m["xsm"] = xsa[sl].reshape(NSEQ * DEC, D)
        m["ck"] = cak[sl].reshape(NSEQ, PAST, 512)
        m["cv"] = cav[sl].reshape(NSEQ, PAST, 512)
        m["ssc"] = sscv[sl]
        m["sff"] = sffv[sl]
        m["cmk"] = cmkv[sl].reshape(NSEQ, NMEM, D)
        m["cmv"] = cmvv[sl].reshape(NSEQ, NMEM, D)
        m["memp"] = mp[sl]
        in_maps.append(m)
    res = run_bass_kernel_spmd(nc, in_maps, core_ids=list(range(NCORES)))
    R = res.results

    def cat(name, shape):
        return np.concatenate([np.asarray(r[name], dtype=np.float32) for r in R], axis=0).reshape(shape)

    B = NCORES * NSEQ
    return (
        cat("y_p", (B, SEQ, D)),
        cat("y_s", (B, DEC, D)),
        cat("k_p", (B, SEQ, 4, 2, 64))[None],
        cat("v_p", (B, SEQ, 4, 128))[None],
        cat("sc_p", (B, 2, 512))[None],
        cat("ff_p", (B, 2, DFF))[None],
        cat("mk_p", (B, NMEM, 4, 256))[None],
        cat("mv_p", (B, NMEM, 4, 256))[None],
        cat("k_s", (B, DEC, 4, 2, 64))[None],
        cat("v_s", (B, DEC, 4, 128))[None],
        cat("sc_s", (B, 2, 512))[None],
        cat("ff_s", (B, 2, DFF))[None],
    )
```
